# Optimizing a Trainium2 kernel written in Bass

```python
import math
import jax, jax.numpy as jnp
from jax import lax
import numpy as np


D_MODEL = 1024
BATCH = 1
SEQ = 16384
DEPTH = 2

GRID_W = 64
CTX_LEN = 256
SSM_WIDTH = D_MODEL // 2
SSM_GROUP = 16
SSM_GROUPS = SSM_WIDTH // SSM_GROUP
SSM_STATE = 64
CONV_WIDTH = D_MODEL
CONV_K = 31
FFN_HIDDEN = ((8 * D_MODEL // 3 + 255) // 256) * 256
N_MOD = 6
W_IN_COLS = SSM_WIDTH + 2 * CONV_WIDTH + 2 * D_MODEL
EPS = 1e-6
STEP_MIN = 1e-3
STEP_MAX = 1e-1
POS_BASE = 10000.0

kernel_name = 'hybrid_s5_conformer_dit_block'


def _rmsnorm(x, g):
    x32 = x.astype(jnp.float32)
    y = x32 * lax.rsqrt(jnp.mean(x32 * x32, axis=-1, keepdims=True) + EPS)
    return (y * g.astype(jnp.float32)).astype(x.dtype)


def _layernorm(x, g, b):
    x32 = x.astype(jnp.float32)
    mu = jnp.mean(x32, axis=-1, keepdims=True)
    xc = x32 - mu
    var = jnp.mean(xc * xc, axis=-1, keepdims=True)
    y = xc * lax.rsqrt(var + EPS) * g.astype(jnp.float32) + b.astype(jnp.float32)
    return y.astype(x.dtype)


def _modulate(x, g, shift, scale):
    return _rmsnorm(x, g) * (1 + scale) + shift


def _pos2d(rows, cols, dim, dtype):
    q = dim // 4
    omega = 1.0 / (POS_BASE ** (jnp.arange(q, dtype=jnp.float32) / q))
    r = jnp.repeat(jnp.arange(rows, dtype=jnp.float32), cols)[:, None] * omega
    cl = jnp.tile(jnp.arange(cols, dtype=jnp.float32), rows)[:, None] * omega
    return jnp.concatenate([jnp.sin(r), jnp.cos(r), jnp.sin(cl), jnp.cos(cl)], axis=-1).astype(dtype)


def _s5_discretize(lam_re, lam_im, log_step, b_re, b_im):
    lam_re = lam_re.astype(jnp.float32)
    lam_im = lam_im.astype(jnp.float32)
    b_re = b_re.astype(jnp.float32)
    b_im = b_im.astype(jnp.float32)
    dt = jnp.exp(log_step.astype(jnp.float32))[:, None]
    mag = jnp.exp(lam_re * dt)
    ang = lam_im * dt
    lb_re = mag * jnp.cos(ang)
    lb_im = mag * jnp.sin(ang)
    num_re = lb_re - 1.0
    num_im = lb_im
    den = lam_re * lam_re + lam_im * lam_im
    f_re = (num_re * lam_re + num_im * lam_im) / den
    f_im = (num_im * lam_re - num_re * lam_im) / den
    bt_re = f_re[..., None] * b_re - f_im[..., None] * b_im
    bt_im = f_re[..., None] * b_im + f_im[..., None] * b_re
    return lb_re, lb_im, bt_re, bt_im


def _scan_combine(e1, e2):
    a1r, a1i, b1r, b1i = e1
    a2r, a2i, b2r, b2i = e2
    return (a2r * a1r - a2i * a1i,
            a2r * a1i + a2i * a1r,
            a2r * b1r - a2i * b1i + b2r,
            a2r * b1i + a2i * b1r + b2i)


def _s5_states(u, lb_re, lb_im, bt_re, bt_im, h0, reverse):
    b, n, _ = u.shape
    ug = u.astype(jnp.float32).reshape(b, n, SSM_GROUPS, SSM_GROUP)
    bu_re = jnp.einsum('bngc,gpc->bngp', ug, bt_re)
    bu_im = jnp.einsum('bngc,gpc->bngp', ug, bt_im)
    if h0 is not None:
        h0_re, h0_im = h0
        edge = n - 1 if reverse else 0
        bu_re = bu_re.at[:, edge].add(lb_re * h0_re - lb_im * h0_im)
        bu_im = bu_im.at[:, edge].add(lb_re * h0_im + lb_im * h0_re)
    a_re = jnp.broadcast_to(lb_re, bu_re.shape)
    a_im = jnp.broadcast_to(lb_im, bu_im.shape)
    _, _, h_re, h_im = lax.associative_scan(_scan_combine, (a_re, a_im, bu_re, bu_im), reverse=reverse, axis=1)
    return h_re, h_im


def _s5_readout(h_re, h_im, c_re, c_im):
    b, n = h_re.shape[:2]
    y = jnp.einsum('bngp,gcp->bngc', h_re, c_re) - jnp.einsum('bngp,gcp->bngc', h_im, c_im)
    return y.reshape(b, n, SSM_WIDTH)


def _s5_bidirectional(u_lat, u_ctx, lam_re, lam_im, log_step, b_re, b_im, c_re, c_im, ctx_out):
    y_lat = []
    y_ctx = []
    for direction, reverse in ((0, False), (1, True)):
        lb_re, lb_im, bt_re, bt_im = _s5_discretize(lam_re[direction], lam_im[direction], log_step[direction], b_re[direction], b_im[direction])
        cr = c_re[direction].astype(jnp.float32)
        ci = c_im[direction].astype(jnp.float32)
        hc_re, hc_im = _s5_states(u_ctx, lb_re, lb_im, bt_re, bt_im, None, reverse)
        edge = 0 if reverse else -1
        h0 = (hc_re[:, edge], hc_im[:, edge])
        hl_re, hl_im = _s5_states(u_lat, lb_re, lb_im, bt_re, bt_im, h0, reverse)
        y_lat.append(_s5_readout(hl_re, hl_im, cr, ci))
        if ctx_out:
            y_ctx.append(_s5_readout(hc_re, hc_im, cr, ci))
    y_lat_sum = y_lat[0] + y_lat[1]
    y_ctx_sum = (y_ctx[0] + y_ctx[1]) if ctx_out else None
    return y_lat_sum, y_ctx_sum


def _mixer(p, y_ssm, ssm_d, w_glu, w_proj_a, dw_kernel, dw_bias, cn_g, cn_b, w_proj_b, w_out):
    o1 = SSM_WIDTH
    o2 = o1 + 2 * CONV_WIDTH
    o3 = o2 + D_MODEL
    u_a = p[..., :o1]
    v = p[..., o1:o2]
    g_a = p[..., o2:o3]
    g_b = p[..., o3:]
    ya = y_ssm.astype(u_a.dtype) + ssm_d * u_a
    ya = jax.nn.gelu(ya)
    ya = ya * jax.nn.sigmoid(ya @ w_glu)
    vb = v[..., :CONV_WIDTH] * jax.nn.sigmoid(v[..., CONV_WIDTH:])
    vb = lax.conv_general_dilated(vb, dw_kernel[:, None, :], window_strides=(1,),
                                  padding=((CONV_K // 2, CONV_K // 2),),
                                  dimension_numbers=('NWC', 'WIO', 'NWC'),
                                  feature_group_count=CONV_WIDTH) + dw_bias
    vb = jax.nn.silu(_layernorm(vb, cn_g, cn_b))
    merged = jax.nn.sigmoid(g_a) * (ya @ w_proj_a) + jax.nn.sigmoid(g_b) * (vb @ w_proj_b)
    return merged @ w_out


def _swiglu(h, wg, wu, wd):
    return (jax.nn.silu(h @ wg) * (h @ wu)) @ wd


def setup_inputs(seed: int = 0) -> dict:
    key = jax.random.key(seed)
    ks = jax.random.split(key, 32)
    f32 = jnp.float32

    def nrm(k, shape, scale):
        return jax.random.normal(k, shape, f32) * scale

    G, P, CH = SSM_GROUPS, SSM_STATE, SSM_GROUP
    lam_im_base = (jnp.pi * jnp.arange(P, dtype=f32))
    return {
        'x': nrm(ks[0], (BATCH, SEQ, D_MODEL), 1.0),
        'c': nrm(ks[1], (BATCH, D_MODEL), 1.0),
        'ctx': nrm(ks[2], (BATCH, CTX_LEN, D_MODEL), 1.0),
        'c_ctx': nrm(ks[3], (D_MODEL,), 1.0),
        'w_mod': nrm(ks[4], (DEPTH, D_MODEL, N_MOD * D_MODEL), 0.5 * D_MODEL ** -0.5),
        'b_mod': nrm(ks[5], (DEPTH, N_MOD * D_MODEL), 0.02),
        'norm1_g': 1.0 + nrm(ks[6], (DEPTH, D_MODEL), 0.05),
        'norm2_g': 1.0 + nrm(ks[7], (DEPTH, D_MODEL), 0.05),
        'w_in': nrm(ks[8], (DEPTH, D_MODEL, W_IN_COLS), D_MODEL ** -0.5),
        'ssm_lam_re': -0.5 + nrm(ks[9], (DEPTH, 2, G, P), 0.01),
        'ssm_lam_im': lam_im_base + nrm(ks[10], (DEPTH, 2, G, P), 0.01),
        'ssm_log_step': jax.random.uniform(ks[11], (DEPTH, 2, G), f32, math.log(STEP_MIN), math.log(STEP_MAX)),
        'ssm_b_re': nrm(ks[12], (DEPTH, 2, G, P, CH), (2 * CH) ** -0.5),
        'ssm_b_im': nrm(ks[13], (DEPTH, 2, G, P, CH), (2 * CH) ** -0.5),
        'ssm_c_re': nrm(ks[14], (DEPTH, 2, G, CH, P), (2 * P) ** -0.5),
        'ssm_c_im': nrm(ks[15], (DEPTH, 2, G, CH, P), (2 * P) ** -0.5),
        'ssm_d': nrm(ks[16], (DEPTH, SSM_WIDTH), 1.0),
        'w_glu': nrm(ks[17], (DEPTH, SSM_WIDTH, SSM_WIDTH), SSM_WIDTH ** -0.5),
        'w_proj_a': nrm(ks[18], (DEPTH, SSM_WIDTH, D_MODEL), SSM_WIDTH ** -0.5),
        'dw_kernel': nrm(ks[19], (DEPTH, CONV_K, CONV_WIDTH), CONV_K ** -0.5),
        'dw_bias': nrm(ks[20], (DEPTH, CONV_WIDTH), 0.02),
        'conv_norm_g': 1.0 + nrm(ks[21], (DEPTH, CONV_WIDTH), 0.05),
        'conv_norm_b': nrm(ks[22], (DEPTH, CONV_WIDTH), 0.02),
        'w_proj_b': nrm(ks[23], (DEPTH, CONV_WIDTH, D_MODEL), CONV_WIDTH ** -0.5),
        'w_out': nrm(ks[24], (DEPTH, D_MODEL, D_MODEL), D_MODEL ** -0.5),
        'w_ffn_gate': nrm(ks[25], (DEPTH, D_MODEL, FFN_HIDDEN), D_MODEL ** -0.5),
        'w_ffn_up': nrm(ks[26], (DEPTH, D_MODEL, FFN_HIDDEN), D_MODEL ** -0.5),
        'w_ffn_down': nrm(ks[27], (DEPTH, FFN_HIDDEN, D_MODEL), FFN_HIDDEN ** -0.5),
        'norm_f_g': 1.0 + nrm(ks[28], (D_MODEL,), 0.05),
    }


def reference(x, c, ctx, c_ctx, w_mod, b_mod, norm1_g, norm2_g, w_in,
              ssm_lam_re, ssm_lam_im, ssm_log_step, ssm_b_re, ssm_b_im, ssm_c_re, ssm_c_im, ssm_d,
              w_glu, w_proj_a, dw_kernel, dw_bias, conv_norm_g, conv_norm_b, w_proj_b, w_out,
              w_ffn_gate, w_ffn_up, w_ffn_down, norm_f_g):
    n_lat = x.shape[1]
    rows = n_lat // GRID_W
    x = x + _pos2d(rows, GRID_W, D_MODEL, x.dtype)[None]
    for l in range(DEPTH):
        last = l == DEPTH - 1
        mod = jax.nn.silu(c) @ w_mod[l] + b_mod[l]
        sh1, sc1, g1, sh2, sc2, g2 = jnp.split(mod[:, None, :], N_MOD, axis=-1)
        mod_c = jax.nn.silu(c_ctx) @ w_mod[l] + b_mod[l]
        csh1, csc1, cg1, csh2, csc2, cg2 = jnp.split(mod_c, N_MOD, axis=-1)

        h = _modulate(x, norm1_g[l], sh1, sc1)
        hc = _modulate(ctx, norm1_g[l], csh1, csc1)
        p = h @ w_in[l]
        pc = hc @ (w_in[l][:, :SSM_WIDTH] if last else w_in[l])
        y_lat, y_ctx = _s5_bidirectional(p[..., :SSM_WIDTH], pc[..., :SSM_WIDTH],
                                         ssm_lam_re[l], ssm_lam_im[l], ssm_log_step[l],
                                         ssm_b_re[l], ssm_b_im[l], ssm_c_re[l], ssm_c_im[l],
                                         not last)
        mix_params = (ssm_d[l], w_glu[l], w_proj_a[l], dw_kernel[l], dw_bias[l],
                      conv_norm_g[l], conv_norm_b[l], w_proj_b[l], w_out[l])
        x = x + g1 * _mixer(p, y_lat, *mix_params)
        x = x + g2 * _swiglu(_modulate(x, norm2_g[l], sh2, sc2), w_ffn_gate[l], w_ffn_up[l], w_ffn_down[l])
        if not last:
            ctx = ctx + cg1 * _mixer(pc, y_ctx, *mix_params)
            ctx = ctx + cg2 * _swiglu(_modulate(ctx, norm2_g[l], csh2, csc2), w_ffn_gate[l], w_ffn_up[l], w_ffn_down[l])
    return _rmsnorm(x, norm_f_g)
```

```python
import math
import numpy as np
import concourse.bass as bass
import concourse.mybir as mybir
from concourse.bass_utils import run_bass_kernel_spmd

F32 = mybir.dt.float32
BF16 = mybir.dt.bfloat16
ALU = mybir.AluOpType
AF = mybir.ActivationFunctionType

NCORES = 8
D = 1024
T = 2048
TC = 256
TT = T + TC
DEPTH = 2
FFN = 2816
WIN = 4608
PI = math.pi
MAGIC = 12582912.0
BLOCKS = [(0, 512), (512, 512), (1024, 512), (1536, 512), (2048, 256)]


class Prog:
    ENG = ("sync", "act", "dve", "pool", "pe")

    def __init__(self, nc):
        self.nc = nc
        self.ops = []
        self.lw = {}
        self.rd = {}

    def _rec(self, eng, fn, r, w, dma):
        i = len(self.ops)
        deps = {}
        for b in r:
            j = self.lw.get(b)
            if j is not None:
                deps[j] = deps.get(j, False) or True
        for b in w:
            j = self.lw.get(b)
            if j is not None:
                deps[j] = True
            for j in self.rd.get(b, ()):
                if j not in deps:
                    deps[j] = False
        for b in r:
            self.rd.setdefault(b, []).append(i)
        for b in w:
            self.lw[b] = i
            self.rd[b] = []
        keep = []
        for j, strong in deps.items():
            oj = self.ops[j]
            if oj["eng"] == eng and not oj["dma"] and not dma:
                if eng == "pe" or not strong:
                    continue
            keep.append(j)
        self.ops.append(dict(eng=eng, fn=fn, deps=keep, dma=dma, sig=None))
        return i

    def op(self, eng, fn, r=(), w=()):
        return self._rec(eng, fn, tuple(r), tuple(w), False)

    def dma(self, eng, out, in_, r=(), w=()):
        return self._rec(eng, lambda e: e.dma_start(out=out, in_=in_), tuple(r), tuple(w), True)

    def emit(self, ndma_sems=64):
        nc = self.nc
        waited_on = set()
        for o in self.ops:
            for j in o["deps"]:
                waited_on.add(j)
        esem = {e: nc.alloc_semaphore("s_" + e) for e in self.ENG}
        dsem = [nc.alloc_semaphore("d%d" % k) for k in range(ndma_sems)]
        ecnt = {e: 0 for e in self.ENG}
        dcnt = [0] * ndma_sems
        nd = 0
        for i, o in enumerate(self.ops):
            if (i in waited_on or o["dma"]) and o["fn"] is not None:
                if o["dma"]:
                    k = nd % ndma_sems
                    nd += 1
                    dcnt[k] += 16
                    o["sig"] = (("d", k), dsem[k], dcnt[k], 16)
                else:
                    ecnt[o["eng"]] += 1
                    o["sig"] = (("e", o["eng"]), esem[o["eng"]], ecnt[o["eng"]], 1)
        per = {e: [] for e in self.ENG}
        for o in self.ops:
            per[o["eng"]].append(o)
        ops = self.ops

        def run(eng_name, e):
            waited = {}
            for o in per[eng_name]:
                for j in sorted(o["deps"]):
                    s = ops[j]["sig"]
                    if s is None:
                        continue
                    ch, sem, val, _ = s
                    if waited.get(ch, 0) >= val:
                        continue
                    waited[ch] = val
                    e.wait_ge(sem, val)
                if o["fn"] is None:
                    continue
                if o["dma"] and o["sig"][2] > 16:
                    ch = o["sig"][0]
                    if waited.get(ch, 0) < o["sig"][2] - 16:
                        waited[ch] = o["sig"][2] - 16
                        e.wait_ge(o["sig"][1], o["sig"][2] - 16)
                ins = o["fn"](e)
                if o["sig"] is not None:
                    ins.then_inc(o["sig"][1], o["sig"][3])

        with nc.Block() as block:
            @block.sync
            def _(e):
                run("sync", e)

            @block.scalar
            def _(e):
                run("act", e)

            @block.vector
            def _(e):
                run("dve", e)

            @block.gpsimd
            def _(e):
                run("pool", e)

            @block.tensor
            def _(e):
                run("pe", e)


NPAY = 64 + 240 + 64
VP = 15 + T + 15 + 15 + TC + 15
VC0 = 15 + T + 15 + 15
HALF = TT // 3
S5_DIRS = (0, 1)


def build(stage):
    nc = bass.Bass("TRN2", target_bir_lowering=False)
    P = Prog(nc)

    def din(name, shape, dt=F32):
        return nc.dram_tensor(name, list(shape), dt, kind="ExternalInput").ap()

    def dout(name, shape, dt=F32):
        return nc.dram_tensor(name, list(shape), dt, kind="ExternalOutput").ap()

    first = stage == "A0"
    last = stage == "B1"
    if first:
        xT_d = din("xT", [128, 8, T])
        cT_d = din("cT", [128, 8, TC])
        omega_d = din("omega", [128, 2])
        ridx_d = din("ridx", [1, T])
        cidx_d = din("cidx", [1, T])
    else:
        xin_d = din("xsp_i", [128, 8, TT])
        hin_d = din("hsp_i", [128, 8, TT], BF16)
        uin_d = din("u_i", [128, 4, TT], BF16)
        vin_d = din("vbp_i", [128, 8, VP], BF16)
        ccs_d = din("ccs_i", [128, NCORES, NPAY])
    cvec_d = din("cvec", [128, 8, 2])
    tau_d = din("tau", [1, T])
    sel_d = din("sel", [128, 24])
    w_mod_d = din("w_mod", [DEPTH, D, 6 * D])
    bmod_d = din("bmodT", [128, DEPTH, 48])
    n1g_d = din("n1gT", [128, DEPTH, 8])
    n2g_d = din("n2gT", [128, DEPTH, 8])
    w_in_d = din("w_in", [DEPTH, D, WIN])
    P_l_d = din("P_l", [128, 3, 64])
    P_c_d = din("P_c", [DEPTH, 2, 2, 128, 512])
    E_l_d = din("E_l", [DEPTH, 2, 3, 128, 512])
    E_b_d = din("E_b", [DEPTH, 2, 2, 128, 512])
    ssmd_d = din("ssmdT", [128, DEPTH, 4])
    w_glu_d = din("w_glu", [DEPTH, 512, 512])
    w_pa_d = din("w_proj_a", [DEPTH, 512, D])
    dwk_d = din("dwkT", [128, DEPTH, 8, 31])
    dwb_d = din("dwbT", [128, DEPTH, 8])
    cng_d = din("cngT", [128, DEPTH, 8])
    cnb_d = din("cnbT", [128, DEPTH, 8])
    w_pb_d = din("w_proj_b", [DEPTH, D, D])
    w_out_d = din("w_out", [DEPTH, D, D])
    w_fg_d = din("w_ffn_gate", [DEPTH, D, FFN])
    w_fu_d = din("w_ffn_up", [DEPTH, D, FFN])
    w_fd_d = din("w_ffn_down", [DEPTH, FFN, D])
    nfg_d = din("nfgT", [128, 8])
    ident_d = din("ident", [128, 128])
    if last:
        out_d = dout("outT", [128, 8, T])
    else:
        xsp_o = dout("xsp_o", [128, 8, TT])
        hsp_o = dout("hsp_o", [128, 8, TT], BF16)
        u_o = dout("u_o", [128, 4, TT], BF16)
        vbp_o = dout("vbp_o", [128, 8, VP], BF16)
        pay_o = dout("pay_o", [128, NPAY])

    _n = [0]

    def sb(name, shape, dt, off):
        _n[0] += 1
        return nc.alloc_sbuf_tensor_at("%s_%d" % (name, _n[0]), [128] + list(shape), dt, offset=off)

    O_PAR, O_W, O_RSTD, O_A, O_B, O_C, O_D, O_E = 16512, 28800, 41088, 50304, 124032, 160896, 179328, 217152
    END = 229376
    po = [O_PAR]

    def par(name, shape, dt=F32):
        sz = int(np.prod(shape)) * (4 if dt == F32 else 2)
        sz = (sz + 31) // 32 * 32
        t = sb(name, shape, dt, po[0])
        po[0] += sz
        assert po[0] <= O_W, po[0]
        return t

    modv = par("modv", [48, 2])
    gs1 = par("gs1", [8, 2])
    gs2 = par("gs2", [8, 2])
    bmod = par("bmod", [DEPTH, 48])
    n1g = par("n1g", [DEPTH, 8])
    n2g = par("n2g", [DEPTH, 8])
    nfg = par("nfg", [8])
    ssmd = par("ssmd", [DEPTH, 4])
    dwk = par("dwk", [DEPTH, 8, 31])
    dwb = par("dwb", [DEPTH, 8])
    cng = par("cng", [DEPTH, 8])
    cnb = par("cnb", [DEPTH, 8])
    omega = par("omega", [2])
    sel = par("sel", [24])
    csil = par("csil", [8, 2], BF16)
    cvec = par("cvec", [8, 2])
    ones_bf = par("ones_bf", [128], BF16)
    Pl = par("Pl", [3, 64])
    Pq = par("Pq", [14, 64])
    hst = par("hst", [10, 32])
    ini = par("ini", [4])
    e6 = par("e6", [8])
    halfpi = par("halfpi", [1])
    scr = par("scr", [1])
    ident = par("ident", [128], BF16)

    Wb = [sb("Wb%d" % i, [8, 384], BF16, O_W + i * 6144) for i in range(2)]
    rstd = sb("rstd", [TT], F32, O_RSTD)
    xT = sb("xT", [8, TT], F32, O_A)
    hT = sb("hT", [8, TT], BF16, O_B)
    uT = sb("uT", [4, TT], BF16, O_C)
    vbp = sb("vbp", [8, VP], BF16, O_D)
    acc0 = sb("acc0", [TT], F32, O_E)
    acc1 = sb("acc1", [TT], F32, O_B)
    o = O_A
    tau = sb("tau", [T], F32, o); o += 8192
    SN = sb("SN", [T], F32, o); o += 8192
    CS = sb("CS", [T], F32, o); o += 8192
    W1 = sb("W1", [TT], F32, o); o += 9216
    W2 = sb("W2", [TT], F32, o); o += 9216
    G1 = sb("G1", [TT], F32, o); o += 9216
    G2 = sb("G2", [TT], F32, o); o += 9216
    HHf = sb("HHf", [TT], F32, o)
    hre = sb("hre", [TT], BF16, o); o += 4608
    him = sb("him", [TT], BF16, o); o += 4608
    assert o <= O_B
    ET = [sb("ET%d" % i, [512], F32, O_A + 24576 + i * 2048) for i in range(16)]
    yacc = sb("yacc", [4, TT], F32, O_B)
    tabB = sb("tabB", [2, 2, 512], BF16, O_RSTD)
    tabC = sb("tabC", [2, 2, 512], BF16, O_RSTD + 4096)
    rmul = sb("rmul", [T], F32, O_E)
    ccs = sb("ccs", [NCORES, NPAY], F32, O_W)
    pay = sb("pay", [NPAY], F32, O_E)
    hal = sb("hal", [2, 8, 15], F32, O_E + 2048)
    ya = sb("ya", [4, TT], BF16, O_A)
    ya2 = sb("ya2", [4, TT], BF16, O_A + 18432)
    cvo = sb("cvo", [8, TT], BF16, O_A + 36864)
    gt0 = sb("gt0", [TT], F32, O_E)
    gt1 = sb("gt1", [TT], F32, O_C)
    dg = sb("dg", [31, 128], BF16, O_C + 9216)
    sqt = sb("sqt", [512], BF16, O_C + 9216 + 7936)
    mrg = sb("mrg", [8, TT], BF16, O_D)
    actT = sb("actT", [22, HALF], BF16, O_C)
    Wd = [sb("Wd%d" % i, [22, 128], BF16, O_C + 22 * HALF * 2 + i * 5632) for i in range(2)]
    assert O_C + 22 * HALF * 2 + 2 * 5632 <= O_E

    ps = [nc.alloc_psum_tensor("ps%d" % i, [128, 512], F32) for i in range(8)]

    def bkey(name, *idx):
        return (name,) + idx

    XK = [bkey("xT", k) for k in range(8)]
    ACTK = [bkey("actT", m) for m in range(22)]
    WDK = [bkey("Wd", 0), bkey("Wd", 1)]
    S5K = ["tau", "SN", "CS", "W1", "W2", "G1", "G2", "HH"]

    def tt(eng, out, a, b, op, r, w):
        P.op(eng, lambda e: e.tensor_tensor(out=out, in0=a, in1=b, op=op), r, w)

    def ts(eng, out, a, s1, s2, op0, op1, r, w):
        if op1 is None:
            P.op(eng, lambda e: e.tensor_scalar(out=out, in0=a, scalar1=s1, scalar2=None, op0=op0), r, w)
        else:
            P.op(eng, lambda e: e.tensor_scalar(out=out, in0=a, scalar1=s1, scalar2=s2, op0=op0, op1=op1), r, w)

    def stt(out, a, s, b, op0, op1, r, w):
        P.op("dve", lambda e: e.scalar_tensor_tensor(out=out, in0=a, scalar=s, in1=b, op0=op0, op1=op1), r, w)

    def act(out, a, func, r, w, bias=None, scale=None):
        kw = {}
        if bias is not None:
            kw["bias"] = bias
        if scale is not None:
            kw["scale"] = scale
        P.op("act", lambda e: e.activation(out=out, in_=a, func=func, **kw), r, w)

    def recip(out, a, r, w):
        P.op("dve", lambda e: e.reciprocal(out=out, in_=a), r, w)

    def cp(eng, out, a, r, w):
        P.op(eng, lambda e: e.tensor_copy(out=out, in_=a), r, w)

    def mset(eng, out, val, w):
        P.op(eng, lambda e: e.memset(out, val), (), w)

    def mm(out, lhsT, rhs, start, stop, r, w, tp=None):
        if tp is None:
            P.op("pe", lambda e: e.matmul(out, lhsT=lhsT, rhs=rhs, start=start, stop=stop), r, w)
        else:
            P.op("pe", lambda e: e.matmul(out, lhsT=lhsT, rhs=rhs, start=start, stop=stop, tile_position=tp), r, w)

    def scan(out, d0, d1, init, r, w):
        P.op("dve", lambda e: e.tensor_tensor_scan(out=out, data0=d0, data1=d1, initial=init, op0=ALU.mult,
                                                   op1=ALU.add), r, w)

    def barrier(keys):
        P.op("dve", lambda e: e.memset(scr[:], 0.0), (), list(keys))

    def ld(out, in_, r=(), w=(), eng="sync"):
        P.dma(eng, out, in_, r, w)

    WK = [bkey("Wb", 0), bkey("Wb", 1)]

    plist = [(bmod, bmod_d), (n1g, n1g_d), (n2g, n2g_d), (nfg, nfg_d), (ssmd, ssmd_d), (dwk, dwk_d),
             (dwb, dwb_d), (cng, cng_d), (cnb, cnb_d), (sel, sel_d), (cvec, cvec_d)]
    if first:
        plist.append((omega, omega_d))
    for t_, d_ in plist:
        ld(t_[:], d_, w=["par"])
    ld(Pl[:], P_l_d, w=["Pl"])
    P.dma("pool", ident[:], ident_d, w=["par"])
    mset("dve", ones_bf[:], 1.0, ["par"])
    mset("dve", halfpi[:], PI / 2, ["par"])
    mset("dve", e6[:], 1e-6, ["par"])
    act(csil[:], cvec[:], AF.Silu, ["par"], ["csil"])

    def sincos(eng, s_out, c_out, u, tmp, r, w):
        ts(eng, tmp, u, MAGIC, None, ALU.add, None, r, w)
        ts(eng, tmp, tmp, -MAGIC, None, ALU.add, None, w, w)
        tt(eng, tmp, u, tmp, ALU.subtract, list(r) + list(w), w)
        act(c_out, tmp, AF.Abs, w, w)
        act(c_out, c_out, AF.Sin, list(w) + ["par"], w, bias=halfpi[:, 0:1], scale=-2 * PI)
        act(s_out, tmp, AF.Sin, w, w, scale=2 * PI)

    def disc(lre, lim, ls, r_o, cth, sth, dt_o, th_o, tmp, rk, wk):
        act(dt_o, ls, AF.Exp, rk, wk)
        tt("dve", r_o, lre, dt_o, ALU.mult, list(rk) + list(wk), wk)
        tt("dve", th_o, lim, dt_o, ALU.mult, list(rk) + list(wk), wk)
        ts("dve", th_o, th_o, 1.0 / (2 * PI), None, ALU.mult, None, wk, wk)
        sincos("dve", sth, cth, th_o, tmp, wk, wk)

    PQ = lambda i: Pq[:, i, :]
    disc(Pl[:, 0, :], Pl[:, 1, :], Pl[:, 2, :], PQ(7), PQ(2), PQ(3), PQ(6), PQ(1), PQ(8), ["Pl", "par"], ["Pq"])
    act(PQ(0), PQ(7), AF.Exp, ["Pq"], ["Pq"])
    act(PQ(9), PQ(7), AF.Exp, ["Pq"], ["Pq"], scale=float(T))
    ts("dve", PQ(10), PQ(1), float(T), None, ALU.mult, None, ["Pq"], ["Pq"])
    sincos("dve", PQ(11), PQ(12), PQ(10), PQ(8), ["Pq"], ["Pq"])
    tt("dve", PQ(4), PQ(9), PQ(12), ALU.mult, ["Pq"], ["Pq"])
    tt("dve", PQ(5), PQ(9), PQ(11), ALU.mult, ["Pq"], ["Pq"])

    if first:
        for k in range(8):
            ld(xT[:, k, 0:T], xT_d[:, k, :], w=[XK[k]])
            ld(xT[:, k, T:TT], cT_d[:, k, :], w=[XK[k]])
        for k in range(8):
            idx_d = ridx_d if k < 4 else cidx_d
            if k in (0, 4):
                ld(acc0[:, 0:T], idx_d[0:1, :].partition_broadcast(128), w=["acc0"])
            ph = (PI / 2 if (k // 2) % 2 == 1 else 0.0)
            ts("dve", acc1[:, 0:T], acc0[:, 0:T], omega[:, k % 2:k % 2 + 1], ph, ALU.mult, ALU.add,
               ["acc0", "par"], ["hT"])
            ts("dve", rstd[:, 0:T], acc1[:, 0:T], 1.0 / (2 * PI), MAGIC, ALU.mult, ALU.add, ["hT"], ["rstd"])
            ts("dve", rstd[:, 0:T], rstd[:, 0:T], -MAGIC, None, ALU.add, None, ["rstd"], ["rstd"])
            stt(acc1[:, 0:T], rstd[:, 0:T], -2 * PI, acc1[:, 0:T], ALU.mult, ALU.add, ["rstd", "hT"], ["hT"])
            act(acc1[:, 0:T], acc1[:, 0:T], AF.Sin, ["hT", "par"], ["hT"])
            tt("dve", xT[:, k, 0:T], xT[:, k, 0:T], acc1[:, 0:T], ALU.add, ["hT", XK[k]], [XK[k]])

    wslot = [0]

    def wload(Wd_ap, KT, c0, cw, wbufs=None, wkey="Wb"):
        wbufs = wbufs or Wb
        s = wslot[0] % 2
        wslot[0] += 1
        wb = wbufs[s]
        src = Wd_ap[:, c0:c0 + cw].rearrange("(k p) f -> p k f", p=128)
        P.dma("pool", wb[:, 0:KT, 0:cw], src, w=[bkey(wkey, s)])
        return wb, bkey(wkey, s)

    pscnt = [0]

    def stream_proj(Wd_ap, KT, ncols, inT, inkeys, evac, col0=0, blocks=BLOCKS, chunk=384):
        nchunk = (ncols + chunk - 1) // chunk
        for c in range(nchunk):
            cw = min(chunk, ncols - c * chunk)
            wb, wk = wload(Wd_ap, KT, col0 + c * chunk, cw)
            for mi in range(cw // 128):
                m = c * (chunk // 128) + mi
                for bi, (b0, bn) in enumerate(blocks):
                    pi = pscnt[0] % 4
                    pscnt[0] += 1
                    for k in range(KT):
                        mm(ps[pi][:, 0:bn], wb[:, k, mi * 128:(mi + 1) * 128], inT[:, k, b0:b0 + bn],
                           k == 0, k == KT - 1, [wk] + list(inkeys), [bkey("ps", pi)])
                    evac(m, bi, b0, bn, ps[pi][:, 0:bn], bkey("ps", pi))

    def rms_stats():
        for bi, (b0, bn) in enumerate(BLOCKS):
            for k in range(8):
                act(hT[:, k, b0:b0 + bn], xT[:, k, b0:b0 + bn], AF.Square, [XK[k]], ["hT"])
            for k in range(8):
                mm(ps[4][:, 0:bn], ones_bf[:, :], hT[:, k, b0:b0 + bn], k == 0, k == 7, ["hT", "par"], [bkey("ps", 4)])
            act(rstd[:, b0:b0 + bn], ps[4][:, 0:bn], AF.Sqrt, [bkey("ps", 4), "par"], ["rstd"], bias=e6[:, 0:1],
                scale=1.0 / D)
            recip(rstd[:, b0:b0 + bn], rstd[:, b0:b0 + bn], ["rstd"], ["rstd"])

    def modulate(gsv, shoff):
        for k in range(8):
            for (c0, c1, mc) in ((0, T, 0), (T, TT, 1)):
                tt("dve", acc0[:, c0:c1], xT[:, k, c0:c1], rstd[:, c0:c1], ALU.mult, [XK[k], "rstd"], ["acc0"])
                act(hT[:, k, c0:c1], acc0[:, c0:c1], AF.Identity, ["acc0", "modv"], ["hT"],
                    bias=modv[:, shoff + k, mc:mc + 1], scale=gsv[:, k, mc:mc + 1])

    def mod_vectors(l):
        for c in range(16):
            wb, wk = wload(w_mod_d[l], 8, c * 384, 384)
            for mi in range(3):
                m = c * 3 + mi
                for k in range(8):
                    mm(ps[5][:, 2 * m:2 * m + 2], wb[:, k, mi * 128:(mi + 1) * 128], csil[:, k, :], k == 0, k == 7,
                       [wk, "csil"], [bkey("ps", 5)])
        for c2 in range(2):
            tt("dve", modv[:, :, c2], ps[5][:, c2:96:2], bmod[:, l, :], ALU.add, [bkey("ps", 5), "par"], ["modv"])
        for (gsv, ng, off) in ((gs1, n1g, 8), (gs2, n2g, 32)):
            for c2 in range(2):
                stt(gsv[:, :, c2], modv[:, off:off + 8, c2], 1.0, ng[:, l, :], ALU.add, ALU.mult,
                    ["modv", "par"], ["modv"])

    def s5_tables(l):
        barrier(["rstd", "tabB", "tabC", "W1", "W2", "G1", "G2", "ET"])
        for d in S5_DIRS:
            lre, lim, ls, bre, bim = ET[0], ET[1], ET[2], ET[3], ET[4]
            for i, tdst in enumerate((lre, lim, ls)):
                ld(tdst[:], E_l_d[l, d, i], w=["ET"])
            for i, tdst in enumerate((bre, bim)):
                ld(tdst[:], E_b_d[l, d, i], w=["ET"])
            r_, cth, sth, dt_, th_, tmp = ET[5], ET[6], ET[7], ET[8], ET[9], ET[10]
            disc(lre[:], lim[:], ls[:], r_[:], cth[:], sth[:], dt_[:], th_[:], tmp[:], ["ET", "par"], ["ET"])
            act(r_[:], r_[:], AF.Exp, ["ET"], ["ET"])
            A_, Bn, den, fre, fim, t1 = ET[11], ET[12], ET[13], ET[14], ET[15], ET[8]
            tt("dve", A_[:], r_[:], cth[:], ALU.mult, ["ET"], ["ET"])
            ts("dve", A_[:], A_[:], -1.0, None, ALU.add, None, ["ET"], ["ET"])
            tt("dve", Bn[:], r_[:], sth[:], ALU.mult, ["ET"], ["ET"])
            tt("dve", den[:], lre[:], lre[:], ALU.mult, ["ET"], ["ET"])
            tt("dve", t1[:], lim[:], lim[:], ALU.mult, ["ET"], ["ET"])
            tt("dve", den[:], den[:], t1[:], ALU.add, ["ET"], ["ET"])
            recip(den[:], den[:], ["ET"], ["ET"])
            tt("dve", fre[:], A_[:], lre[:], ALU.mult, ["ET"], ["ET"])
            tt("dve", t1[:], Bn[:], lim[:], ALU.mult, ["ET"], ["ET"])
            tt("dve", fre[:], fre[:], t1[:], ALU.add, ["ET"], ["ET"])
            tt("dve", fre[:], fre[:], den[:], ALU.mult, ["ET"], ["ET"])
            tt("dve", fim[:], Bn[:], lre[:], ALU.mult, ["ET"], ["ET"])
            tt("dve", t1[:], A_[:], lim[:], ALU.mult, ["ET"], ["ET"])
            tt("dve", fim[:], fim[:], t1[:], ALU.subtract, ["ET"], ["ET"])
            tt("dve", fim[:], fim[:], den[:], ALU.mult, ["ET"], ["ET"])
            t2 = ET[9]
            tt("dve", t1[:], fre[:], bre[:], ALU.mult, ["ET"], ["ET"])
            tt("dve", t2[:], fim[:], bim[:], ALU.mult, ["ET"], ["ET"])
            tt("dve", tabB[:, d, 0, :], t1[:], t2[:], ALU.subtract, ["ET"], ["tabB"])
            tt("dve", t1[:], fre[:], bim[:], ALU.mult, ["ET"], ["ET"])
            tt("dve", t2[:], fim[:], bre[:], ALU.mult, ["ET"], ["ET"])
            tt("dve", tabB[:, d, 1, :], t1[:], t2[:], ALU.add, ["ET"], ["tabB"])
            ld(ET[0][:], P_c_d[l, d, 0], w=["ET"])
            ld(ET[1][:], P_c_d[l, d, 1], w=["ET"])
            act(tabC[:, d, 0, :], ET[0][:], AF.Copy, ["ET"], ["tabC"])
            act(tabC[:, d, 1, :], ET[1][:], AF.Copy, ["ET"], ["tabC"], scale=-1.0)
        barrier(["W1", "W2", "G1", "G2", "ET"])

    def rot(o1, o2, x1, x2, tmp, cs, sn, r, w, tkey):
        tt("pool", o1, x1, cs, ALU.mult, r, w)
        tt("dve", tmp, x2, sn, ALU.mult, r, [tkey])
        tt("pool", o1, o1, tmp, ALU.add, list(w) + [tkey], w)
        tt("pool", o2, x1, sn, ALU.mult, list(r) + list(w) + [tkey], w)
        tt("dve", tmp, x2, cs, ALU.mult, list(r) + list(w), [tkey])
        tt("pool", o2, o2, tmp, ALU.subtract, list(w) + [tkey], w)

    SEGS = ((T, TT, TC), (0, T, T))

    def s5_tile(l, d, j, phase):
        c = (l * 2 + d) * 16 + j
        dj = d * 16 + j
        q, a = j // 4, j % 4
        pcol = lambda i: Pq[:, i, c:c + 1]
        rev = d == 1

        def tab(t_, n):
            return t_[:, n - 1::-1] if rev and n > 1 else (t_[:, 0:n] if not rev else t_[:, 0:1])

        def tabn(t_, n):
            if not rev:
                return t_[:, 0:n]
            return t_[:, n - 1::-1] if n > 1 else t_[:, 0:1]

        ts("pool", SN[:], tau[:], pcol(1), MAGIC, ALU.mult, ALU.add, ["tau", "Pq"], ["SN"])
        ts("dve", SN[:], SN[:], -MAGIC, None, ALU.add, None, ["SN"], ["SN"])
        stt(SN[:], tau[:], pcol(1), SN[:], ALU.mult, ALU.subtract, ["tau", "Pq", "SN"], ["SN"])
        act(CS[:], SN[:], AF.Abs, ["SN"], ["CS"])
        act(CS[:], CS[:], AF.Sin, ["CS", "par"], ["CS"], bias=halfpi[:, 0:1], scale=-2 * PI)
        act(SN[:], SN[:], AF.Sin, ["SN"], ["SN"], scale=2 * PI)
        ts("pool", rmul[:], tau[:], 0.0, pcol(0), ALU.mult, ALU.add, ["tau", "Pq"], ["acc0"])
        for bi, (b0, bn) in enumerate(BLOCKS):
            p0 = 2 * (bi % 2)
            for ri in range(2):
                mm(ps[p0 + ri][:, 0:bn], tabB[32 * a:32 * a + 32, d, ri, q * 128:(q + 1) * 128],
                   uT[32 * a:32 * a + 32, q, b0:b0 + bn], True, True, ["tabB", "uT", "rstd"], [bkey("ps", p0 + ri)],
                   tp=(32 * a, 0))
            act(G1[:, b0:b0 + bn], ps[p0][:, 0:bn], AF.Copy, [bkey("ps", p0)], ["G1"])
            act(G2[:, b0:b0 + bn], ps[p0 + 1][:, 0:bn], AF.Copy, [bkey("ps", p0 + 1)], ["G2"])
        for (c0, c1, n) in SEGS:
            rot(W1[:, c0:c1], W2[:, c0:c1], G1[:, c0:c1], G2[:, c0:c1], HHf[:, c0:c1], tabn(CS, n), tabn(SN, n),
                ["G1", "G2", "CS", "SN"], ["W1", "W2"], "HH")
        if phase == 2:
            hr, hi_ = hst[:, 4, dj:dj + 1], hst[:, 5, dj:dj + 1]
            tt("dve", ini[:, 0:1], hr, pcol(2), ALU.mult, ["hst", "Pq"], ["ini"])
            tt("dve", ini[:, 2:3], hi_, pcol(3), ALU.mult, ["hst", "Pq"], ["ini"])
            tt("dve", ini[:, 0:1], ini[:, 0:1], ini[:, 2:3], ALU.subtract, ["ini"], ["ini"])
            tt("dve", ini[:, 1:2], hr, pcol(3), ALU.mult, ["hst", "Pq", "ini"], ["ini"])
            tt("dve", ini[:, 2:3], hi_, pcol(2), ALU.mult, ["hst", "Pq", "ini"], ["ini"])
            tt("dve", ini[:, 1:2], ini[:, 1:2], ini[:, 2:3], ALU.add, ["ini"], ["ini"])
            ts("dve", ini[:, 1:2], ini[:, 1:2], -1.0, None, ALU.mult, None, ["ini"], ["ini"])
        for (c0, c1, n) in SEGS:
            for gi, (Gx, Wx, gk, wk) in enumerate(((G1, W1, "G1", "W1"), (G2, W2, "G2", "W2"))):
                if phase == 2 and c0 == 0:
                    init = ini[:, gi:gi + 1]
                else:
                    init = 0.0
                if not rev:
                    scan(Gx[:, c0:c1], rmul[:, 0:n], Wx[:, c0:c1], init, [wk, "acc0", "ini"], [gk])
                else:
                    lo = c0 - 1 if c0 > 0 else None
                    scan(Gx[:, c1 - 1:lo:-1], rmul[:, 0:n], Wx[:, c1 - 1:lo:-1], init, [wk, "acc0", "ini"], [gk])
        if phase == 1:
            for (c0, c1, n), row in zip(SEGS, (0, 2)):
                ecol = c1 - 1 if not rev else c0
                tcol = n - 1
                g1, g2 = G1[:, ecol:ecol + 1], G2[:, ecol:ecol + 1]
                cs_, sn_ = CS[:, tcol:tcol + 1], SN[:, tcol:tcol + 1]
                hr, hi_ = hst[:, row, dj:dj + 1], hst[:, row + 1, dj:dj + 1]
                tt("dve", hr, g1, cs_, ALU.mult, ["G1", "CS"], ["hst"])
                tt("dve", ini[:, 3:4], g2, sn_, ALU.mult, ["G2", "SN"], ["ini"])
                tt("dve", hr, hr, ini[:, 3:4], ALU.add, ["hst", "ini"], ["hst"])
                tt("dve", hi_, g1, sn_, ALU.mult, ["G1", "SN"], ["hst"])
                tt("dve", ini[:, 3:4], g2, cs_, ALU.mult, ["G2", "CS", "hst"], ["ini"])
                tt("dve", hi_, hi_, ini[:, 3:4], ALU.subtract, ["hst", "ini"], ["hst"])
            return
        for (c0, c1, n) in SEGS:
            cs_, sn_ = tabn(CS, n), tabn(SN, n)
            tt("pool", W1[:, c0:c1], G1[:, c0:c1], cs_, ALU.mult, ["G1", "CS"], ["W1"])
            tt("dve", W2[:, c0:c1], G2[:, c0:c1], sn_, ALU.mult, ["G2", "SN"], ["W2"])
            tt("pool", hre[:, c0:c1], W1[:, c0:c1], W2[:, c0:c1], ALU.add, ["W1", "W2"], ["HH"])
            tt("pool", W1[:, c0:c1], G1[:, c0:c1], sn_, ALU.mult, ["G1", "SN", "HH"], ["W1"])
            tt("dve", W2[:, c0:c1], G2[:, c0:c1], cs_, ALU.mult, ["G2", "CS", "HH"], ["W2"])
            tt("pool", him[:, c0:c1], W1[:, c0:c1], W2[:, c0:c1], ALU.subtract, ["W1", "W2"], ["HH"])
        for bi, (b0, bn) in enumerate(BLOCKS):
            pi = 4 + bi % 2
            mm(ps[pi][32 * a:32 * a + 32, 0:bn], tabC[:, d, 0, j * 32:(j + 1) * 32], hre[:, b0:b0 + bn], True, False,
               ["tabC", "HH", "rstd"], [bkey("ps", pi)], tp=(0, 32 * a))
            mm(ps[pi][32 * a:32 * a + 32, 0:bn], tabC[:, d, 1, j * 32:(j + 1) * 32], him[:, b0:b0 + bn], False, True,
               ["tabC", "HH", "rstd"], [bkey("ps", pi)], tp=(0, 32 * a))
            ydst = yacc[32 * a:32 * a + 32, q, b0:b0 + bn]
            if d == S5_DIRS[0]:
                cp("dve", ydst, ps[pi][32 * a:32 * a + 32, 0:bn], [bkey("ps", pi)], ["yacc"])
            else:
                tt("dve", ydst, ydst, ps[pi][32 * a:32 * a + 32, 0:bn], ALU.add, [bkey("ps", pi), "yacc"], ["yacc"])

    def layer_A(l):
        mod_vectors(l)
        rms_stats()
        modulate(gs1, 0)
        def ev_u(m, bi, b0, bn, p_, pk):
            act(uT[:, m, b0:b0 + bn], p_, AF.Copy, [pk], ["uT"])
        stream_proj(w_in_d[l], 8, 512, hT, ["hT"], ev_u)
        mset("pool", vbp[:, :, 0:15], 0.0, ["vbp"])
        mset("pool", vbp[:, :, 15 + T:VC0], 0.0, ["vbp"])
        mset("pool", vbp[:, :, VC0 + TC:VP], 0.0, ["vbp"])
        for m in range(8):
            wa, wka = wload(w_in_d[l], 8, 512 + m * 128, 128)
            wg, wkg = wload(w_in_d[l], 8, 1536 + m * 128, 128)
            for bi, (b0, bn) in enumerate(BLOCKS):
                pa, pg = (bi % 2) * 2, (bi % 2) * 2 + 1
                for k in range(8):
                    mm(ps[pg][:, 0:bn], wg[:, k, 0:128], hT[:, k, b0:b0 + bn], k == 0, k == 7, [wkg, "hT"], [bkey("ps", pg)])
                for k in range(8):
                    mm(ps[pa][:, 0:bn], wa[:, k, 0:128], hT[:, k, b0:b0 + bn], k == 0, k == 7, [wka, "hT"], [bkey("ps", pa)])
                act(acc0[:, 0:bn], ps[pg][:, 0:bn], AF.Sigmoid, [bkey("ps", pg)], ["acc0"])
                vc = 15 + b0 if b0 < T else VC0
                tt("dve", vbp[:, m, vc:vc + bn], ps[pa][:, 0:bn], acc0[:, 0:bn], ALU.mult, [bkey("ps", pa), "acc0"], ["vbp"])
        for k in range(8):
            ld(xsp_o[:, k, :], xT[:, k, :], r=[XK[k]], w=["xsp"])
        ld(hsp_o[:, :, :], hT[:, :, :], r=["hT"], w=["hsp"])
        ld(u_o[:, :, :], uT[:, :, :], r=["uT"], w=["uo"])
        barrier(XK + S5K + ["hT", "yacc"])
        ld(tau[:], tau_d[0:1, :].partition_broadcast(128), w=["tau"])
        s5_tables(l)
        for j in range(16):
            for d in S5_DIRS:
                s5_tile(l, d, j, 1)
        barrier(["acc0", "pay"])
        mset("dve", pay[:], 0.0, ["pay"])
        cp("dve", pay[:, 0:32], hst[:, 2, :], ["hst"], ["pay"])
        cp("dve", pay[:, 32:64], hst[:, 3, :], ["hst"], ["pay"])
        cp("dve", pay[:, 304:336], hst[:, 0, :], ["hst"], ["pay"])
        cp("dve", pay[:, 336:368], hst[:, 1, :], ["hst"], ["pay"])
        pe_ = pay[:, 64:304].rearrange("p (q s f) -> p q s f", q=8, s=2)
        cp("dve", pe_[:, :, 0, :], vbp[:, :, 15:30], ["vbp"], ["pay"])
        cp("dve", pe_[:, :, 1, :], vbp[:, :, 15 + T - 15:15 + T], ["vbp"], ["pay"])
        ld(pay_o[:, :], pay[:], r=["pay"], w=["payo"])
        ld(vbp_o[:, :, :], vbp[:, :, :], r=["vbp"], w=["vo"])
        P.op("sync", None, r=["xsp", "hsp", "uo", "payo", "vo"], w=["doneA"])

    def layer_B(l, x_src):
        mod_vectors(l)
        ld(uT[:, :, :], uin_d[:, :, :], w=["uT"])
        ld(vbp[:, :, :], vin_d[:, :, :], w=["vbp"])
        P.dma("sync", ccs[:, :, :], ccs_d[:, :, :], w=WK + ["ccs"])
        ld(tau[:], tau_d[0:1, :].partition_broadcast(128), w=["tau"])
        HR, HI, T1, T2 = hst[:, 6, 0:16], hst[:, 7, 0:16], hst[:, 8, 0:16], hst[:, 9, 0:16]
        for d in S5_DIRS:
            c0 = (l * 2 + d) * 16
            LR, LI = Pq[:, 4, c0:c0 + 16], Pq[:, 5, c0:c0 + 16]
            SR, SI = hst[:, 4, d * 16:(d + 1) * 16], hst[:, 5, d * 16:(d + 1) * 16]
            cp("dve", HR, ccs[:, 0, 304 + d * 16:304 + d * 16 + 16], ["ccs"], ["hst"])
            cp("dve", HI, ccs[:, 0, 336 + d * 16:336 + d * 16 + 16], ["ccs"], ["hst"])
            order = list(range(NCORES)) if d == 0 else list(range(NCORES - 1, -1, -1))
            for n_, ci in enumerate(order):
                sc_ = sel[:, ci:ci + 1]
                if n_ == 0:
                    ts("dve", SR, HR, sc_, None, ALU.mult, None, ["hst", "par"], ["hst"])
                    ts("dve", SI, HI, sc_, None, ALU.mult, None, ["hst", "par"], ["hst"])
                else:
                    stt(SR, HR, sc_, SR, ALU.mult, ALU.add, ["hst", "par"], ["hst"])
                    stt(SI, HI, sc_, SI, ALU.mult, ALU.add, ["hst", "par"], ["hst"])
                if n_ == NCORES - 1:
                    break
                lr_, li_ = ccs[:, ci, d * 16:d * 16 + 16], ccs[:, ci, 32 + d * 16:32 + d * 16 + 16]
                tt("dve", T1, LR, HR, ALU.mult, ["hst", "Pq"], ["hst"])
                tt("dve", T2, LI, HI, ALU.mult, ["hst", "Pq"], ["hst"])
                tt("dve", T1, T1, T2, ALU.subtract, ["hst"], ["hst"])
                tt("dve", T2, LI, HR, ALU.mult, ["hst", "Pq"], ["hst"])
                tt("dve", HR, T1, lr_, ALU.add, ["hst", "ccs"], ["hst"])
                tt("dve", T1, LR, HI, ALU.mult, ["hst", "Pq"], ["hst"])
                tt("dve", T1, T1, T2, ALU.add, ["hst"], ["hst"])
                tt("dve", HI, T1, li_, ALU.add, ["hst", "ccs"], ["hst"])
        barrier(["acc0", "hal"])
        for side, so in ((0, 16), (1, 8)):
            hs = hal[:, side, :, :]
            for ci in range(NCORES):
                e_ = ccs[:, ci, 64:304].rearrange("p (q s f) -> p q s f", q=8, s=2)[:, :, 1 - side, :]
                sc_ = sel[:, 8 + side * 8 + ci: 8 + side * 8 + ci + 1]
                if ci == 0:
                    ts("dve", hs, e_, sc_, None, ALU.mult, None, ["ccs", "par"], ["hal"])
                else:
                    stt(hs, e_, sc_, hs, ALU.mult, ALU.add, ["ccs", "par", "hal"], ["hal"])
        cp("dve", vbp[:, :, 0:15], hal[:, 0, :, :], ["hal"], ["vbp"])
        cp("dve", vbp[:, :, 15 + T:15 + T + 15], hal[:, 1, :, :], ["hal"], ["vbp"])
        barrier(S5K + ["yacc", "hT", "hal", "acc0", "pay"] + XK)
        s5_tables(l)
        for j in range(16):
            for d in S5_DIRS:
                s5_tile(l, d, j, 2)
        barrier(S5K + ["ya", "ya2", "cvo", "acc0", "gt0", "tabB", "tabC", "rstd"] + XK)
        gtmp = sb("gtmp", [TT], F32, O_A + 36864)
        for q in range(4):
            stt(gt0[:], uT[:, q, :], ssmd[:, l, q:q + 1], yacc[:, q, :], ALU.mult, ALU.add, ["uT", "par", "yacc"], ["gt0"])
            act(gtmp[:], gt0[:], AF.Square, ["gt0"], ["cvo"])
            ts("pool", gtmp[:], gtmp[:], 0.044715, 1.0, ALU.mult, ALU.add, ["cvo"], ["cvo"])
            tt("pool", gtmp[:], gtmp[:], gt0[:], ALU.mult, ["cvo", "gt0"], ["cvo"])
            act(gtmp[:], gtmp[:], AF.Sigmoid, ["cvo"], ["cvo"], scale=1.5957691216057308)
            tt("dve", ya[:, q, :], gt0[:], gtmp[:], ALU.mult, ["gt0", "cvo"], ["ya"])

        def ev_glu(m, bi, b0, bn, p_, pk):
            act(gt0[:, 0:bn], p_, AF.Sigmoid, [pk], ["gt0"])
            tt("dve", ya2[:, m, b0:b0 + bn], ya[:, m, b0:b0 + bn], gt0[:, 0:bn], ALU.mult, ["ya", "gt0"], ["ya2"])
        stream_proj(w_glu_d[l], 4, 512, ya, ["ya"], ev_glu)
        barrier(["uT", "dg", "gt1", "sqt"])
        for q in range(8):
            for jt in range(31):
                ts("pool", dg[:, jt, :], ident[:, :], dwk[:, l, q, jt:jt + 1], None, ALU.mult, None, ["par"], ["dg"])
            for (o0, v0, n) in ((0, 0, 512), (512, 512, 512), (1024, 1024, 512), (1536, 1536, 512), (T, VC0 - 15, TC)):
                pi = pscnt[0] % 4
                pscnt[0] += 1
                for jt in range(31):
                    mm(ps[pi][:, 0:n], dg[:, jt, :], vbp[:, q, v0 + jt:v0 + jt + n], jt == 0, jt == 30, ["dg", "vbp"],
                       [bkey("ps", pi)])
                act(cvo[:, q, o0:o0 + n], ps[pi][:, 0:n], AF.Identity, [bkey("ps", pi), "par"], ["cvo"],
                    bias=dwb[:, l, q:q + 1])
        for bi, (b0, bn) in enumerate(BLOCKS):
            for q in range(8):
                act(sqt[:, 0:bn], cvo[:, q, b0:b0 + bn], AF.Square, ["cvo"], ["sqt"])
                mm(ps[4][:, 0:bn], ones_bf[:, :], cvo[:, q, b0:b0 + bn], q == 0, q == 7, ["cvo", "par"], [bkey("ps", 4)])
                mm(ps[5][:, 0:bn], ones_bf[:, :], sqt[:, 0:bn], q == 0, q == 7, ["sqt", "par"], [bkey("ps", 5)])
            mean, rs_ = gt0[:, 0:bn], gt0[:, 512:512 + bn]
            ts("dve", mean, ps[4][:, 0:bn], 1.0 / D, None, ALU.mult, None, [bkey("ps", 4)], ["gt0"])
            tt("dve", rs_, mean, mean, ALU.mult, ["gt0"], ["gt0"])
            stt(rs_, ps[5][:, 0:bn], 1.0 / D, rs_, ALU.mult, ALU.subtract, [bkey("ps", 5), "gt0"], ["gt0"])
            act(rs_, rs_, AF.Sqrt, ["gt0", "par"], ["gt0"], bias=e6[:, 0:1])
            recip(rs_, rs_, ["gt0"], ["gt0"])
            for q in range(8):
                t_ = gt0[:, 1024:1024 + bn]
                tt("dve", t_, cvo[:, q, b0:b0 + bn], mean, ALU.subtract, ["cvo", "gt0"], ["gt0"])
                tt("dve", t_, t_, rs_, ALU.mult, ["gt0"], ["gt0"])
                act(cvo[:, q, b0:b0 + bn], t_, AF.Silu, ["gt0", "par"], ["cvo"], bias=cnb[:, l, q:q + 1],
                    scale=cng[:, l, q:q + 1])
        barrier(["yacc", "hT", "mrg", "vbp"])
        ld(hT[:, :, :], hin_d[:, :, :], w=["hT"])
        for m in range(8):
            wga, kga = wload(w_in_d[l], 8, 2560 + m * 128, 128)
            wgb, kgb = wload(w_in_d[l], 8, 3584 + m * 128, 128)
            for bi, (b0, bn) in enumerate(BLOCKS):
                for k in range(8):
                    mm(ps[0][:, 0:bn], wga[:, k, 0:128], hT[:, k, b0:b0 + bn], k == 0, k == 7, [kga, "hT"], [bkey("ps", 0)])
                for k in range(8):
                    mm(ps[1][:, 0:bn], wgb[:, k, 0:128], hT[:, k, b0:b0 + bn], k == 0, k == 7, [kgb, "hT"], [bkey("ps", 1)])
                act(gt0[:, b0:b0 + bn], ps[0][:, 0:bn], AF.Sigmoid, [bkey("ps", 0)], ["gt0"])
                act(gt1[:, b0:b0 + bn], ps[1][:, 0:bn], AF.Sigmoid, [bkey("ps", 1)], ["gt1"])
            wpa, kpa = wload(w_pa_d[l], 4, m * 128, 128)
            wpb, kpb = wload(w_pb_d[l], 8, m * 128, 128)
            for bi, (b0, bn) in enumerate(BLOCKS):
                for k in range(4):
                    mm(ps[2][:, 0:bn], wpa[:, k, 0:128], ya2[:, k, b0:b0 + bn], k == 0, k == 3, [kpa, "ya2"], [bkey("ps", 2)])
                for k in range(8):
                    mm(ps[3][:, 0:bn], wpb[:, k, 0:128], cvo[:, k, b0:b0 + bn], k == 0, k == 7, [kpb, "cvo"], [bkey("ps", 3)])
                tt("dve", gt0[:, b0:b0 + bn], gt0[:, b0:b0 + bn], ps[2][:, 0:bn], ALU.mult, ["gt0", bkey("ps", 2)], ["gt0"])
                tt("dve", gt1[:, b0:b0 + bn], gt1[:, b0:b0 + bn], ps[3][:, 0:bn], ALU.mult, ["gt1", bkey("ps", 3)], ["gt1"])
                tt("pool", mrg[:, m, b0:b0 + bn], gt0[:, b0:b0 + bn], gt1[:, b0:b0 + bn], ALU.add, ["gt0", "gt1"], ["mrg"])
        barrier(["ya", "ya2", "cvo"] + XK)
        for k in range(8):
            ld(xT[:, k, :], x_src[:, k, :], w=[XK[k]])

        def ev_out(m, bi, b0, bn, p_, pk):
            mc = 1 if b0 >= T else 0
            stt(xT[:, m, b0:b0 + bn], p_, modv[:, 16 + m, mc:mc + 1], xT[:, m, b0:b0 + bn], ALU.mult, ALU.add,
                [pk, "modv", XK[m]], [XK[m]])
        stream_proj(w_out_d[l], 8, D, mrg, ["mrg"], ev_out)
        barrier(["acc0", "gt0", "rstd", "tabB", "tabC", "hT", "uT", "gt1", "dg", "sqt", "mrg", "vbp"] + ACTK + WDK)
        rms_stats()
        modulate(gs2, 24)
        hb = [(0, 512), (512, 256)]
        for part in range(3):
            h0 = part * HALF
            hTh = hT[:, :, h0:h0 + HALF]

            def ev_gate(m, bi, b0, bn, p_, pk):
                act(actT[:, m, b0:b0 + bn], p_, AF.Silu, [pk], [bkey("actT", m)])

            def ev_up(m, bi, b0, bn, p_, pk):
                tt("dve", actT[:, m, b0:b0 + bn], actT[:, m, b0:b0 + bn], p_, ALU.mult, [pk, bkey("actT", m)],
                   [bkey("actT", m)])
            stream_proj(w_fg_d[l], 8, FFN, hTh, ["hT"], ev_gate, blocks=hb)
            stream_proj(w_fu_d[l], 8, FFN, hTh, ["hT"], ev_up, blocks=hb)
            for m in range(8):
                wb, wk = wload(w_fd_d[l], 22, m * 128, 128, wbufs=Wd, wkey="Wd")
                for bi, (b0, bn) in enumerate(hb):
                    pi = pscnt[0] % 4
                    pscnt[0] += 1
                    for k in range(22):
                        mm(ps[pi][:, 0:bn], wb[:, k, :], actT[:, k, b0:b0 + bn], k == 0, k == 21,
                           [wk, bkey("actT", k)], [bkey("ps", pi)])
                    t0 = h0 + b0
                    segs = [(t0, T, 0), (T, t0 + bn, 1)] if t0 < T < t0 + bn else [(t0, t0 + bn, 0 if t0 + bn <= T else 1)]
                    for (s0, s1, mc) in segs:
                        stt(xT[:, m, s0:s1], ps[pi][:, s0 - t0:s1 - t0], modv[:, 40 + m, mc:mc + 1], xT[:, m, s0:s1],
                            ALU.mult, ALU.add, [bkey("ps", pi), "modv", XK[m]], [XK[m]])
        barrier(["hT", "uT", "vbp", "mrg", "acc0", "gt0", "gt1", "dg", "sqt"] + ACTK + WDK)

    if stage == "A0":
        layer_A(0)
    elif stage == "B0A1":
        layer_B(0, xin_d)
        layer_A(1)
    else:
        layer_B(1, xin_d)
        rms_stats()
        for k in range(8):
            tt("dve", acc0[:, 0:T], xT[:, k, 0:T], rstd[:, 0:T], ALU.mult, [XK[k], "rstd"], ["acc0"])
            ts("dve", acc0[:, 0:T], acc0[:, 0:T], nfg[:, k:k + 1], None, ALU.mult, None, ["acc0", "par"], ["acc0"])
            P.dma("sync", out_d[:, k, :], acc0[:, 0:T], r=["acc0"], w=["out"])
        P.op("sync", None, r=["out"], w=["done"])
    P.emit()
    return nc


def _tT(v):
    return np.ascontiguousarray(v.reshape(-1, 128).T)


def prep_inputs(inp):
    f = np.float32
    x = np.asarray(inp["x"], f)[0]
    ctx = np.asarray(inp["ctx"], f)[0]
    common = {}
    cT = np.ascontiguousarray(ctx.T.reshape(8, 128, TC).transpose(1, 0, 2))
    cv = np.stack([np.asarray(inp["c"], f)[0], np.asarray(inp["c_ctx"], f)], axis=-1)
    common["cvec"] = np.ascontiguousarray(cv.reshape(8, 128, 2).transpose(1, 0, 2))
    q = 256
    om = (1.0 / (np.float32(10000.0) ** (np.arange(q, dtype=f) / f(q)))).astype(f)
    omega = np.ascontiguousarray(om.reshape(2, 128).T)
    cidx = (np.arange(T) % 64).astype(f)[None]
    common["tau"] = np.arange(T).astype(f)[None]
    common["ident"] = np.eye(128, dtype=f)
    for k in ("w_mod", "w_in", "w_glu", "w_proj_a", "w_proj_b", "w_out", "w_ffn_gate", "w_ffn_up", "w_ffn_down"):
        common[k] = np.ascontiguousarray(np.asarray(inp[k], f))

    def pl(v):
        v = np.asarray(v, f)
        return np.ascontiguousarray(v.reshape(DEPTH, -1, 128).transpose(2, 0, 1))

    common["bmodT"] = pl(inp["b_mod"])
    common["n1gT"] = pl(inp["norm1_g"])
    common["n2gT"] = pl(inp["norm2_g"])
    common["ssmdT"] = pl(inp["ssm_d"])
    common["dwbT"] = pl(inp["dw_bias"])
    common["cngT"] = pl(inp["conv_norm_g"])
    common["cnbT"] = pl(inp["conv_norm_b"])
    common["nfgT"] = _tT(np.asarray(inp["norm_f_g"], f))
    dk = np.asarray(inp["dw_kernel"], f)
    common["dwkT"] = np.ascontiguousarray(dk.reshape(DEPTH, 31, 8, 128).transpose(3, 0, 2, 1))
    lre = np.asarray(inp["ssm_lam_re"], f); lim = np.asarray(inp["ssm_lam_im"], f)
    ls = np.asarray(inp["ssm_log_step"], f)
    bre = np.asarray(inp["ssm_b_re"], f); bim = np.asarray(inp["ssm_b_im"], f)
    cre = np.asarray(inp["ssm_c_re"], f); cim = np.asarray(inp["ssm_c_im"], f)
    lsx = np.ascontiguousarray(np.broadcast_to(ls[..., None], lre.shape))

    def pview(a):
        return a.reshape(DEPTH, 2, 16, 128).transpose(3, 0, 1, 2).reshape(128, -1)
    common["P_l"] = np.ascontiguousarray(np.stack([pview(lre), pview(lim), pview(lsx)], axis=1))

    def eview(a):
        outa = np.zeros((DEPTH, 2, 128, 4, 128), f)
        for gi in range(8):
            for qq in range(4):
                for gl in range(2):
                    g = 8 * qq + 2 * (gi // 2) + gl
                    outa[:, :, gi * 16:(gi + 1) * 16, qq, gl * 64:(gl + 1) * 64] = a[:, :, g][:, :, None, :]
        return outa.reshape(DEPTH, 2, 128, 512)
    common["E_l"] = np.ascontiguousarray(np.stack([eview(lre), eview(lim), eview(lsx)], axis=2))

    def ebuild(b):
        outa = np.zeros((DEPTH, 2, 128, 4, 128), f)
        for gi in range(8):
            for qq in range(4):
                gl = gi % 2
                outa[:, :, gi * 16:(gi + 1) * 16, qq, gl * 64:(gl + 1) * 64] = b[:, :, 8 * qq + gi].transpose(0, 1, 3, 2)
        return outa.reshape(DEPTH, 2, 128, 512)
    common["E_b"] = np.ascontiguousarray(np.stack([ebuild(bre), ebuild(bim)], axis=2))

    def cbuild(cc):
        outa = np.zeros((DEPTH, 2, 128, 16, 32), f)
        for j in range(16):
            for gl in range(2):
                outa[:, :, gl * 64:(gl + 1) * 64, j, gl * 16:(gl + 1) * 16] = cc[:, :, 2 * j + gl].transpose(0, 1, 3, 2)
        return outa.reshape(DEPTH, 2, 128, 512)
    common["P_c"] = np.ascontiguousarray(np.stack([cbuild(cre), cbuild(cim)], axis=2))
    maps = []
    for ci in range(NCORES):
        m = dict(common)
        s = np.zeros((128, 24), f)
        s[:, ci] = 1.0
        if ci > 0:
            s[:, 8 + ci - 1] = 1.0
        if ci < NCORES - 1:
            s[:, 16 + ci + 1] = 1.0
        m["sel"] = s
        maps.append(m)
    first = []
    for ci in range(NCORES):
        xs = x[ci * T:(ci + 1) * T]
        first.append({
            "xT": np.ascontiguousarray(xs.T.reshape(8, 128, T).transpose(1, 0, 2)),
            "cT": cT, "omega": omega, "cidx": cidx,
            "ridx": ((ci * T + np.arange(T)) // 64).astype(f)[None],
        })
    return maps, first


_NC = {}
_DBG = None


def _prog(stage):
    if stage not in _NC:
        _NC[stage] = build(stage)
    return _NC[stage]


def _handover(res):
    pays = np.stack([np.asarray(res.results[ci]["pay_o"]) for ci in range(NCORES)], axis=1)
    nxt = []
    for ci in range(NCORES):
        r = res.results[ci]
        nxt.append({"xsp_i": r["xsp_o"], "hsp_i": r["hsp_o"], "u_i": r["u_o"], "vbp_i": r["vbp_o"],
                    "ccs_i": np.ascontiguousarray(pays)})
    return nxt


def kernel(**inputs):
    maps, first = prep_inputs(inputs)
    cores = list(range(NCORES))
    res = run_bass_kernel_spmd(_prog("A0"), [dict(maps[c], **first[c]) for c in cores], core_ids=cores)
    nxt = _handover(res)
    if _DBG is not None:
        _DBG["h1"] = nxt
    res = run_bass_kernel_spmd(_prog("B0A1"), [dict(maps[c], **nxt[c]) for c in cores], core_ids=cores)
    nxt = _handover(res)
    if _DBG is not None:
        _DBG["h2"] = nxt
    res = run_bass_kernel_spmd(_prog("B1"), [dict(maps[c], **nxt[c]) for c in cores], core_ids=cores)
    outs = []
    for ci in cores:
        o = np.asarray(res.results[ci]["outT"])
        outs.append(o.transpose(2, 1, 0).reshape(T, D))
    return np.concatenate(outs, axis=0)[None].astype(np.float32)
```

```python
import math
import numpy as np
import concourse.bass as bass
import concourse.mybir as mybir
from concourse.bass_utils import run_bass_kernel_spmd

F32 = mybir.dt.float32
BF16 = mybir.dt.bfloat16
ALU = mybir.AluOpType
AF = mybir.ActivationFunctionType

NCORES = 8
D = 1024
T = 2048
TC = 256
TT = T + TC
DEPTH = 2
FFN = 2816
WIN = 4608
PI = math.pi
MAGIC = 12582912.0
BLOCKS = [(0, 512), (512, 512), (1024, 512), (1536, 512), (2048, 256)]


class Prog:
    ENG = ("sync", "act", "dve", "pool", "pe")

    def __init__(self, nc):
        self.nc = nc
        self.ops = []
        self.lw = {}
        self.rd = {}

    def _rec(self, eng, fn, r, w, dma):
        i = len(self.ops)
        deps = {}
        for b in r:
            j = self.lw.get(b)
            if j is not None:
                deps[j] = deps.get(j, False) or True
        for b in w:
            j = self.lw.get(b)
            if j is not None:
                deps[j] = True
            for j in self.rd.get(b, ()):
                if j not in deps:
                    deps[j] = False
        for b in r:
            self.rd.setdefault(b, []).append(i)
        for b in w:
            self.lw[b] = i
            self.rd[b] = []
        keep = []
        for j, strong in deps.items():
            oj = self.ops[j]
            if oj["eng"] == eng and not oj["dma"] and not dma:
                if eng == "pe" or not strong:
                    continue
            keep.append(j)
        self.ops.append(dict(eng=eng, fn=fn, deps=keep, dma=dma, sig=None))
        return i

    def op(self, eng, fn, r=(), w=()):
        return self._rec(eng, fn, tuple(r), tuple(w), False)

    def dma(self, eng, out, in_, r=(), w=()):
        return self._rec(eng, lambda e: e.dma_start(out=out, in_=in_), tuple(r), tuple(w), True)

    def emit(self, ndma_sems=64):
        nc = self.nc
        waited_on = set()
        for o in self.ops:
            for j in o["deps"]:
                waited_on.add(j)
        esem = {e: nc.alloc_semaphore("s_" + e) for e in self.ENG}
        dsem = [nc.alloc_semaphore("d%d" % k) for k in range(ndma_sems)]
        ecnt = {e: 0 for e in self.ENG}
        dcnt = [0] * ndma_sems
        nd = 0
        for i, o in enumerate(self.ops):
            if (i in waited_on or o["dma"]) and o["fn"] is not None:
                if o["dma"]:
                    k = nd % ndma_sems
                    nd += 1
                    dcnt[k] += 16
                    o["sig"] = (("d", k), dsem[k], dcnt[k], 16)
                else:
                    ecnt[o["eng"]] += 1
                    o["sig"] = (("e", o["eng"]), esem[o["eng"]], ecnt[o["eng"]], 1)
        per = {e: [] for e in self.ENG}
        for o in self.ops:
            per[o["eng"]].append(o)
        ops = self.ops

        def run(eng_name, e):
            waited = {}
            for o in per[eng_name]:
                for j in sorted(o["deps"]):
                    s = ops[j]["sig"]
                    if s is None:
                        continue
                    ch, sem, val, _ = s
                    if waited.get(ch, 0) >= val:
                        continue
                    waited[ch] = val
                    e.wait_ge(sem, val)
                if o["fn"] is None:
                    continue
                if o["dma"] and o["sig"][2] > 16:
                    ch = o["sig"][0]
                    if waited.get(ch, 0) < o["sig"][2] - 16:
                        waited[ch] = o["sig"][2] - 16
                        e.wait_ge(o["sig"][1], o["sig"][2] - 16)
                ins = o["fn"](e)
                if o["sig"] is not None:
                    ins.then_inc(o["sig"][1], o["sig"][3])

        with nc.Block() as block:
            @block.sync
            def _(e):
                run("sync", e)

            @block.scalar
            def _(e):
                run("act", e)

            @block.vector
            def _(e):
                run("dve", e)

            @block.gpsimd
            def _(e):
                run("pool", e)

            @block.tensor
            def _(e):
                run("pe", e)


NPAY = 64 + 240 + 64
VP = 15 + T + 15 + 15 + TC + 15
VC0 = 15 + T + 15 + 15
HALF = TT // 3
S5_DIRS = (0, 1)


def build(stage):
    nc = bass.Bass("TRN2", target_bir_lowering=False)
    P = Prog(nc)

    def din(name, shape, dt=F32):
        return nc.dram_tensor(name, list(shape), dt, kind="ExternalInput").ap()

    def dout(name, shape, dt=F32):
        return nc.dram_tensor(name, list(shape), dt, kind="ExternalOutput").ap()

    first = stage == "A0"
    last = stage == "B1"
    if first:
        xT_d = din("xT", [128, 8, T])
        cT_d = din("cT", [128, 8, TC])
        omega_d = din("omega", [128, 2])
        ridx_d = din("ridx", [1, T])
        cidx_d = din("cidx", [1, T])
    else:
        xin_d = din("xsp_i", [128, 8, TT])
        hin_d = din("hsp_i", [128, 8, TT], BF16)
        uin_d = din("u_i", [128, 4, TT], BF16)
        vin_d = din("vbp_i", [128, 8, VP], BF16)
        ccs_d = din("ccs_i", [128, NCORES, NPAY])
    cvec_d = din("cvec", [128, 8, 2])
    tau_d = din("tau", [1, T])
    sel_d = din("sel", [128, 24])
    w_mod_d = din("w_mod", [DEPTH, D, 6 * D])
    bmod_d = din("bmodT", [128, DEPTH, 48])
    n1g_d = din("n1gT", [128, DEPTH, 8])
    n2g_d = din("n2gT", [128, DEPTH, 8])
    w_in_d = din("w_in", [DEPTH, D, WIN])
    P_l_d = din("P_l", [128, 3, 64])
    P_c_d = din("P_c", [DEPTH, 2, 2, 128, 512])
    E_l_d = din("E_l", [DEPTH, 2, 3, 128, 512])
    E_b_d = din("E_b", [DEPTH, 2, 2, 128, 512])
    ssmd_d = din("ssmdT", [128, DEPTH, 4])
    w_glu_d = din("w_glu", [DEPTH, 512, 512])
    w_pa_d = din("w_proj_a", [DEPTH, 512, D])
    dwk_d = din("dwkT", [128, DEPTH, 8, 31])
    dwb_d = din("dwbT", [128, DEPTH, 8])
    cng_d = din("cngT", [128, DEPTH, 8])
    cnb_d = din("cnbT", [128, DEPTH, 8])
    w_pb_d = din("w_proj_b", [DEPTH, D, D])
    w_out_d = din("w_out", [DEPTH, D, D])
    w_fg_d = din("w_ffn_gate", [DEPTH, D, FFN])
    w_fu_d = din("w_ffn_up", [DEPTH, D, FFN])
    w_fd_d = din("w_ffn_down", [DEPTH, FFN, D])
    nfg_d = din("nfgT", [128, 8])
    ident_d = din("ident", [128, 128])
    if last:
        out_d = dout("outT", [128, 8, T])
    else:
        xsp_o = dout("xsp_o", [128, 8, TT])
        hsp_o = dout("hsp_o", [128, 8, TT], BF16)
        u_o = dout("u_o", [128, 4, TT], BF16)
        vbp_o = dout("vbp_o", [128, 8, VP], BF16)
        pay_o = dout("pay_o", [128, NPAY])

    _n = [0]

    def sb(name, shape, dt, off):
        _n[0] += 1
        return nc.alloc_sbuf_tensor_at("%s_%d" % (name, _n[0]), [128] + list(shape), dt, offset=off)

    O_PAR, O_W, O_RSTD, O_A, O_B, O_C, O_D, O_E = 16512, 28800, 41088, 50304, 124032, 160896, 179328, 217152
    END = 229376
    po = [O_PAR]

    def par(name, shape, dt=F32):
        sz = int(np.prod(shape)) * (4 if dt == F32 else 2)
        sz = (sz + 31) // 32 * 32
        t = sb(name, shape, dt, po[0])
        po[0] += sz
        assert po[0] <= O_W, po[0]
        return t

    modv = par("modv", [48, 2])
    gs1 = par("gs1", [8, 2])
    gs2 = par("gs2", [8, 2])
    bmod = par("bmod", [DEPTH, 48])
    n1g = par("n1g", [DEPTH, 8])
    n2g = par("n2g", [DEPTH, 8])
    nfg = par("nfg", [8])
    ssmd = par("ssmd", [DEPTH, 4])
    dwk = par("dwk", [DEPTH, 8, 31])
    dwb = par("dwb", [DEPTH, 8])
    cng = par("cng", [DEPTH, 8])
    cnb = par("cnb", [DEPTH, 8])
    omega = par("omega", [2])
    sel = par("sel", [24])
    csil = par("csil", [8, 2], BF16)
    cvec = par("cvec", [8, 2])
    ones_bf = par("ones_bf", [128], BF16)
    Pl = par("Pl", [3, 64])
    Pq = par("Pq", [14, 64])
    hst = par("hst", [10, 32])
    ini = par("ini", [4])
    e6 = par("e6", [8])
    halfpi = par("halfpi", [1])
    scr = par("scr", [1])
    ident = par("ident", [128], BF16)

    Wb = [sb("Wb%d" % i, [8, 384], BF16, O_W + i * 6144) for i in range(2)]
    rstd = sb("rstd", [TT], F32, O_RSTD)
    xT = sb("xT", [8, TT], F32, O_A)
    hT = sb("hT", [8, TT], BF16, O_B)
    uT = sb("uT", [4, TT], BF16, O_C)
    vbp = sb("vbp", [8, VP], BF16, O_D)
    acc0 = sb("acc0", [TT], F32, O_E)
    acc1 = sb("acc1", [TT], F32, O_B)
    o = O_A
    tau = sb("tau", [T], F32, o); o += 8192
    SN = sb("SN", [T], F32, o); o += 8192
    CS = sb("CS", [T], F32, o); o += 8192
    W1 = sb("W1", [TT], F32, o); o += 9216
    W2 = sb("W2", [TT], F32, o); o += 9216
    G1 = sb("G1", [TT], F32, o); o += 9216
    G2 = sb("G2", [TT], F32, o); o += 9216
    HHf = sb("HHf", [TT], F32, o)
    hre = sb("hre", [TT], BF16, o); o += 4608
    him = sb("him", [TT], BF16, o); o += 4608
    assert o <= O_B
    ET = [sb("ET%d" % i, [512], F32, O_A + 24576 + i * 2048) for i in range(16)]
    yacc = sb("yacc", [4, TT], F32, O_B)
    tabB = sb("tabB", [2, 2, 512], BF16, O_RSTD)
    tabC = sb("tabC", [2, 2, 512], BF16, O_RSTD + 4096)
    rmul = sb("rmul", [T], F32, O_E)
    ccs = sb("ccs", [NCORES, NPAY], F32, O_W)
    pay = sb("pay", [NPAY], F32, O_E)
    hal = sb("hal", [2, 8, 15], F32, O_E + 2048)
    ya = sb("ya", [4, TT], BF16, O_A)
    ya2 = sb("ya2", [4, TT], BF16, O_A + 18432)
    cvo = sb("cvo", [8, TT], BF16, O_A + 36864)
    gt0 = sb("gt0", [TT], F32, O_E)
    gt1 = sb("gt1", [TT], F32, O_C)
    dg = sb("dg", [31, 128], BF16, O_C + 9216)
    sqt = sb("sqt", [512], BF16, O_C + 9216 + 7936)
    mrg = sb("mrg", [8, TT], BF16, O_D)
    actT = sb("actT", [22, HALF], BF16, O_C)
    Wd = [sb("Wd%d" % i, [22, 128], BF16, O_C + 22 * HALF * 2 + i * 5632) for i in range(2)]
    assert O_C + 22 * HALF * 2 + 2 * 5632 <= O_E

    ps = [nc.alloc_psum_tensor("ps%d" % i, [128, 512], F32) for i in range(8)]

    def bkey(name, *idx):
        return (name,) + idx

    XK = [bkey("xT", k) for k in range(8)]
    ACTK = [bkey("actT", m) for m in range(22)]
    WDK = [bkey("Wd", 0), bkey("Wd", 1)]
    S5K = ["tau", "SN", "CS", "W1", "W2", "G1", "G2", "HH"]

    def tt(eng, out, a, b, op, r, w):
        P.op(eng, lambda e: e.tensor_tensor(out=out, in0=a, in1=b, op=op), r, w)

    def ts(eng, out, a, s1, s2, op0, op1, r, w):
        if op1 is None:
            P.op(eng, lambda e: e.tensor_scalar(out=out, in0=a, scalar1=s1, scalar2=None, op0=op0), r, w)
        else:
            P.op(eng, lambda e: e.tensor_scalar(out=out, in0=a, scalar1=s1, scalar2=s2, op0=op0, op1=op1), r, w)

    def stt(out, a, s, b, op0, op1, r, w):
        P.op("dve", lambda e: e.scalar_tensor_tensor(out=out, in0=a, scalar=s, in1=b, op0=op0, op1=op1), r, w)

    def act(out, a, func, r, w, bias=None, scale=None):
        kw = {}
        if bias is not None:
            kw["bias"] = bias
        if scale is not None:
            kw["scale"] = scale
        P.op("act", lambda e: e.activation(out=out, in_=a, func=func, **kw), r, w)

    def recip(out, a, r, w):
        P.op("dve", lambda e: e.reciprocal(out=out, in_=a), r, w)

    def cp(eng, out, a, r, w):
        P.op(eng, lambda e: e.tensor_copy(out=out, in_=a), r, w)

    def mset(eng, out, val, w):
        P.op(eng, lambda e: e.memset(out, val), (), w)

    def mm(out, lhsT, rhs, start, stop, r, w, tp=None):
        if tp is None:
            P.op("pe", lambda e: e.matmul(out, lhsT=lhsT, rhs=rhs, start=start, stop=stop), r, w)
        else:
            P.op("pe", lambda e: e.matmul(out, lhsT=lhsT, rhs=rhs, start=start, stop=stop, tile_position=tp), r, w)

    def scan(out, d0, d1, init, r, w):
        P.op("dve", lambda e: e.tensor_tensor_scan(out=out, data0=d0, data1=d1, initial=init, op0=ALU.mult,
                                                   op1=ALU.add), r, w)

    def barrier(keys):
        P.op("dve", lambda e: e.memset(scr[:], 0.0), (), list(keys))

    def ld(out, in_, r=(), w=(), eng="sync"):
        P.dma(eng, out, in_, r, w)

    WK = [bkey("Wb", 0), bkey("Wb", 1)]

    plist = [(bmod, bmod_d), (n1g, n1g_d), (n2g, n2g_d), (nfg, nfg_d), (ssmd, ssmd_d), (dwk, dwk_d),
             (dwb, dwb_d), (cng, cng_d), (cnb, cnb_d), (sel, sel_d), (cvec, cvec_d)]
    if first:
        plist.append((omega, omega_d))
    for t_, d_ in plist:
        ld(t_[:], d_, w=["par"])
    ld(Pl[:], P_l_d, w=["Pl"])
    P.dma("pool", ident[:], ident_d, w=["par"])
    mset("dve", ones_bf[:], 1.0, ["par"])
    mset("dve", halfpi[:], PI / 2, ["par"])
    mset("dve", e6[:], 1e-6, ["par"])
    act(csil[:], cvec[:], AF.Silu, ["par"], ["csil"])

    def sincos(eng, s_out, c_out, u, tmp, r, w):
        ts(eng, tmp, u, MAGIC, None, ALU.add, None, r, w)
        ts(eng, tmp, tmp, -MAGIC, None, ALU.add, None, w, w)
        tt(eng, tmp, u, tmp, ALU.subtract, list(r) + list(w), w)
        act(c_out, tmp, AF.Abs, w, w)
        act(c_out, c_out, AF.Sin, list(w) + ["par"], w, bias=halfpi[:, 0:1], scale=-2 * PI)
        act(s_out, tmp, AF.Sin, w, w, scale=2 * PI)

    def disc(lre, lim, ls, r_o, cth, sth, dt_o, th_o, tmp, rk, wk):
        act(dt_o, ls, AF.Exp, rk, wk)
        tt("dve", r_o, lre, dt_o, ALU.mult, list(rk) + list(wk), wk)
        tt("dve", th_o, lim, dt_o, ALU.mult, list(rk) + list(wk), wk)
        ts("dve", th_o, th_o, 1.0 / (2 * PI), None, ALU.mult, None, wk, wk)
        sincos("dve", sth, cth, th_o, tmp, wk, wk)

    PQ = lambda i: Pq[:, i, :]
    disc(Pl[:, 0, :], Pl[:, 1, :], Pl[:, 2, :], PQ(7), PQ(2), PQ(3), PQ(6), PQ(1), PQ(8), ["Pl", "par"], ["Pq"])
    act(PQ(0), PQ(7), AF.Exp, ["Pq"], ["Pq"])
    act(PQ(9), PQ(7), AF.Exp, ["Pq"], ["Pq"], scale=float(T))
    ts("dve", PQ(10), PQ(1), float(T), None, ALU.mult, None, ["Pq"], ["Pq"])
    sincos("dve", PQ(11), PQ(12), PQ(10), PQ(8), ["Pq"], ["Pq"])
    tt("dve", PQ(4), PQ(9), PQ(12), ALU.mult, ["Pq"], ["Pq"])
    tt("dve", PQ(5), PQ(9), PQ(11), ALU.mult, ["Pq"], ["Pq"])

    if first:
        for k in range(8):
            ld(xT[:, k, 0:T], xT_d[:, k, :], w=[XK[k]])
            ld(xT[:, k, T:TT], cT_d[:, k, :], w=[XK[k]])
        for k in range(8):
            idx_d = ridx_d if k < 4 else cidx_d
            if k in (0, 4):
                ld(acc0[:, 0:T], idx_d[0:1, :].partition_broadcast(128), w=["acc0"])
            ph = (PI / 2 if (k // 2) % 2 == 1 else 0.0)
            ts("dve", acc1[:, 0:T], acc0[:, 0:T], omega[:, k % 2:k % 2 + 1], ph, ALU.mult, ALU.add,
               ["acc0", "par"], ["hT"])
            ts("dve", rstd[:, 0:T], acc1[:, 0:T], 1.0 / (2 * PI), MAGIC, ALU.mult, ALU.add, ["hT"], ["rstd"])
            ts("dve", rstd[:, 0:T], rstd[:, 0:T], -MAGIC, None, ALU.add, None, ["rstd"], ["rstd"])
            stt(acc1[:, 0:T], rstd[:, 0:T], -2 * PI, acc1[:, 0:T], ALU.mult, ALU.add, ["rstd", "hT"], ["hT"])
            act(acc1[:, 0:T], acc1[:, 0:T], AF.Sin, ["hT", "par"], ["hT"])
            tt("dve", xT[:, k, 0:T], xT[:, k, 0:T], acc1[:, 0:T], ALU.add, ["hT", XK[k]], [XK[k]])

    wslot = [0]

    def wload(Wd_ap, KT, c0, cw, wbufs=None, wkey="Wb"):
        wbufs = wbufs or Wb
        s = wslot[0] % 2
        wslot[0] += 1
        wb = wbufs[s]
        src = Wd_ap[:, c0:c0 + cw].rearrange("(k p) f -> p k f", p=128)
        P.dma("pool", wb[:, 0:KT, 0:cw], src, w=[bkey(wkey, s)])
        return wb, bkey(wkey, s)

    pscnt = [0]

    def stream_proj(Wd_ap, KT, ncols, inT, inkeys, evac, col0=0, blocks=BLOCKS, chunk=384):
        nchunk = (ncols + chunk - 1) // chunk
        for c in range(nchunk):
            cw = min(chunk, ncols - c * chunk)
            wb, wk = wload(Wd_ap, KT, col0 + c * chunk, cw)
            for mi in range(cw // 128):
                m = c * (chunk // 128) + mi
                for bi, (b0, bn) in enumerate(blocks):
                    pi = pscnt[0] % 4
                    pscnt[0] += 1
                    for k in range(KT):
                        mm(ps[pi][:, 0:bn], wb[:, k, mi * 128:(mi + 1) * 128], inT[:, k, b0:b0 + bn],
                           k == 0, k == KT - 1, [wk] + list(inkeys), [bkey("ps", pi)])
                    evac(m, bi, b0, bn, ps[pi][:, 0:bn], bkey("ps", pi))

    def rms_stats():
        for bi, (b0, bn) in enumerate(BLOCKS):
            for k in range(8):
                act(hT[:, k, b0:b0 + bn], xT[:, k, b0:b0 + bn], AF.Square, [XK[k]], ["hT"])
            for k in range(8):
                mm(ps[4][:, 0:bn], ones_bf[:, :], hT[:, k, b0:b0 + bn], k == 0, k == 7, ["hT", "par"], [bkey("ps", 4)])
            act(rstd[:, b0:b0 + bn], ps[4][:, 0:bn], AF.Sqrt, [bkey("ps", 4), "par"], ["rstd"], bias=e6[:, 0:1],
                scale=1.0 / D)
            recip(rstd[:, b0:b0 + bn], rstd[:, b0:b0 + bn], ["rstd"], ["rstd"])

    def modulate(gsv, shoff):
        for k in range(8):
            for (c0, c1, mc) in ((0, T, 0), (T, TT, 1)):
                tt("dve", acc0[:, c0:c1], xT[:, k, c0:c1], rstd[:, c0:c1], ALU.mult, [XK[k], "rstd"], ["acc0"])
                act(hT[:, k, c0:c1], acc0[:, c0:c1], AF.Identity, ["acc0", "modv"], ["hT"],
                    bias=modv[:, shoff + k, mc:mc + 1], scale=gsv[:, k, mc:mc + 1])

    def mod_vectors(l):
        for c in range(16):
            wb, wk = wload(w_mod_d[l], 8, c * 384, 384)
            for mi in range(3):
                m = c * 3 + mi
                for k in range(8):
                    mm(ps[5][:, 2 * m:2 * m + 2], wb[:, k, mi * 128:(mi + 1) * 128], csil[:, k, :], k == 0, k == 7,
                       [wk, "csil"], [bkey("ps", 5)])
        for c2 in range(2):
            tt("dve", modv[:, :, c2], ps[5][:, c2:96:2], bmod[:, l, :], ALU.add, [bkey("ps", 5), "par"], ["modv"])
        for (gsv, ng, off) in ((gs1, n1g, 8), (gs2, n2g, 32)):
            for c2 in range(2):
                stt(gsv[:, :, c2], modv[:, off:off + 8, c2], 1.0, ng[:, l, :], ALU.add, ALU.mult,
                    ["modv", "par"], ["modv"])

    def s5_tables(l):
        barrier(["rstd", "tabB", "tabC", "W1", "W2", "G1", "G2", "ET"])
        for d in S5_DIRS:
            lre, lim, ls, bre, bim = ET[0], ET[1], ET[2], ET[3], ET[4]
            for i, tdst in enumerate((lre, lim, ls)):
                ld(tdst[:], E_l_d[l, d, i], w=["ET"])
            for i, tdst in enumerate((bre, bim)):
                ld(tdst[:], E_b_d[l, d, i], w=["ET"])
            r_, cth, sth, dt_, th_, tmp = ET[5], ET[6], ET[7], ET[8], ET[9], ET[10]
            disc(lre[:], lim[:], ls[:], r_[:], cth[:], sth[:], dt_[:], th_[:], tmp[:], ["ET", "par"], ["ET"])
            act(r_[:], r_[:], AF.Exp, ["ET"], ["ET"])
            A_, Bn, den, fre, fim, t1 = ET[11], ET[12], ET[13], ET[14], ET[15], ET[8]
            tt("dve", A_[:], r_[:], cth[:], ALU.mult, ["ET"], ["ET"])
            ts("dve", A_[:], A_[:], -1.0, None, ALU.add, None, ["ET"], ["ET"])
            tt("dve", Bn[:], r_[:], sth[:], ALU.mult, ["ET"], ["ET"])
            tt("dve", den[:], lre[:], lre[:], ALU.mult, ["ET"], ["ET"])
            tt("dve", t1[:], lim[:], lim[:], ALU.mult, ["ET"], ["ET"])
            tt("dve", den[:], den[:], t1[:], ALU.add, ["ET"], ["ET"])
            recip(den[:], den[:], ["ET"], ["ET"])
            tt("dve", fre[:], A_[:], lre[:], ALU.mult, ["ET"], ["ET"])
            tt("dve", t1[:], Bn[:], lim[:], ALU.mult, ["ET"], ["ET"])
            tt("dve", fre[:], fre[:], t1[:], ALU.add, ["ET"], ["ET"])
            tt("dve", fre[:], fre[:], den[:], ALU.mult, ["ET"], ["ET"])
            tt("dve", fim[:], Bn[:], lre[:], ALU.mult, ["ET"], ["ET"])
            tt("dve", t1[:], A_[:], lim[:], ALU.mult, ["ET"], ["ET"])
            tt("dve", fim[:], fim[:], t1[:], ALU.subtract, ["ET"], ["ET"])
            tt("dve", fim[:], fim[:], den[:], ALU.mult, ["ET"], ["ET"])
            t2 = ET[9]
            tt("dve", t1[:], fre[:], bre[:], ALU.mult, ["ET"], ["ET"])
            tt("dve", t2[:], fim[:], bim[:], ALU.mult, ["ET"], ["ET"])
            tt("dve", tabB[:, d, 0, :], t1[:], t2[:], ALU.subtract, ["ET"], ["tabB"])
            tt("dve", t1[:], fre[:], bim[:], ALU.mult, ["ET"], ["ET"])
            tt("dve", t2[:], fim[:], bre[:], ALU.mult, ["ET"], ["ET"])
            tt("dve", tabB[:, d, 1, :], t1[:], t2[:], ALU.add, ["ET"], ["tabB"])
            ld(ET[0][:], P_c_d[l, d, 0], w=["ET"])
            ld(ET[1][:], P_c_d[l, d, 1], w=["ET"])
            act(tabC[:, d, 0, :], ET[0][:], AF.Copy, ["ET"], ["tabC"])
            act(tabC[:, d, 1, :], ET[1][:], AF.Copy, ["ET"], ["tabC"], scale=-1.0)
        barrier(["W1", "W2", "G1", "G2", "ET"])

    def rot(o1, o2, x1, x2, tmp, cs, sn, r, w, tkey):
        tt("pool", o1, x1, cs, ALU.mult, r, w)
        tt("dve", tmp, x2, sn, ALU.mult, r, [tkey])
        tt("pool", o1, o1, tmp, ALU.add, list(w) + [tkey], w)
        tt("pool", o2, x1, sn, ALU.mult, list(r) + list(w) + [tkey], w)
        tt("dve", tmp, x2, cs, ALU.mult, list(r) + list(w), [tkey])
        tt("pool", o2, o2, tmp, ALU.subtract, list(w) + [tkey], w)

    SEGS = ((T, TT, TC), (0, T, T))

    def s5_tile(l, d, j, phase):
        c = (l * 2 + d) * 16 + j
        dj = d * 16 + j
        q, a = j // 4, j % 4
        pcol = lambda i: Pq[:, i, c:c + 1]
        rev = d == 1

        def tab(t_, n):
            return t_[:, n - 1::-1] if rev and n > 1 else (t_[:, 0:n] if not rev else t_[:, 0:1])

        def tabn(t_, n):
            if not rev:
                return t_[:, 0:n]
            return t_[:, n - 1::-1] if n > 1 else t_[:, 0:1]

        ts("pool", SN[:], tau[:], pcol(1), MAGIC, ALU.mult, ALU.add, ["tau", "Pq"], ["SN"])
        ts("pool", SN[:], SN[:], 1.0, -MAGIC, ALU.mult, ALU.add, ["SN"], ["SN"])
        stt(SN[:], tau[:], pcol(1), SN[:], ALU.mult, ALU.subtract, ["tau", "Pq", "SN"], ["SN"])
        act(CS[:], SN[:], AF.Abs, ["SN"], ["CS"])
        act(CS[:], CS[:], AF.Sin, ["CS", "par"], ["CS"], bias=halfpi[:, 0:1], scale=-2 * PI)
        act(SN[:], SN[:], AF.Sin, ["SN"], ["SN"], scale=2 * PI)
        ts("pool", rmul[:], tau[:], 0.0, pcol(0), ALU.mult, ALU.add, ["tau", "Pq"], ["acc0"])
        for bi, (b0, bn) in enumerate(BLOCKS):
            p0 = 2 * (bi % 2)
            for ri in range(2):
                mm(ps[p0 + ri][:, 0:bn], tabB[32 * a:32 * a + 32, d, ri, q * 128:(q + 1) * 128],
                   uT[32 * a:32 * a + 32, q, b0:b0 + bn], True, True, ["tabB", "uT", "rstd"], [bkey("ps", p0 + ri)],
                   tp=(32 * a, 0))
            act(G1[:, b0:b0 + bn], ps[p0][:, 0:bn], AF.Copy, [bkey("ps", p0)], ["G1"])
            act(G2[:, b0:b0 + bn], ps[p0 + 1][:, 0:bn], AF.Copy, [bkey("ps", p0 + 1)], ["G2"])
        for (c0, c1, n) in SEGS:
            rot(W1[:, c0:c1], W2[:, c0:c1], G1[:, c0:c1], G2[:, c0:c1], HHf[:, c0:c1], tabn(CS, n), tabn(SN, n),
                ["G1", "G2", "CS", "SN"], ["W1", "W2"], "HH")
        if phase == 2:
            hr, hi_ = hst[:, 4, dj:dj + 1], hst[:, 5, dj:dj + 1]
            tt("dve", ini[:, 0:1], hr, pcol(2), ALU.mult, ["hst", "Pq"], ["ini"])
            tt("dve", ini[:, 2:3], hi_, pcol(3), ALU.mult, ["hst", "Pq"], ["ini"])
            tt("dve", ini[:, 0:1], ini[:, 0:1], ini[:, 2:3], ALU.subtract, ["ini"], ["ini"])
            tt("dve", ini[:, 1:2], hr, pcol(3), ALU.mult, ["hst", "Pq", "ini"], ["ini"])
            tt("dve", ini[:, 2:3], hi_, pcol(2), ALU.mult, ["hst", "Pq", "ini"], ["ini"])
            tt("dve", ini[:, 1:2], ini[:, 1:2], ini[:, 2:3], ALU.add, ["ini"], ["ini"])
            ts("dve", ini[:, 1:2], ini[:, 1:2], -1.0, None, ALU.mult, None, ["ini"], ["ini"])
        for (c0, c1, n) in SEGS:
            for gi, (Gx, Wx, gk, wk) in enumerate(((G1, W1, "G1", "W1"), (G2, W2, "G2", "W2"))):
                if phase == 2 and c0 == 0:
                    init = ini[:, gi:gi + 1]
                else:
                    init = 0.0
                if not rev:
                    scan(Gx[:, c0:c1], rmul[:, 0:n], Wx[:, c0:c1], init, [wk, "acc0", "ini"], [gk])
                else:
                    lo = c0 - 1 if c0 > 0 else None
                    scan(Gx[:, c1 - 1:lo:-1], rmul[:, 0:n], Wx[:, c1 - 1:lo:-1], init, [wk, "acc0", "ini"], [gk])
        if phase == 1:
            for (c0, c1, n), row in zip(SEGS, (0, 2)):
                ecol = c1 - 1 if not rev else c0
                tcol = n - 1
                g1, g2 = G1[:, ecol:ecol + 1], G2[:, ecol:ecol + 1]
                cs_, sn_ = CS[:, tcol:tcol + 1], SN[:, tcol:tcol + 1]
                hr, hi_ = hst[:, row, dj:dj + 1], hst[:, row + 1, dj:dj + 1]
                tt("dve", hr, g1, cs_, ALU.mult, ["G1", "CS"], ["hst"])
                tt("dve", ini[:, 3:4], g2, sn_, ALU.mult, ["G2", "SN"], ["ini"])
                tt("dve", hr, hr, ini[:, 3:4], ALU.add, ["hst", "ini"], ["hst"])
                tt("dve", hi_, g1, sn_, ALU.mult, ["G1", "SN"], ["hst"])
                tt("dve", ini[:, 3:4], g2, cs_, ALU.mult, ["G2", "CS", "hst"], ["ini"])
                tt("dve", hi_, hi_, ini[:, 3:4], ALU.subtract, ["hst", "ini"], ["hst"])
            return
        for (c0, c1, n) in SEGS:
            cs_, sn_ = tabn(CS, n), tabn(SN, n)
            tt("pool", W1[:, c0:c1], G1[:, c0:c1], cs_, ALU.mult, ["G1", "CS"], ["W1"])
            tt("dve", W2[:, c0:c1], G2[:, c0:c1], sn_, ALU.mult, ["G2", "SN"], ["W2"])
            tt("pool", hre[:, c0:c1], W1[:, c0:c1], W2[:, c0:c1], ALU.add, ["W1", "W2"], ["HH"])
            tt("pool", W1[:, c0:c1], G1[:, c0:c1], sn_, ALU.mult, ["G1", "SN", "HH"], ["W1"])
            tt("dve", W2[:, c0:c1], G2[:, c0:c1], cs_, ALU.mult, ["G2", "CS", "HH"], ["W2"])
            tt("pool", him[:, c0:c1], W1[:, c0:c1], W2[:, c0:c1], ALU.subtract, ["W1", "W2"], ["HH"])
        for bi, (b0, bn) in enumerate(BLOCKS):
            pi = 4 + bi % 2
            mm(ps[pi][32 * a:32 * a + 32, 0:bn], tabC[:, d, 0, j * 32:(j + 1) * 32], hre[:, b0:b0 + bn], True, False,
               ["tabC", "HH", "rstd"], [bkey("ps", pi)], tp=(0, 32 * a))
            mm(ps[pi][32 * a:32 * a + 32, 0:bn], tabC[:, d, 1, j * 32:(j + 1) * 32], him[:, b0:b0 + bn], False, True,
               ["tabC", "HH", "rstd"], [bkey("ps", pi)], tp=(0, 32 * a))
            ydst = yacc[32 * a:32 * a + 32, q, b0:b0 + bn]
            if d == S5_DIRS[0]:
                cp("dve", ydst, ps[pi][32 * a:32 * a + 32, 0:bn], [bkey("ps", pi)], ["yacc"])
            else:
                tt("dve", ydst, ydst, ps[pi][32 * a:32 * a + 32, 0:bn], ALU.add, [bkey("ps", pi), "yacc"], ["yacc"])

    def layer_A(l):
        mod_vectors(l)
        rms_stats()
        modulate(gs1, 0)
        def ev_u(m, bi, b0, bn, p_, pk):
            act(uT[:, m, b0:b0 + bn], p_, AF.Copy, [pk], ["uT"])
        stream_proj(w_in_d[l], 8, 512, hT, ["hT"], ev_u)
        mset("pool", vbp[:, :, 0:15], 0.0, ["vbp"])
        mset("pool", vbp[:, :, 15 + T:VC0], 0.0, ["vbp"])
        mset("pool", vbp[:, :, VC0 + TC:VP], 0.0, ["vbp"])
        for m in range(8):
            wa, wka = wload(w_in_d[l], 8, 512 + m * 128, 128)
            wg, wkg = wload(w_in_d[l], 8, 1536 + m * 128, 128)
            for bi, (b0, bn) in enumerate(BLOCKS):
                pa, pg = (bi % 2) * 2, (bi % 2) * 2 + 1
                for k in range(8):
                    mm(ps[pg][:, 0:bn], wg[:, k, 0:128], hT[:, k, b0:b0 + bn], k == 0, k == 7, [wkg, "hT"], [bkey("ps", pg)])
                for k in range(8):
                    mm(ps[pa][:, 0:bn], wa[:, k, 0:128], hT[:, k, b0:b0 + bn], k == 0, k == 7, [wka, "hT"], [bkey("ps", pa)])
                act(acc0[:, 0:bn], ps[pg][:, 0:bn], AF.Sigmoid, [bkey("ps", pg)], ["acc0"])
                vc = 15 + b0 if b0 < T else VC0
                tt("dve", vbp[:, m, vc:vc + bn], ps[pa][:, 0:bn], acc0[:, 0:bn], ALU.mult, [bkey("ps", pa), "acc0"], ["vbp"])
        for k in range(8):
            ld(xsp_o[:, k, :], xT[:, k, :], r=[XK[k]], w=["xsp"])
        ld(hsp_o[:, :, :], hT[:, :, :], r=["hT"], w=["hsp"])
        ld(u_o[:, :, :], uT[:, :, :], r=["uT"], w=["uo"])
        barrier(XK + S5K + ["hT", "yacc"])
        ld(tau[:], tau_d[0:1, :].partition_broadcast(128), w=["tau"])
        s5_tables(l)
        for j in range(16):
            for d in S5_DIRS:
                s5_tile(l, d, j, 1)
        barrier(["acc0", "pay"])
        mset("dve", pay[:], 0.0, ["pay"])
        cp("dve", pay[:, 0:32], hst[:, 2, :], ["hst"], ["pay"])
        cp("dve", pay[:, 32:64], hst[:, 3, :], ["hst"], ["pay"])
        cp("dve", pay[:, 304:336], hst[:, 0, :], ["hst"], ["pay"])
        cp("dve", pay[:, 336:368], hst[:, 1, :], ["hst"], ["pay"])
        pe_ = pay[:, 64:304].rearrange("p (q s f) -> p q s f", q=8, s=2)
        cp("dve", pe_[:, :, 0, :], vbp[:, :, 15:30], ["vbp"], ["pay"])
        cp("dve", pe_[:, :, 1, :], vbp[:, :, 15 + T - 15:15 + T], ["vbp"], ["pay"])
        ld(pay_o[:, :], pay[:], r=["pay"], w=["payo"])
        ld(vbp_o[:, :, :], vbp[:, :, :], r=["vbp"], w=["vo"])
        P.op("sync", None, r=["xsp", "hsp", "uo", "payo", "vo"], w=["doneA"])

    def layer_B(l, x_src):
        mod_vectors(l)
        ld(uT[:, :, :], uin_d[:, :, :], w=["uT"])
        ld(vbp[:, :, :], vin_d[:, :, :], w=["vbp"])
        P.dma("sync", ccs[:, :, :], ccs_d[:, :, :], w=WK + ["ccs"])
        ld(tau[:], tau_d[0:1, :].partition_broadcast(128), w=["tau"])
        HR, HI, T1, T2 = hst[:, 6, 0:16], hst[:, 7, 0:16], hst[:, 8, 0:16], hst[:, 9, 0:16]
        for d in S5_DIRS:
            c0 = (l * 2 + d) * 16
            LR, LI = Pq[:, 4, c0:c0 + 16], Pq[:, 5, c0:c0 + 16]
            SR, SI = hst[:, 4, d * 16:(d + 1) * 16], hst[:, 5, d * 16:(d + 1) * 16]
            cp("dve", HR, ccs[:, 0, 304 + d * 16:304 + d * 16 + 16], ["ccs"], ["hst"])
            cp("dve", HI, ccs[:, 0, 336 + d * 16:336 + d * 16 + 16], ["ccs"], ["hst"])
            order = list(range(NCORES)) if d == 0 else list(range(NCORES - 1, -1, -1))
            for n_, ci in enumerate(order):
                sc_ = sel[:, ci:ci + 1]
                if n_ == 0:
                    ts("dve", SR, HR, sc_, None, ALU.mult, None, ["hst", "par"], ["hst"])
                    ts("dve", SI, HI, sc_, None, ALU.mult, None, ["hst", "par"], ["hst"])
                else:
                    stt(SR, HR, sc_, SR, ALU.mult, ALU.add, ["hst", "par"], ["hst"])
                    stt(SI, HI, sc_, SI, ALU.mult, ALU.add, ["hst", "par"], ["hst"])
                if n_ == NCORES - 1:
                    break
                lr_, li_ = ccs[:, ci, d * 16:d * 16 + 16], ccs[:, ci, 32 + d * 16:32 + d * 16 + 16]
                tt("dve", T1, LR, HR, ALU.mult, ["hst", "Pq"], ["hst"])
                tt("dve", T2, LI, HI, ALU.mult, ["hst", "Pq"], ["hst"])
                tt("dve", T1, T1, T2, ALU.subtract, ["hst"], ["hst"])
                tt("dve", T2, LI, HR, ALU.mult, ["hst", "Pq"], ["hst"])
                tt("dve", HR, T1, lr_, ALU.add, ["hst", "ccs"], ["hst"])
                tt("dve", T1, LR, HI, ALU.mult, ["hst", "Pq"], ["hst"])
                tt("dve", T1, T1, T2, ALU.add, ["hst"], ["hst"])
                tt("dve", HI, T1, li_, ALU.add, ["hst", "ccs"], ["hst"])
        barrier(["acc0", "hal"])
        for side, so in ((0, 16), (1, 8)):
            hs = hal[:, side, :, :]
            for ci in range(NCORES):
                e_ = ccs[:, ci, 64:304].rearrange("p (q s f) -> p q s f", q=8, s=2)[:, :, 1 - side, :]
                sc_ = sel[:, 8 + side * 8 + ci: 8 + side * 8 + ci + 1]
                if ci == 0:
                    ts("dve", hs, e_, sc_, None, ALU.mult, None, ["ccs", "par"], ["hal"])
                else:
                    stt(hs, e_, sc_, hs, ALU.mult, ALU.add, ["ccs", "par", "hal"], ["hal"])
        cp("dve", vbp[:, :, 0:15], hal[:, 0, :, :], ["hal"], ["vbp"])
        cp("dve", vbp[:, :, 15 + T:15 + T + 15], hal[:, 1, :, :], ["hal"], ["vbp"])
        barrier(S5K + ["yacc", "hT", "hal", "acc0", "pay"] + XK)
        s5_tables(l)
        for j in range(16):
            for d in S5_DIRS:
                s5_tile(l, d, j, 2)
        barrier(S5K + ["ya", "ya2", "cvo", "acc0", "gt0", "tabB", "tabC", "rstd"] + XK)
        gtmp = sb("gtmp", [TT], F32, O_A + 36864)
        for q in range(4):
            stt(gt0[:], uT[:, q, :], ssmd[:, l, q:q + 1], yacc[:, q, :], ALU.mult, ALU.add, ["uT", "par", "yacc"], ["gt0"])
            act(gtmp[:], gt0[:], AF.Square, ["gt0"], ["cvo"])
            ts("pool", gtmp[:], gtmp[:], 0.044715, 1.0, ALU.mult, ALU.add, ["cvo"], ["cvo"])
            tt("pool", gtmp[:], gtmp[:], gt0[:], ALU.mult, ["cvo", "gt0"], ["cvo"])
            act(gtmp[:], gtmp[:], AF.Sigmoid, ["cvo"], ["cvo"], scale=1.5957691216057308)
            tt("dve", ya[:, q, :], gt0[:], gtmp[:], ALU.mult, ["gt0", "cvo"], ["ya"])

        def ev_glu(m, bi, b0, bn, p_, pk):
            act(gt0[:, 0:bn], p_, AF.Sigmoid, [pk], ["gt0"])
            tt("dve", ya2[:, m, b0:b0 + bn], ya[:, m, b0:b0 + bn], gt0[:, 0:bn], ALU.mult, ["ya", "gt0"], ["ya2"])
        stream_proj(w_glu_d[l], 4, 512, ya, ["ya"], ev_glu)
        barrier(["uT", "dg", "gt1", "sqt"])
        for q in range(8):
            for jt in range(31):
                ts("pool", dg[:, jt, :], ident[:, :], dwk[:, l, q, jt:jt + 1], None, ALU.mult, None, ["par"], ["dg"])
            for (o0, v0, n) in ((0, 0, 512), (512, 512, 512), (1024, 1024, 512), (1536, 1536, 512), (T, VC0 - 15, TC)):
                pi = pscnt[0] % 4
                pscnt[0] += 1
                for jt in range(31):
                    mm(ps[pi][:, 0:n], dg[:, jt, :], vbp[:, q, v0 + jt:v0 + jt + n], jt == 0, jt == 30, ["dg", "vbp"],
                       [bkey("ps", pi)])
                act(cvo[:, q, o0:o0 + n], ps[pi][:, 0:n], AF.Identity, [bkey("ps", pi), "par"], ["cvo"],
                    bias=dwb[:, l, q:q + 1])
        for bi, (b0, bn) in enumerate(BLOCKS):
            for q in range(8):
                act(sqt[:, 0:bn], cvo[:, q, b0:b0 + bn], AF.Square, ["cvo"], ["sqt"])
                mm(ps[4][:, 0:bn], ones_bf[:, :], cvo[:, q, b0:b0 + bn], q == 0, q == 7, ["cvo", "par"], [bkey("ps", 4)])
                mm(ps[5][:, 0:bn], ones_bf[:, :], sqt[:, 0:bn], q == 0, q == 7, ["sqt", "par"], [bkey("ps", 5)])
            mean, rs_ = gt0[:, 0:bn], gt0[:, 512:512 + bn]
            ts("dve", mean, ps[4][:, 0:bn], 1.0 / D, None, ALU.mult, None, [bkey("ps", 4)], ["gt0"])
            tt("dve", rs_, mean, mean, ALU.mult, ["gt0"], ["gt0"])
            stt(rs_, ps[5][:, 0:bn], 1.0 / D, rs_, ALU.mult, ALU.subtract, [bkey("ps", 5), "gt0"], ["gt0"])
            act(rs_, rs_, AF.Sqrt, ["gt0", "par"], ["gt0"], bias=e6[:, 0:1])
            recip(rs_, rs_, ["gt0"], ["gt0"])
            for q in range(8):
                t_ = gt0[:, 1024:1024 + bn]
                tt("dve", t_, cvo[:, q, b0:b0 + bn], mean, ALU.subtract, ["cvo", "gt0"], ["gt0"])
                tt("dve", t_, t_, rs_, ALU.mult, ["gt0"], ["gt0"])
                act(cvo[:, q, b0:b0 + bn], t_, AF.Silu, ["gt0", "par"], ["cvo"], bias=cnb[:, l, q:q + 1],
                    scale=cng[:, l, q:q + 1])
        barrier(["yacc", "hT", "mrg", "vbp"])
        ld(hT[:, :, :], hin_d[:, :, :], w=["hT"])
        for m in range(8):
            wga, kga = wload(w_in_d[l], 8, 2560 + m * 128, 128)
            wgb, kgb = wload(w_in_d[l], 8, 3584 + m * 128, 128)
            for bi, (b0, bn) in enumerate(BLOCKS):
                for k in range(8):
                    mm(ps[0][:, 0:bn], wga[:, k, 0:128], hT[:, k, b0:b0 + bn], k == 0, k == 7, [kga, "hT"], [bkey("ps", 0)])
                for k in range(8):
                    mm(ps[1][:, 0:bn], wgb[:, k, 0:128], hT[:, k, b0:b0 + bn], k == 0, k == 7, [kgb, "hT"], [bkey("ps", 1)])
                act(gt0[:, b0:b0 + bn], ps[0][:, 0:bn], AF.Sigmoid, [bkey("ps", 0)], ["gt0"])
                act(gt1[:, b0:b0 + bn], ps[1][:, 0:bn], AF.Sigmoid, [bkey("ps", 1)], ["gt1"])
            wpa, kpa = wload(w_pa_d[l], 4, m * 128, 128)
            wpb, kpb = wload(w_pb_d[l], 8, m * 128, 128)
            for bi, (b0, bn) in enumerate(BLOCKS):
                for k in range(4):
                    mm(ps[2][:, 0:bn], wpa[:, k, 0:128], ya2[:, k, b0:b0 + bn], k == 0, k == 3, [kpa, "ya2"], [bkey("ps", 2)])
                for k in range(8):
                    mm(ps[3][:, 0:bn], wpb[:, k, 0:128], cvo[:, k, b0:b0 + bn], k == 0, k == 7, [kpb, "cvo"], [bkey("ps", 3)])
                tt("dve", gt0[:, b0:b0 + bn], gt0[:, b0:b0 + bn], ps[2][:, 0:bn], ALU.mult, ["gt0", bkey("ps", 2)], ["gt0"])
                tt("dve", gt1[:, b0:b0 + bn], gt1[:, b0:b0 + bn], ps[3][:, 0:bn], ALU.mult, ["gt1", bkey("ps", 3)], ["gt1"])
                tt("pool", mrg[:, m, b0:b0 + bn], gt0[:, b0:b0 + bn], gt1[:, b0:b0 + bn], ALU.add, ["gt0", "gt1"], ["mrg"])
        barrier(["ya", "ya2", "cvo"] + XK)
        for k in range(8):
            ld(xT[:, k, :], x_src[:, k, :], w=[XK[k]])

        def ev_out(m, bi, b0, bn, p_, pk):
            mc = 1 if b0 >= T else 0
            stt(xT[:, m, b0:b0 + bn], p_, modv[:, 16 + m, mc:mc + 1], xT[:, m, b0:b0 + bn], ALU.mult, ALU.add,
                [pk, "modv", XK[m]], [XK[m]])
        stream_proj(w_out_d[l], 8, D, mrg, ["mrg"], ev_out)
        barrier(["acc0", "gt0", "rstd", "tabB", "tabC", "hT", "uT", "gt1", "dg", "sqt", "mrg", "vbp"] + ACTK + WDK)
        rms_stats()
        modulate(gs2, 24)
        hb = [(0, 512), (512, 256)]
        for part in range(3):
            h0 = part * HALF
            hTh = hT[:, :, h0:h0 + HALF]

            def ev_gate(m, bi, b0, bn, p_, pk):
                act(actT[:, m, b0:b0 + bn], p_, AF.Silu, [pk], [bkey("actT", m)])

            def ev_up(m, bi, b0, bn, p_, pk):
                tt("dve", actT[:, m, b0:b0 + bn], actT[:, m, b0:b0 + bn], p_, ALU.mult, [pk, bkey("actT", m)],
                   [bkey("actT", m)])
            stream_proj(w_fg_d[l], 8, FFN, hTh, ["hT"], ev_gate, blocks=hb)
            stream_proj(w_fu_d[l], 8, FFN, hTh, ["hT"], ev_up, blocks=hb)
            for m in range(8):
                wb, wk = wload(w_fd_d[l], 22, m * 128, 128, wbufs=Wd, wkey="Wd")
                for bi, (b0, bn) in enumerate(hb):
                    pi = pscnt[0] % 4
                    pscnt[0] += 1
                    for k in range(22):
                        mm(ps[pi][:, 0:bn], wb[:, k, :], actT[:, k, b0:b0 + bn], k == 0, k == 21,
                           [wk, bkey("actT", k)], [bkey("ps", pi)])
                    t0 = h0 + b0
                    segs = [(t0, T, 0), (T, t0 + bn, 1)] if t0 < T < t0 + bn else [(t0, t0 + bn, 0 if t0 + bn <= T else 1)]
                    for (s0, s1, mc) in segs:
                        stt(xT[:, m, s0:s1], ps[pi][:, s0 - t0:s1 - t0], modv[:, 40 + m, mc:mc + 1], xT[:, m, s0:s1],
                            ALU.mult, ALU.add, [bkey("ps", pi), "modv", XK[m]], [XK[m]])
        barrier(["hT", "uT", "vbp", "mrg", "acc0", "gt0", "gt1", "dg", "sqt"] + ACTK + WDK)

    if stage == "A0":
        layer_A(0)
    elif stage == "B0A1":
        layer_B(0, xin_d)
        layer_A(1)
    else:
        layer_B(1, xin_d)
        rms_stats()
        for k in range(8):
            tt("dve", acc0[:, 0:T], xT[:, k, 0:T], rstd[:, 0:T], ALU.mult, [XK[k], "rstd"], ["acc0"])
            ts("dve", acc0[:, 0:T], acc0[:, 0:T], nfg[:, k:k + 1], None, ALU.mult, None, ["acc0", "par"], ["acc0"])
            P.dma("sync", out_d[:, k, :], acc0[:, 0:T], r=["acc0"], w=["out"])
        P.op("sync", None, r=["out"], w=["done"])
    P.emit()
    return nc


def _tT(v):
    return np.ascontiguousarray(v.reshape(-1, 128).T)


def prep_inputs(inp):
    f = np.float32
    x = np.asarray(inp["x"], f)[0]
    ctx = np.asarray(inp["ctx"], f)[0]
    common = {}
    cT = np.ascontiguousarray(ctx.T.reshape(8, 128, TC).transpose(1, 0, 2))
    cv = np.stack([np.asarray(inp["c"], f)[0], np.asarray(inp["c_ctx"], f)], axis=-1)
    common["cvec"] = np.ascontiguousarray(cv.reshape(8, 128, 2).transpose(1, 0, 2))
    q = 256
    om = (1.0 / (np.float32(10000.0) ** (np.arange(q, dtype=f) / f(q)))).astype(f)
    omega = np.ascontiguousarray(om.reshape(2, 128).T)
    cidx = (np.arange(T) % 64).astype(f)[None]
    common["tau"] = np.arange(T).astype(f)[None]
    common["ident"] = np.eye(128, dtype=f)
    for k in ("w_mod", "w_in", "w_glu", "w_proj_a", "w_proj_b", "w_out", "w_ffn_gate", "w_ffn_up", "w_ffn_down"):
        common[k] = np.ascontiguousarray(np.asarray(inp[k], f))

    def pl(v):
        v = np.asarray(v, f)
        return np.ascontiguousarray(v.reshape(DEPTH, -1, 128).transpose(2, 0, 1))

    common["bmodT"] = pl(inp["b_mod"])
    common["n1gT"] = pl(inp["norm1_g"])
    common["n2gT"] = pl(inp["norm2_g"])
    common["ssmdT"] = pl(inp["ssm_d"])
    common["dwbT"] = pl(inp["dw_bias"])
    common["cngT"] = pl(inp["conv_norm_g"])
    common["cnbT"] = pl(inp["conv_norm_b"])
    common["nfgT"] = _tT(np.asarray(inp["norm_f_g"], f))
    dk = np.asarray(inp["dw_kernel"], f)
    common["dwkT"] = np.ascontiguousarray(dk.reshape(DEPTH, 31, 8, 128).transpose(3, 0, 2, 1))
    lre = np.asarray(inp["ssm_lam_re"], f); lim = np.asarray(inp["ssm_lam_im"], f)
    ls = np.asarray(inp["ssm_log_step"], f)
    bre = np.asarray(inp["ssm_b_re"], f); bim = np.asarray(inp["ssm_b_im"], f)
    cre = np.asarray(inp["ssm_c_re"], f); cim = np.asarray(inp["ssm_c_im"], f)
    lsx = np.ascontiguousarray(np.broadcast_to(ls[..., None], lre.shape))

    def pview(a):
        return a.reshape(DEPTH, 2, 16, 128).transpose(3, 0, 1, 2).reshape(128, -1)
    common["P_l"] = np.ascontiguousarray(np.stack([pview(lre), pview(lim), pview(lsx)], axis=1))

    def eview(a):
        outa = np.zeros((DEPTH, 2, 128, 4, 128), f)
        for gi in range(8):
            for qq in range(4):
                for gl in range(2):
                    g = 8 * qq + 2 * (gi // 2) + gl
                    outa[:, :, gi * 16:(gi + 1) * 16, qq, gl * 64:(gl + 1) * 64] = a[:, :, g][:, :, None, :]
        return outa.reshape(DEPTH, 2, 128, 512)
    common["E_l"] = np.ascontiguousarray(np.stack([eview(lre), eview(lim), eview(lsx)], axis=2))

    def ebuild(b):
        outa = np.zeros((DEPTH, 2, 128, 4, 128), f)
        for gi in range(8):
            for qq in range(4):
                gl = gi % 2
                outa[:, :, gi * 16:(gi + 1) * 16, qq, gl * 64:(gl + 1) * 64] = b[:, :, 8 * qq + gi].transpose(0, 1, 3, 2)
        return outa.reshape(DEPTH, 2, 128, 512)
    common["E_b"] = np.ascontiguousarray(np.stack([ebuild(bre), ebuild(bim)], axis=2))

    def cbuild(cc):
        outa = np.zeros((DEPTH, 2, 128, 16, 32), f)
        for j in range(16):
            for gl in range(2):
                outa[:, :, gl * 64:(gl + 1) * 64, j, gl * 16:(gl + 1) * 16] = cc[:, :, 2 * j + gl].transpose(0, 1, 3, 2)
        return outa.reshape(DEPTH, 2, 128, 512)
    common["P_c"] = np.ascontiguousarray(np.stack([cbuild(cre), cbuild(cim)], axis=2))
    maps = []
    for ci in range(NCORES):
        m = dict(common)
        s = np.zeros((128, 24), f)
        s[:, ci] = 1.0
        if ci > 0:
            s[:, 8 + ci - 1] = 1.0
        if ci < NCORES - 1:
            s[:, 16 + ci + 1] = 1.0
        m["sel"] = s
        maps.append(m)
    first = []
    for ci in range(NCORES):
        xs = x[ci * T:(ci + 1) * T]
        first.append({
            "xT": np.ascontiguousarray(xs.T.reshape(8, 128, T).transpose(1, 0, 2)),
            "cT": cT, "omega": omega, "cidx": cidx,
            "ridx": ((ci * T + np.arange(T)) // 64).astype(f)[None],
        })
    return maps, first


_NC = {}
_DBG = None


def _prog(stage):
    if stage not in _NC:
        _NC[stage] = build(stage)
    return _NC[stage]


def _handover(res):
    pays = np.stack([np.asarray(res.results[ci]["pay_o"]) for ci in range(NCORES)], axis=1)
    nxt = []
    for ci in range(NCORES):
        r = res.results[ci]
        nxt.append({"xsp_i": r["xsp_o"], "hsp_i": r["hsp_o"], "u_i": r["u_o"], "vbp_i": r["vbp_o"],
                    "ccs_i": np.ascontiguousarray(pays)})
    return nxt


def kernel(**inputs):
    maps, first = prep_inputs(inputs)
    cores = list(range(NCORES))
    res = run_bass_kernel_spmd(_prog("A0"), [dict(maps[c], **first[c]) for c in cores], core_ids=cores)
    nxt = _handover(res)
    if _DBG is not None:
        _DBG["h1"] = nxt
    res = run_bass_kernel_spmd(_prog("B0A1"), [dict(maps[c], **nxt[c]) for c in cores], core_ids=cores)
    nxt = _handover(res)
    if _DBG is not None:
        _DBG["h2"] = nxt
    res = run_bass_kernel_spmd(_prog("B1"), [dict(maps[c], **nxt[c]) for c in cores], core_ids=cores)
    outs = []
    for ci in cores:
        o = np.asarray(res.results[ci]["outT"])
        outs.append(o.transpose(2, 1, 0).reshape(T, D))
    return np.concatenate(outs, axis=0)[None].astype(np.float32)
```

```python
import math
import numpy as np
import concourse.bass as bass
import concourse.mybir as mybir
from concourse.bass_utils import run_bass_kernel_spmd

F32 = mybir.dt.float32
BF16 = mybir.dt.bfloat16
ALU = mybir.AluOpType
AF = mybir.ActivationFunctionType

NCORES = 8
D = 1024
T = 2048
TC = 256
TT = T + TC
DEPTH = 2
FFN = 2816
WIN = 4608
PI = math.pi
MAGIC = 12582912.0
BLOCKS = [(0, 512), (512, 512), (1024, 512), (1536, 512), (2048, 256)]


class Prog:
    ENG = ("sync", "act", "dve", "pool", "pe")

    def __init__(self, nc):
        self.nc = nc
        self.ops = []
        self.lw = {}
        self.rd = {}

    def _rec(self, eng, fn, r, w, dma):
        i = len(self.ops)
        deps = {}
        for b in r:
            j = self.lw.get(b)
            if j is not None:
                deps[j] = deps.get(j, False) or True
        for b in w:
            j = self.lw.get(b)
            if j is not None:
                deps[j] = True
            for j in self.rd.get(b, ()):
                if j not in deps:
                    deps[j] = False
        for b in r:
            self.rd.setdefault(b, []).append(i)
        for b in w:
            self.lw[b] = i
            self.rd[b] = []
        keep = []
        for j, strong in deps.items():
            oj = self.ops[j]
            if oj["eng"] == eng and not oj["dma"] and not dma:
                if eng == "pe" or not strong:
                    continue
            keep.append(j)
        self.ops.append(dict(eng=eng, fn=fn, deps=keep, dma=dma, sig=None))
        return i

    def op(self, eng, fn, r=(), w=()):
        return self._rec(eng, fn, tuple(r), tuple(w), False)

    def dma(self, eng, out, in_, r=(), w=()):
        return self._rec(eng, lambda e: e.dma_start(out=out, in_=in_), tuple(r), tuple(w), True)

    def emit(self, ndma_sems=64):
        nc = self.nc
        waited_on = set()
        for o in self.ops:
            for j in o["deps"]:
                waited_on.add(j)
        esem = {e: nc.alloc_semaphore("s_" + e) for e in self.ENG}
        dsem = [nc.alloc_semaphore("d%d" % k) for k in range(ndma_sems)]
        ecnt = {e: 0 for e in self.ENG}
        dcnt = [0] * ndma_sems
        nd = 0
        for i, o in enumerate(self.ops):
            if (i in waited_on or o["dma"]) and o["fn"] is not None:
                if o["dma"]:
                    k = nd % ndma_sems
                    nd += 1
                    dcnt[k] += 16
                    o["sig"] = (("d", k), dsem[k], dcnt[k], 16)
                else:
                    ecnt[o["eng"]] += 1
                    o["sig"] = (("e", o["eng"]), esem[o["eng"]], ecnt[o["eng"]], 1)
        per = {e: [] for e in self.ENG}
        for o in self.ops:
            per[o["eng"]].append(o)
        ops = self.ops

        def run(eng_name, e):
            waited = {}
            for o in per[eng_name]:
                for j in sorted(o["deps"]):
                    s = ops[j]["sig"]
                    if s is None:
                        continue
                    ch, sem, val, _ = s
                    if waited.get(ch, 0) >= val:
                        continue
                    waited[ch] = val
                    e.wait_ge(sem, val)
                if o["fn"] is None:
                    continue
                if o["dma"] and o["sig"][2] > 16:
                    ch = o["sig"][0]
                    if waited.get(ch, 0) < o["sig"][2] - 16:
                        waited[ch] = o["sig"][2] - 16
                        e.wait_ge(o["sig"][1], o["sig"][2] - 16)
                ins = o["fn"](e)
                if o["sig"] is not None:
                    ins.then_inc(o["sig"][1], o["sig"][3])

        with nc.Block() as block:
            @block.sync
            def _(e):
                run("sync", e)

            @block.scalar
            def _(e):
                run("act", e)

            @block.vector
            def _(e):
                run("dve", e)

            @block.gpsimd
            def _(e):
                run("pool", e)

            @block.tensor
            def _(e):
                run("pe", e)


NPAY = 64 + 240 + 64
VP = 15 + T + 15 + 15 + TC + 15
VC0 = 15 + T + 15 + 15
HALF = TT // 3
S5_DIRS = (0, 1)


def build(stage):
    nc = bass.Bass("TRN2", target_bir_lowering=False)
    P = Prog(nc)

    def din(name, shape, dt=F32):
        return nc.dram_tensor(name, list(shape), dt, kind="ExternalInput").ap()

    def dout(name, shape, dt=F32):
        return nc.dram_tensor(name, list(shape), dt, kind="ExternalOutput").ap()

    first = stage == "A0"
    last = stage == "B1"
    if first:
        xT_d = din("xT", [128, 8, T])
        cT_d = din("cT", [128, 8, TC])
        omega_d = din("omega", [128, 2])
        ridx_d = din("ridx", [1, T])
        cidx_d = din("cidx", [1, T])
    else:
        xin_d = din("xsp_i", [128, 8, TT])
        hin_d = din("hsp_i", [128, 8, TT], BF16)
        uin_d = din("u_i", [128, 4, TT], BF16)
        vin_d = din("vbp_i", [128, 8, VP], BF16)
        ccs_d = din("ccs_i", [128, NCORES, NPAY])
    cvec_d = din("cvec", [128, 8, 2])
    tau_d = din("tau", [1, T])
    sel_d = din("sel", [128, 24])
    w_mod_d = din("w_mod", [DEPTH, D, 6 * D])
    bmod_d = din("bmodT", [128, DEPTH, 48])
    n1g_d = din("n1gT", [128, DEPTH, 8])
    n2g_d = din("n2gT", [128, DEPTH, 8])
    w_in_d = din("w_in", [DEPTH, D, WIN])
    P_l_d = din("P_l", [128, 3, 64])
    P_c_d = din("P_c", [DEPTH, 2, 2, 128, 512])
    E_l_d = din("E_l", [DEPTH, 2, 3, 128, 512])
    E_b_d = din("E_b", [DEPTH, 2, 2, 128, 512])
    ssmd_d = din("ssmdT", [128, DEPTH, 4])
    w_glu_d = din("w_glu", [DEPTH, 512, 512])
    w_pa_d = din("w_proj_a", [DEPTH, 512, D])
    dwk_d = din("dwkT", [128, DEPTH, 8, 31])
    dwb_d = din("dwbT", [128, DEPTH, 8])
    cng_d = din("cngT", [128, DEPTH, 8])
    cnb_d = din("cnbT", [128, DEPTH, 8])
    w_pb_d = din("w_proj_b", [DEPTH, D, D])
    w_out_d = din("w_out", [DEPTH, D, D])
    w_fg_d = din("w_ffn_gate", [DEPTH, D, FFN])
    w_fu_d = din("w_ffn_up", [DEPTH, D, FFN])
    w_fd_d = din("w_ffn_down", [DEPTH, FFN, D])
    nfg_d = din("nfgT", [128, 8])
    ident_d = din("ident", [128, 128])
    if last:
        out_d = dout("outT", [128, 8, T])
    else:
        xsp_o = dout("xsp_o", [128, 8, TT])
        hsp_o = dout("hsp_o", [128, 8, TT], BF16)
        u_o = dout("u_o", [128, 4, TT], BF16)
        vbp_o = dout("vbp_o", [128, 8, VP], BF16)
        pay_o = dout("pay_o", [128, NPAY])

    _n = [0]

    def sb(name, shape, dt, off):
        _n[0] += 1
        return nc.alloc_sbuf_tensor_at("%s_%d" % (name, _n[0]), [128] + list(shape), dt, offset=off)

    O_PAR, O_W, O_RSTD, O_A, O_B, O_C, O_D, O_E = 16512, 28800, 41088, 50304, 124032, 160896, 179328, 217152
    END = 229376
    po = [O_PAR]

    def par(name, shape, dt=F32):
        sz = int(np.prod(shape)) * (4 if dt == F32 else 2)
        sz = (sz + 31) // 32 * 32
        t = sb(name, shape, dt, po[0])
        po[0] += sz
        assert po[0] <= O_W, po[0]
        return t

    modv = par("modv", [48, 2])
    gs1 = par("gs1", [8, 2])
    gs2 = par("gs2", [8, 2])
    bmod = par("bmod", [DEPTH, 48])
    n1g = par("n1g", [DEPTH, 8])
    n2g = par("n2g", [DEPTH, 8])
    nfg = par("nfg", [8])
    ssmd = par("ssmd", [DEPTH, 4])
    dwk = par("dwk", [DEPTH, 8, 31])
    dwb = par("dwb", [DEPTH, 8])
    cng = par("cng", [DEPTH, 8])
    cnb = par("cnb", [DEPTH, 8])
    omega = par("omega", [2])
    sel = par("sel", [24])
    csil = par("csil", [8, 2], BF16)
    cvec = par("cvec", [8, 2])
    ones_bf = par("ones_bf", [128], BF16)
    Pl = par("Pl", [3, 64])
    Pq = par("Pq", [14, 64])
    hst = par("hst", [10, 32])
    ini = par("ini", [4])
    e6 = par("e6", [8])
    halfpi = par("halfpi", [1])
    scr = par("scr", [1])
    ident = par("ident", [128], BF16)

    Wb = [sb("Wb%d" % i, [8, 384], BF16, O_W + i * 6144) for i in range(2)]
    rstd = sb("rstd", [TT], F32, O_RSTD)
    xT = sb("xT", [8, TT], F32, O_A)
    hT = sb("hT", [8, TT], BF16, O_B)
    uT = sb("uT", [4, TT], BF16, O_C)
    vbp = sb("vbp", [8, VP], BF16, O_D)
    acc0 = sb("acc0", [TT], F32, O_E)
    acc1 = sb("acc1", [TT], F32, O_B)
    o = O_A
    tau = sb("tau", [T], F32, o); o += 8192
    SN = sb("SN", [T], F32, o); o += 8192
    CS = sb("CS", [T], F32, o); o += 8192
    W1 = sb("W1", [TT], F32, o); o += 9216
    W2 = sb("W2", [TT], F32, o); o += 9216
    G1 = sb("G1", [TT], F32, o); o += 9216
    G2 = sb("G2", [TT], F32, o); o += 9216
    HHf = sb("HHf", [TT], F32, o)
    hre = sb("hre", [TT], BF16, o); o += 4608
    him = sb("him", [TT], BF16, o); o += 4608
    assert o <= O_B
    ET = [sb("ET%d" % i, [512], F32, O_A + 24576 + i * 2048) for i in range(16)]
    yacc = sb("yacc", [4, TT], F32, O_B)
    tabB = sb("tabB", [2, 2, 512], BF16, O_RSTD)
    tabC = sb("tabC", [2, 2, 512], BF16, O_RSTD + 4096)
    rmul = sb("rmul", [T], F32, O_E)
    ccs = sb("ccs", [NCORES, NPAY], F32, O_W)
    pay = sb("pay", [NPAY], F32, O_E)
    hal = sb("hal", [2, 8, 15], F32, O_E + 2048)
    ya = sb("ya", [4, TT], BF16, O_A)
    ya2 = sb("ya2", [4, TT], BF16, O_A + 18432)
    cvo = sb("cvo", [8, TT], BF16, O_A + 36864)
    gt0 = sb("gt0", [TT], F32, O_E)
    gt1 = sb("gt1", [TT], F32, O_C)
    dg = sb("dg", [31, 128], BF16, O_C + 9216)
    sqt = sb("sqt", [512], BF16, O_C + 9216 + 7936)
    mrg = sb("mrg", [8, TT], BF16, O_D)
    actT = sb("actT", [22, HALF], BF16, O_C)
    Wd = [sb("Wd%d" % i, [22, 128], BF16, O_C + 22 * HALF * 2 + i * 5632) for i in range(2)]
    assert O_C + 22 * HALF * 2 + 2 * 5632 <= O_E

    ps = [nc.alloc_psum_tensor("ps%d" % i, [128, 512], F32) for i in range(8)]

    def bkey(name, *idx):
        return (name,) + idx

    XK = [bkey("xT", k) for k in range(8)]
    ACTK = [bkey("actT", m) for m in range(22)]
    WDK = [bkey("Wd", 0), bkey("Wd", 1)]
    S5K = ["tau", "SN", "CS", "W1", "W2", "G1", "G2", "HH"]

    def tt(eng, out, a, b, op, r, w):
        P.op(eng, lambda e: e.tensor_tensor(out=out, in0=a, in1=b, op=op), r, w)

    def ts(eng, out, a, s1, s2, op0, op1, r, w):
        if op1 is None:
            P.op(eng, lambda e: e.tensor_scalar(out=out, in0=a, scalar1=s1, scalar2=None, op0=op0), r, w)
        else:
            P.op(eng, lambda e: e.tensor_scalar(out=out, in0=a, scalar1=s1, scalar2=s2, op0=op0, op1=op1), r, w)

    def stt(out, a, s, b, op0, op1, r, w):
        P.op("dve", lambda e: e.scalar_tensor_tensor(out=out, in0=a, scalar=s, in1=b, op0=op0, op1=op1), r, w)

    def act(out, a, func, r, w, bias=None, scale=None):
        kw = {}
        if bias is not None:
            kw["bias"] = bias
        if scale is not None:
            kw["scale"] = scale
        P.op("act", lambda e: e.activation(out=out, in_=a, func=func, **kw), r, w)

    def recip(out, a, r, w):
        P.op("dve", lambda e: e.reciprocal(out=out, in_=a), r, w)

    def cp(eng, out, a, r, w):
        P.op(eng, lambda e: e.tensor_copy(out=out, in_=a), r, w)

    def mset(eng, out, val, w):
        P.op(eng, lambda e: e.memset(out, val), (), w)

    def mm(out, lhsT, rhs, start, stop, r, w, tp=None):
        if tp is None:
            P.op("pe", lambda e: e.matmul(out, lhsT=lhsT, rhs=rhs, start=start, stop=stop), r, w)
        else:
            P.op("pe", lambda e: e.matmul(out, lhsT=lhsT, rhs=rhs, start=start, stop=stop, tile_position=tp), r, w)

    def scan(out, d0, d1, init, r, w):
        P.op("dve", lambda e: e.tensor_tensor_scan(out=out, data0=d0, data1=d1, initial=init, op0=ALU.mult,
                                                   op1=ALU.add), r, w)

    def barrier(keys):
        P.op("dve", lambda e: e.memset(scr[:], 0.0), (), list(keys))

    def ld(out, in_, r=(), w=(), eng="sync"):
        P.dma(eng, out, in_, r, w)

    WK = [bkey("Wb", 0), bkey("Wb", 1)]

    plist = [(bmod, bmod_d), (n1g, n1g_d), (n2g, n2g_d), (nfg, nfg_d), (ssmd, ssmd_d), (dwk, dwk_d),
             (dwb, dwb_d), (cng, cng_d), (cnb, cnb_d), (sel, sel_d), (cvec, cvec_d)]
    if first:
        plist.append((omega, omega_d))
    for t_, d_ in plist:
        ld(t_[:], d_, w=["par"])
    ld(Pl[:], P_l_d, w=["Pl"])
    P.dma("pool", ident[:], ident_d, w=["par"])
    mset("dve", ones_bf[:], 1.0, ["par"])
    mset("dve", halfpi[:], PI / 2, ["par"])
    mset("dve", e6[:], 1e-6, ["par"])
    act(csil[:], cvec[:], AF.Silu, ["par"], ["csil"])

    def sincos(eng, s_out, c_out, u, tmp, r, w):
        ts(eng, tmp, u, MAGIC, None, ALU.add, None, r, w)
        ts(eng, tmp, tmp, -MAGIC, None, ALU.add, None, w, w)
        tt(eng, tmp, u, tmp, ALU.subtract, list(r) + list(w), w)
        act(c_out, tmp, AF.Abs, w, w)
        act(c_out, c_out, AF.Sin, list(w) + ["par"], w, bias=halfpi[:, 0:1], scale=-2 * PI)
        act(s_out, tmp, AF.Sin, w, w, scale=2 * PI)

    def disc(lre, lim, ls, r_o, cth, sth, dt_o, th_o, tmp, rk, wk):
        act(dt_o, ls, AF.Exp, rk, wk)
        tt("dve", r_o, lre, dt_o, ALU.mult, list(rk) + list(wk), wk)
        tt("dve", th_o, lim, dt_o, ALU.mult, list(rk) + list(wk), wk)
        ts("dve", th_o, th_o, 1.0 / (2 * PI), None, ALU.mult, None, wk, wk)
        sincos("dve", sth, cth, th_o, tmp, wk, wk)

    PQ = lambda i: Pq[:, i, :]
    disc(Pl[:, 0, :], Pl[:, 1, :], Pl[:, 2, :], PQ(7), PQ(2), PQ(3), PQ(6), PQ(1), PQ(8), ["Pl", "par"], ["Pq"])
    act(PQ(0), PQ(7), AF.Exp, ["Pq"], ["Pq"])
    act(PQ(9), PQ(7), AF.Exp, ["Pq"], ["Pq"], scale=float(T))
    ts("dve", PQ(10), PQ(1), float(T), None, ALU.mult, None, ["Pq"], ["Pq"])
    sincos("dve", PQ(11), PQ(12), PQ(10), PQ(8), ["Pq"], ["Pq"])
    tt("dve", PQ(4), PQ(9), PQ(12), ALU.mult, ["Pq"], ["Pq"])
    tt("dve", PQ(5), PQ(9), PQ(11), ALU.mult, ["Pq"], ["Pq"])

    if first:
        for k in range(8):
            ld(xT[:, k, 0:T], xT_d[:, k, :], w=[XK[k]])
            ld(xT[:, k, T:TT], cT_d[:, k, :], w=[XK[k]])
        for k in range(8):
            idx_d = ridx_d if k < 4 else cidx_d
            if k in (0, 4):
                ld(acc0[:, 0:T], idx_d[0:1, :].partition_broadcast(128), w=["acc0"])
            ph = (PI / 2 if (k // 2) % 2 == 1 else 0.0)
            ts("dve", acc1[:, 0:T], acc0[:, 0:T], omega[:, k % 2:k % 2 + 1], ph, ALU.mult, ALU.add,
               ["acc0", "par"], ["hT"])
            ts("dve", rstd[:, 0:T], acc1[:, 0:T], 1.0 / (2 * PI), MAGIC, ALU.mult, ALU.add, ["hT"], ["rstd"])
            ts("dve", rstd[:, 0:T], rstd[:, 0:T], -MAGIC, None, ALU.add, None, ["rstd"], ["rstd"])
            stt(acc1[:, 0:T], rstd[:, 0:T], -2 * PI, acc1[:, 0:T], ALU.mult, ALU.add, ["rstd", "hT"], ["hT"])
            act(acc1[:, 0:T], acc1[:, 0:T], AF.Sin, ["hT", "par"], ["hT"])
            tt("dve", xT[:, k, 0:T], xT[:, k, 0:T], acc1[:, 0:T], ALU.add, ["hT", XK[k]], [XK[k]])

    wslot = [0]

    def wload(Wd_ap, KT, c0, cw, wbufs=None, wkey="Wb"):
        wbufs = wbufs or Wb
        s = wslot[0] % 2
        wslot[0] += 1
        wb = wbufs[s]
        src = Wd_ap[:, c0:c0 + cw].rearrange("(k p) f -> p k f", p=128)
        P.dma("pool", wb[:, 0:KT, 0:cw], src, w=[bkey(wkey, s)])
        return wb, bkey(wkey, s)

    pscnt = [0]

    def stream_proj(Wd_ap, KT, ncols, inT, inkeys, evac, col0=0, blocks=BLOCKS, chunk=384):
        nchunk = (ncols + chunk - 1) // chunk
        for c in range(nchunk):
            cw = min(chunk, ncols - c * chunk)
            wb, wk = wload(Wd_ap, KT, col0 + c * chunk, cw)
            for mi in range(cw // 128):
                m = c * (chunk // 128) + mi
                for bi, (b0, bn) in enumerate(blocks):
                    pi = pscnt[0] % 4
                    pscnt[0] += 1
                    for k in range(KT):
                        mm(ps[pi][:, 0:bn], wb[:, k, mi * 128:(mi + 1) * 128], inT[:, k, b0:b0 + bn],
                           k == 0, k == KT - 1, [wk] + list(inkeys), [bkey("ps", pi)])
                    evac(m, bi, b0, bn, ps[pi][:, 0:bn], bkey("ps", pi))

    def rms_stats():
        for bi, (b0, bn) in enumerate(BLOCKS):
            for k in range(8):
                act(hT[:, k, b0:b0 + bn], xT[:, k, b0:b0 + bn], AF.Square, [XK[k]], ["hT"])
            for k in range(8):
                mm(ps[4][:, 0:bn], ones_bf[:, :], hT[:, k, b0:b0 + bn], k == 0, k == 7, ["hT", "par"], [bkey("ps", 4)])
            act(rstd[:, b0:b0 + bn], ps[4][:, 0:bn], AF.Sqrt, [bkey("ps", 4), "par"], ["rstd"], bias=e6[:, 0:1],
                scale=1.0 / D)
            recip(rstd[:, b0:b0 + bn], rstd[:, b0:b0 + bn], ["rstd"], ["rstd"])

    def modulate(gsv, shoff):
        for k in range(8):
            for (c0, c1, mc) in ((0, T, 0), (T, TT, 1)):
                tt("dve", acc0[:, c0:c1], xT[:, k, c0:c1], rstd[:, c0:c1], ALU.mult, [XK[k], "rstd"], ["acc0"])
                act(hT[:, k, c0:c1], acc0[:, c0:c1], AF.Identity, ["acc0", "modv"], ["hT"],
                    bias=modv[:, shoff + k, mc:mc + 1], scale=gsv[:, k, mc:mc + 1])

    def mod_vectors(l):
        for c in range(16):
            wb, wk = wload(w_mod_d[l], 8, c * 384, 384)
            for mi in range(3):
                m = c * 3 + mi
                for k in range(8):
                    mm(ps[5][:, 2 * m:2 * m + 2], wb[:, k, mi * 128:(mi + 1) * 128], csil[:, k, :], k == 0, k == 7,
                       [wk, "csil"], [bkey("ps", 5)])
        for c2 in range(2):
            tt("dve", modv[:, :, c2], ps[5][:, c2:96:2], bmod[:, l, :], ALU.add, [bkey("ps", 5), "par"], ["modv"])
        for (gsv, ng, off) in ((gs1, n1g, 8), (gs2, n2g, 32)):
            for c2 in range(2):
                stt(gsv[:, :, c2], modv[:, off:off + 8, c2], 1.0, ng[:, l, :], ALU.add, ALU.mult,
                    ["modv", "par"], ["modv"])

    def s5_tables(l):
        barrier(["rstd", "tabB", "tabC", "W1", "W2", "G1", "G2", "ET"])
        for d in S5_DIRS:
            lre, lim, ls, bre, bim = ET[0], ET[1], ET[2], ET[3], ET[4]
            for i, tdst in enumerate((lre, lim, ls)):
                ld(tdst[:], E_l_d[l, d, i], w=["ET"])
            for i, tdst in enumerate((bre, bim)):
                ld(tdst[:], E_b_d[l, d, i], w=["ET"])
            r_, cth, sth, dt_, th_, tmp = ET[5], ET[6], ET[7], ET[8], ET[9], ET[10]
            disc(lre[:], lim[:], ls[:], r_[:], cth[:], sth[:], dt_[:], th_[:], tmp[:], ["ET", "par"], ["ET"])
            act(r_[:], r_[:], AF.Exp, ["ET"], ["ET"])
            A_, Bn, den, fre, fim, t1 = ET[11], ET[12], ET[13], ET[14], ET[15], ET[8]
            tt("dve", A_[:], r_[:], cth[:], ALU.mult, ["ET"], ["ET"])
            ts("dve", A_[:], A_[:], -1.0, None, ALU.add, None, ["ET"], ["ET"])
            tt("dve", Bn[:], r_[:], sth[:], ALU.mult, ["ET"], ["ET"])
            tt("dve", den[:], lre[:], lre[:], ALU.mult, ["ET"], ["ET"])
            tt("dve", t1[:], lim[:], lim[:], ALU.mult, ["ET"], ["ET"])
            tt("dve", den[:], den[:], t1[:], ALU.add, ["ET"], ["ET"])
            recip(den[:], den[:], ["ET"], ["ET"])
            tt("dve", fre[:], A_[:], lre[:], ALU.mult, ["ET"], ["ET"])
            tt("dve", t1[:], Bn[:], lim[:], ALU.mult, ["ET"], ["ET"])
            tt("dve", fre[:], fre[:], t1[:], ALU.add, ["ET"], ["ET"])
            tt("dve", fre[:], fre[:], den[:], ALU.mult, ["ET"], ["ET"])
            tt("dve", fim[:], Bn[:], lre[:], ALU.mult, ["ET"], ["ET"])
            tt("dve", t1[:], A_[:], lim[:], ALU.mult, ["ET"], ["ET"])
            tt("dve", fim[:], fim[:], t1[:], ALU.subtract, ["ET"], ["ET"])
            tt("dve", fim[:], fim[:], den[:], ALU.mult, ["ET"], ["ET"])
            t2 = ET[9]
            tt("dve", t1[:], fre[:], bre[:], ALU.mult, ["ET"], ["ET"])
            tt("dve", t2[:], fim[:], bim[:], ALU.mult, ["ET"], ["ET"])
            tt("dve", tabB[:, d, 0, :], t1[:], t2[:], ALU.subtract, ["ET"], ["tabB"])
            tt("dve", t1[:], fre[:], bim[:], ALU.mult, ["ET"], ["ET"])
            tt("dve", t2[:], fim[:], bre[:], ALU.mult, ["ET"], ["ET"])
            tt("dve", tabB[:, d, 1, :], t1[:], t2[:], ALU.add, ["ET"], ["tabB"])
            ld(ET[0][:], P_c_d[l, d, 0], w=["ET"])
            ld(ET[1][:], P_c_d[l, d, 1], w=["ET"])
            act(tabC[:, d, 0, :], ET[0][:], AF.Copy, ["ET"], ["tabC"])
            act(tabC[:, d, 1, :], ET[1][:], AF.Copy, ["ET"], ["tabC"], scale=-1.0)
        barrier(["W1", "W2", "G1", "G2", "ET"])

    def rot(o1, o2, x1, x2, tmp, cs, sn, r, w, tkey):
        tt("pool", o1, x1, cs, ALU.mult, r, w)
        tt("dve", tmp, x2, sn, ALU.mult, r, [tkey])
        tt("pool", o1, o1, tmp, ALU.add, list(w) + [tkey], w)
        tt("pool", o2, x1, sn, ALU.mult, list(r) + list(w) + [tkey], w)
        tt("dve", tmp, x2, cs, ALU.mult, list(r) + list(w), [tkey])
        tt("pool", o2, o2, tmp, ALU.subtract, list(w) + [tkey], w)

    SEGS = ((T, TT, TC), (0, T, T))

    def s5_tile(l, d, j, phase):
        c = (l * 2 + d) * 16 + j
        dj = d * 16 + j
        q, a = j // 4, j % 4
        pcol = lambda i: Pq[:, i, c:c + 1]
        rev = d == 1

        def tab(t_, n):
            return t_[:, n - 1::-1] if rev and n > 1 else (t_[:, 0:n] if not rev else t_[:, 0:1])

        def tabn(t_, n):
            if not rev:
                return t_[:, 0:n]
            return t_[:, n - 1::-1] if n > 1 else t_[:, 0:1]

        ts("pool", SN[:], tau[:], pcol(1), MAGIC, ALU.mult, ALU.add, ["tau", "Pq"], ["SN"])
        ts("pool", SN[:], SN[:], 1.0, -MAGIC, ALU.mult, ALU.add, ["SN"], ["SN"])
        stt(SN[:], tau[:], pcol(1), SN[:], ALU.mult, ALU.subtract, ["tau", "Pq", "SN"], ["SN"])
        act(CS[:], SN[:], AF.Abs, ["SN"], ["CS"])
        act(CS[:], CS[:], AF.Sin, ["CS", "par"], ["CS"], bias=halfpi[:, 0:1], scale=-2 * PI)
        act(SN[:], SN[:], AF.Sin, ["SN"], ["SN"], scale=2 * PI)
        ts("pool", rmul[:], tau[:], 0.0, pcol(0), ALU.mult, ALU.add, ["tau", "Pq"], ["acc0"])
        for bi, (b0, bn) in enumerate(BLOCKS):
            p0 = 2 * (bi % 2)
            for ri in range(2):
                mm(ps[p0 + ri][:, 0:bn], tabB[32 * a:32 * a + 32, d, ri, q * 128:(q + 1) * 128],
                   uT[32 * a:32 * a + 32, q, b0:b0 + bn], True, True, ["tabB", "uT", "rstd"], [bkey("ps", p0 + ri)],
                   tp=(32 * a, 0))
            act(G1[:, b0:b0 + bn], ps[p0][:, 0:bn], AF.Copy, [bkey("ps", p0)], ["G1"])
            act(G2[:, b0:b0 + bn], ps[p0 + 1][:, 0:bn], AF.Copy, [bkey("ps", p0 + 1)], ["G2"])
        for (c0, c1, n) in SEGS:
            rot(W1[:, c0:c1], W2[:, c0:c1], G1[:, c0:c1], G2[:, c0:c1], HHf[:, c0:c1], tabn(CS, n), tabn(SN, n),
                ["G1", "G2", "CS", "SN"], ["W1", "W2"], "HH")
        if phase == 2:
            hr, hi_ = hst[:, 4, dj:dj + 1], hst[:, 5, dj:dj + 1]
            tt("dve", ini[:, 0:1], hr, pcol(2), ALU.mult, ["hst", "Pq"], ["ini"])
            tt("dve", ini[:, 2:3], hi_, pcol(3), ALU.mult, ["hst", "Pq"], ["ini"])
            tt("dve", ini[:, 0:1], ini[:, 0:1], ini[:, 2:3], ALU.subtract, ["ini"], ["ini"])
            tt("dve", ini[:, 1:2], hr, pcol(3), ALU.mult, ["hst", "Pq", "ini"], ["ini"])
            tt("dve", ini[:, 2:3], hi_, pcol(2), ALU.mult, ["hst", "Pq", "ini"], ["ini"])
            tt("dve", ini[:, 1:2], ini[:, 1:2], ini[:, 2:3], ALU.add, ["ini"], ["ini"])
            ts("dve", ini[:, 1:2], ini[:, 1:2], -1.0, None, ALU.mult, None, ["ini"], ["ini"])
        for (c0, c1, n) in SEGS:
            for gi, (Gx, Wx, gk, wk) in enumerate(((G1, W1, "G1", "W1"), (G2, W2, "G2", "W2"))):
                if phase == 2 and c0 == 0:
                    init = ini[:, gi:gi + 1]
                else:
                    init = 0.0
                if not rev:
                    scan(Gx[:, c0:c1], rmul[:, 0:n], Wx[:, c0:c1], init, [wk, "acc0", "ini"], [gk])
                else:
                    lo = c0 - 1 if c0 > 0 else None
                    scan(Gx[:, c1 - 1:lo:-1], rmul[:, 0:n], Wx[:, c1 - 1:lo:-1], init, [wk, "acc0", "ini"], [gk])
        if phase == 1:
            for (c0, c1, n), row in zip(SEGS, (0, 2)):
                ecol = c1 - 1 if not rev else c0
                tcol = n - 1
                g1, g2 = G1[:, ecol:ecol + 1], G2[:, ecol:ecol + 1]
                cs_, sn_ = CS[:, tcol:tcol + 1], SN[:, tcol:tcol + 1]
                hr, hi_ = hst[:, row, dj:dj + 1], hst[:, row + 1, dj:dj + 1]
                tt("dve", hr, g1, cs_, ALU.mult, ["G1", "CS"], ["hst"])
                tt("dve", ini[:, 3:4], g2, sn_, ALU.mult, ["G2", "SN"], ["ini"])
                tt("dve", hr, hr, ini[:, 3:4], ALU.add, ["hst", "ini"], ["hst"])
                tt("dve", hi_, g1, sn_, ALU.mult, ["G1", "SN"], ["hst"])
                tt("dve", ini[:, 3:4], g2, cs_, ALU.mult, ["G2", "CS", "hst"], ["ini"])
                tt("dve", hi_, hi_, ini[:, 3:4], ALU.subtract, ["hst", "ini"], ["hst"])
            return
        for (c0, c1, n) in SEGS:
            cs_, sn_ = tabn(CS, n), tabn(SN, n)
            tt("pool", W1[:, c0:c1], G1[:, c0:c1], cs_, ALU.mult, ["G1", "CS"], ["W1"])
            tt("dve", W2[:, c0:c1], G2[:, c0:c1], sn_, ALU.mult, ["G2", "SN"], ["W2"])
            tt("pool", hre[:, c0:c1], W1[:, c0:c1], W2[:, c0:c1], ALU.add, ["W1", "W2"], ["HH"])
            tt("pool", W1[:, c0:c1], G1[:, c0:c1], sn_, ALU.mult, ["G1", "SN", "HH"], ["W1"])
            tt("dve", W2[:, c0:c1], G2[:, c0:c1], cs_, ALU.mult, ["G2", "CS", "HH"], ["W2"])
            tt("pool", him[:, c0:c1], W1[:, c0:c1], W2[:, c0:c1], ALU.subtract, ["W1", "W2"], ["HH"])
        for bi, (b0, bn) in enumerate(BLOCKS):
            pi = 4 + bi % 2
            mm(ps[pi][32 * a:32 * a + 32, 0:bn], tabC[:, d, 0, j * 32:(j + 1) * 32], hre[:, b0:b0 + bn], True, False,
               ["tabC", "HH", "rstd"], [bkey("ps", pi)], tp=(0, 32 * a))
            mm(ps[pi][32 * a:32 * a + 32, 0:bn], tabC[:, d, 1, j * 32:(j + 1) * 32], him[:, b0:b0 + bn], False, True,
               ["tabC", "HH", "rstd"], [bkey("ps", pi)], tp=(0, 32 * a))
            ydst = yacc[32 * a:32 * a + 32, q, b0:b0 + bn]
            if d == S5_DIRS[0]:
                cp("dve", ydst, ps[pi][32 * a:32 * a + 32, 0:bn], [bkey("ps", pi)], ["yacc"])
            else:
                tt("dve", ydst, ydst, ps[pi][32 * a:32 * a + 32, 0:bn], ALU.add, [bkey("ps", pi), "yacc"], ["yacc"])

    def layer_A(l):
        mod_vectors(l)
        rms_stats()
        modulate(gs1, 0)
        def ev_u(m, bi, b0, bn, p_, pk):
            act(uT[:, m, b0:b0 + bn], p_, AF.Copy, [pk], ["uT"])
        stream_proj(w_in_d[l], 8, 512, hT, ["hT"], ev_u)
        mset("pool", vbp[:, :, 0:15], 0.0, ["vbp"])
        mset("pool", vbp[:, :, 15 + T:VC0], 0.0, ["vbp"])
        mset("pool", vbp[:, :, VC0 + TC:VP], 0.0, ["vbp"])
        for m in range(8):
            wa, wka = wload(w_in_d[l], 8, 512 + m * 128, 128)
            wg, wkg = wload(w_in_d[l], 8, 1536 + m * 128, 128)
            for bi, (b0, bn) in enumerate(BLOCKS):
                pa, pg = (bi % 2) * 2, (bi % 2) * 2 + 1
                for k in range(8):
                    mm(ps[pg][:, 0:bn], wg[:, k, 0:128], hT[:, k, b0:b0 + bn], k == 0, k == 7, [wkg, "hT"], [bkey("ps", pg)])
                for k in range(8):
                    mm(ps[pa][:, 0:bn], wa[:, k, 0:128], hT[:, k, b0:b0 + bn], k == 0, k == 7, [wka, "hT"], [bkey("ps", pa)])
                act(acc0[:, 0:bn], ps[pg][:, 0:bn], AF.Sigmoid, [bkey("ps", pg)], ["acc0"])
                vc = 15 + b0 if b0 < T else VC0
                tt("dve", vbp[:, m, vc:vc + bn], ps[pa][:, 0:bn], acc0[:, 0:bn], ALU.mult, [bkey("ps", pa), "acc0"], ["vbp"])
        for k in range(8):
            ld(xsp_o[:, k, :], xT[:, k, :], r=[XK[k]], w=["xsp"])
        ld(hsp_o[:, :, :], hT[:, :, :], r=["hT"], w=["hsp"])
        ld(u_o[:, :, :], uT[:, :, :], r=["uT"], w=["uo"])
        barrier(XK + S5K + ["hT", "yacc"])
        ld(tau[:], tau_d[0:1, :].partition_broadcast(128), w=["tau"])
        s5_tables(l)
        for j in range(16):
            for d in S5_DIRS:
                s5_tile(l, d, j, 1)
        barrier(["acc0", "pay"])
        mset("dve", pay[:], 0.0, ["pay"])
        cp("dve", pay[:, 0:32], hst[:, 2, :], ["hst"], ["pay"])
        cp("dve", pay[:, 32:64], hst[:, 3, :], ["hst"], ["pay"])
        cp("dve", pay[:, 304:336], hst[:, 0, :], ["hst"], ["pay"])
        cp("dve", pay[:, 336:368], hst[:, 1, :], ["hst"], ["pay"])
        pe_ = pay[:, 64:304].rearrange("p (q s f) -> p q s f", q=8, s=2)
        cp("dve", pe_[:, :, 0, :], vbp[:, :, 15:30], ["vbp"], ["pay"])
        cp("dve", pe_[:, :, 1, :], vbp[:, :, 15 + T - 15:15 + T], ["vbp"], ["pay"])
        ld(pay_o[:, :], pay[:], r=["pay"], w=["payo"])
        ld(vbp_o[:, :, :], vbp[:, :, :], r=["vbp"], w=["vo"])
        P.op("sync", None, r=["xsp", "hsp", "uo", "payo", "vo"], w=["doneA"])

    def layer_B(l, x_src):
        mod_vectors(l)
        ld(uT[:, :, :], uin_d[:, :, :], w=["uT"])
        ld(vbp[:, :, :], vin_d[:, :, :], w=["vbp"])
        P.dma("sync", ccs[:, :, :], ccs_d[:, :, :], w=WK + ["ccs"])
        ld(tau[:], tau_d[0:1, :].partition_broadcast(128), w=["tau"])
        HR, HI, T1, T2 = hst[:, 6, 0:16], hst[:, 7, 0:16], hst[:, 8, 0:16], hst[:, 9, 0:16]
        for d in S5_DIRS:
            c0 = (l * 2 + d) * 16
            LR, LI = Pq[:, 4, c0:c0 + 16], Pq[:, 5, c0:c0 + 16]
            SR, SI = hst[:, 4, d * 16:(d + 1) * 16], hst[:, 5, d * 16:(d + 1) * 16]
            cp("dve", HR, ccs[:, 0, 304 + d * 16:304 + d * 16 + 16], ["ccs"], ["hst"])
            cp("dve", HI, ccs[:, 0, 336 + d * 16:336 + d * 16 + 16], ["ccs"], ["hst"])
            order = list(range(NCORES)) if d == 0 else list(range(NCORES - 1, -1, -1))
            for n_, ci in enumerate(order):
                sc_ = sel[:, ci:ci + 1]
                if n_ == 0:
                    ts("dve", SR, HR, sc_, None, ALU.mult, None, ["hst", "par"], ["hst"])
                    ts("dve", SI, HI, sc_, None, ALU.mult, None, ["hst", "par"], ["hst"])
                else:
                    stt(SR, HR, sc_, SR, ALU.mult, ALU.add, ["hst", "par"], ["hst"])
                    stt(SI, HI, sc_, SI, ALU.mult, ALU.add, ["hst", "par"], ["hst"])
                if n_ == NCORES - 1:
                    break
                lr_, li_ = ccs[:, ci, d * 16:d * 16 + 16], ccs[:, ci, 32 + d * 16:32 + d * 16 + 16]
                tt("dve", T1, LR, HR, ALU.mult, ["hst", "Pq"], ["hst"])
                tt("dve", T2, LI, HI, ALU.mult, ["hst", "Pq"], ["hst"])
                tt("dve", T1, T1, T2, ALU.subtract, ["hst"], ["hst"])
                tt("dve", T2, LI, HR, ALU.mult, ["hst", "Pq"], ["hst"])
                tt("dve", HR, T1, lr_, ALU.add, ["hst", "ccs"], ["hst"])
                tt("dve", T1, LR, HI, ALU.mult, ["hst", "Pq"], ["hst"])
                tt("dve", T1, T1, T2, ALU.add, ["hst"], ["hst"])
                tt("dve", HI, T1, li_, ALU.add, ["hst", "ccs"], ["hst"])
        barrier(["acc0", "hal"])
        for side, so in ((0, 16), (1, 8)):
            hs = hal[:, side, :, :]
            for ci in range(NCORES):
                e_ = ccs[:, ci, 64:304].rearrange("p (q s f) -> p q s f", q=8, s=2)[:, :, 1 - side, :]
                sc_ = sel[:, 8 + side * 8 + ci: 8 + side * 8 + ci + 1]
                if ci == 0:
                    ts("dve", hs, e_, sc_, None, ALU.mult, None, ["ccs", "par"], ["hal"])
                else:
                    stt(hs, e_, sc_, hs, ALU.mult, ALU.add, ["ccs", "par", "hal"], ["hal"])
        cp("dve", vbp[:, :, 0:15], hal[:, 0, :, :], ["hal"], ["vbp"])
        cp("dve", vbp[:, :, 15 + T:15 + T + 15], hal[:, 1, :, :], ["hal"], ["vbp"])
        barrier(S5K + ["yacc", "hT", "hal", "acc0", "pay"] + XK)
        s5_tables(l)
        for j in range(16):
            for d in S5_DIRS:
                s5_tile(l, d, j, 2)
        barrier(S5K + ["ya", "ya2", "cvo", "acc0", "gt0", "tabB", "tabC", "rstd"] + XK)
        gtmp = sb("gtmp", [TT], F32, O_A + 36864)
        for q in range(4):
            stt(gt0[:], uT[:, q, :], ssmd[:, l, q:q + 1], yacc[:, q, :], ALU.mult, ALU.add, ["uT", "par", "yacc"], ["gt0"])
            act(gtmp[:], gt0[:], AF.Square, ["gt0"], ["cvo"])
            ts("pool", gtmp[:], gtmp[:], 0.044715, 1.0, ALU.mult, ALU.add, ["cvo"], ["cvo"])
            tt("pool", gtmp[:], gtmp[:], gt0[:], ALU.mult, ["cvo", "gt0"], ["cvo"])
            act(gtmp[:], gtmp[:], AF.Sigmoid, ["cvo"], ["cvo"], scale=1.5957691216057308)
            tt("dve", ya[:, q, :], gt0[:], gtmp[:], ALU.mult, ["gt0", "cvo"], ["ya"])

        def ev_glu(m, bi, b0, bn, p_, pk):
            act(gt0[:, 0:bn], p_, AF.Sigmoid, [pk], ["gt0"])
            tt("dve", ya2[:, m, b0:b0 + bn], ya[:, m, b0:b0 + bn], gt0[:, 0:bn], ALU.mult, ["ya", "gt0"], ["ya2"])
        stream_proj(w_glu_d[l], 4, 512, ya, ["ya"], ev_glu)
        barrier(["uT", "dg", "gt1", "sqt"])
        for q in range(8):
            for jt in range(31):
                ts("pool", dg[:, jt, :], ident[:, :], dwk[:, l, q, jt:jt + 1], 0.0, ALU.mult, ALU.add, ["par"], ["dg"])
            for (o0, v0, n) in ((0, 0, 512), (512, 512, 512), (1024, 1024, 512), (1536, 1536, 512), (T, VC0 - 15, TC)):
                pi = pscnt[0] % 4
                pscnt[0] += 1
                for jt in range(31):
                    mm(ps[pi][:, 0:n], dg[:, jt, :], vbp[:, q, v0 + jt:v0 + jt + n], jt == 0, jt == 30, ["dg", "vbp"],
                       [bkey("ps", pi)])
                act(cvo[:, q, o0:o0 + n], ps[pi][:, 0:n], AF.Identity, [bkey("ps", pi), "par"], ["cvo"],
                    bias=dwb[:, l, q:q + 1])
        for bi, (b0, bn) in enumerate(BLOCKS):
            for q in range(8):
                act(sqt[:, 0:bn], cvo[:, q, b0:b0 + bn], AF.Square, ["cvo"], ["sqt"])
                mm(ps[4][:, 0:bn], ones_bf[:, :], cvo[:, q, b0:b0 + bn], q == 0, q == 7, ["cvo", "par"], [bkey("ps", 4)])
                mm(ps[5][:, 0:bn], ones_bf[:, :], sqt[:, 0:bn], q == 0, q == 7, ["sqt", "par"], [bkey("ps", 5)])
            mean, rs_ = gt0[:, 0:bn], gt0[:, 512:512 + bn]
            ts("dve", mean, ps[4][:, 0:bn], 1.0 / D, None, ALU.mult, None, [bkey("ps", 4)], ["gt0"])
            tt("dve", rs_, mean, mean, ALU.mult, ["gt0"], ["gt0"])
            stt(rs_, ps[5][:, 0:bn], 1.0 / D, rs_, ALU.mult, ALU.subtract, [bkey("ps", 5), "gt0"], ["gt0"])
            act(rs_, rs_, AF.Sqrt, ["gt0", "par"], ["gt0"], bias=e6[:, 0:1])
            recip(rs_, rs_, ["gt0"], ["gt0"])
            for q in range(8):
                t_ = gt0[:, 1024:1024 + bn]
                tt("dve", t_, cvo[:, q, b0:b0 + bn], mean, ALU.subtract, ["cvo", "gt0"], ["gt0"])
                tt("dve", t_, t_, rs_, ALU.mult, ["gt0"], ["gt0"])
                act(cvo[:, q, b0:b0 + bn], t_, AF.Silu, ["gt0", "par"], ["cvo"], bias=cnb[:, l, q:q + 1],
                    scale=cng[:, l, q:q + 1])
        barrier(["yacc", "hT", "mrg", "vbp"])
        ld(hT[:, :, :], hin_d[:, :, :], w=["hT"])
        for m in range(8):
            wga, kga = wload(w_in_d[l], 8, 2560 + m * 128, 128)
            wgb, kgb = wload(w_in_d[l], 8, 3584 + m * 128, 128)
            for bi, (b0, bn) in enumerate(BLOCKS):
                for k in range(8):
                    mm(ps[0][:, 0:bn], wga[:, k, 0:128], hT[:, k, b0:b0 + bn], k == 0, k == 7, [kga, "hT"], [bkey("ps", 0)])
                for k in range(8):
                    mm(ps[1][:, 0:bn], wgb[:, k, 0:128], hT[:, k, b0:b0 + bn], k == 0, k == 7, [kgb, "hT"], [bkey("ps", 1)])
                act(gt0[:, b0:b0 + bn], ps[0][:, 0:bn], AF.Sigmoid, [bkey("ps", 0)], ["gt0"])
                act(gt1[:, b0:b0 + bn], ps[1][:, 0:bn], AF.Sigmoid, [bkey("ps", 1)], ["gt1"])
            wpa, kpa = wload(w_pa_d[l], 4, m * 128, 128)
            wpb, kpb = wload(w_pb_d[l], 8, m * 128, 128)
            for bi, (b0, bn) in enumerate(BLOCKS):
                for k in range(4):
                    mm(ps[2][:, 0:bn], wpa[:, k, 0:128], ya2[:, k, b0:b0 + bn], k == 0, k == 3, [kpa, "ya2"], [bkey("ps", 2)])
                for k in range(8):
                    mm(ps[3][:, 0:bn], wpb[:, k, 0:128], cvo[:, k, b0:b0 + bn], k == 0, k == 7, [kpb, "cvo"], [bkey("ps", 3)])
                tt("dve", gt0[:, b0:b0 + bn], gt0[:, b0:b0 + bn], ps[2][:, 0:bn], ALU.mult, ["gt0", bkey("ps", 2)], ["gt0"])
                tt("dve", gt1[:, b0:b0 + bn], gt1[:, b0:b0 + bn], ps[3][:, 0:bn], ALU.mult, ["gt1", bkey("ps", 3)], ["gt1"])
                tt("pool", mrg[:, m, b0:b0 + bn], gt0[:, b0:b0 + bn], gt1[:, b0:b0 + bn], ALU.add, ["gt0", "gt1"], ["mrg"])
        barrier(["ya", "ya2", "cvo"] + XK)
        for k in range(8):
            ld(xT[:, k, :], x_src[:, k, :], w=[XK[k]])

        def ev_out(m, bi, b0, bn, p_, pk):
            mc = 1 if b0 >= T else 0
            stt(xT[:, m, b0:b0 + bn], p_, modv[:, 16 + m, mc:mc + 1], xT[:, m, b0:b0 + bn], ALU.mult, ALU.add,
                [pk, "modv", XK[m]], [XK[m]])
        stream_proj(w_out_d[l], 8, D, mrg, ["mrg"], ev_out)
        barrier(["acc0", "gt0", "rstd", "tabB", "tabC", "hT", "uT", "gt1", "dg", "sqt", "mrg", "vbp"] + ACTK + WDK)
        rms_stats()
        modulate(gs2, 24)
        hb = [(0, 512), (512, 256)]
        for part in range(3):
            h0 = part * HALF
            hTh = hT[:, :, h0:h0 + HALF]

            def ev_gate(m, bi, b0, bn, p_, pk):
                act(actT[:, m, b0:b0 + bn], p_, AF.Silu, [pk], [bkey("actT", m)])

            def ev_up(m, bi, b0, bn, p_, pk):
                tt("dve", actT[:, m, b0:b0 + bn], actT[:, m, b0:b0 + bn], p_, ALU.mult, [pk, bkey("actT", m)],
                   [bkey("actT", m)])
            stream_proj(w_fg_d[l], 8, FFN, hTh, ["hT"], ev_gate, blocks=hb)
            stream_proj(w_fu_d[l], 8, FFN, hTh, ["hT"], ev_up, blocks=hb)
            for m in range(8):
                wb, wk = wload(w_fd_d[l], 22, m * 128, 128, wbufs=Wd, wkey="Wd")
                for bi, (b0, bn) in enumerate(hb):
                    pi = pscnt[0] % 4
                    pscnt[0] += 1
                    for k in range(22):
                        mm(ps[pi][:, 0:bn], wb[:, k, :], actT[:, k, b0:b0 + bn], k == 0, k == 21,
                           [wk, bkey("actT", k)], [bkey("ps", pi)])
                    t0 = h0 + b0
                    segs = [(t0, T, 0), (T, t0 + bn, 1)] if t0 < T < t0 + bn else [(t0, t0 + bn, 0 if t0 + bn <= T else 1)]
                    for (s0, s1, mc) in segs:
                        stt(xT[:, m, s0:s1], ps[pi][:, s0 - t0:s1 - t0], modv[:, 40 + m, mc:mc + 1], xT[:, m, s0:s1],
                            ALU.mult, ALU.add, [bkey("ps", pi), "modv", XK[m]], [XK[m]])
        barrier(["hT", "uT", "vbp", "mrg", "acc0", "gt0", "gt1", "dg", "sqt"] + ACTK + WDK)

    if stage == "A0":
        layer_A(0)
    elif stage == "B0A1":
        layer_B(0, xin_d)
        layer_A(1)
    else:
        layer_B(1, xin_d)
        rms_stats()
        for k in range(8):
            tt("dve", acc0[:, 0:T], xT[:, k, 0:T], rstd[:, 0:T], ALU.mult, [XK[k], "rstd"], ["acc0"])
            ts("dve", acc0[:, 0:T], acc0[:, 0:T], nfg[:, k:k + 1], None, ALU.mult, None, ["acc0", "par"], ["acc0"])
            P.dma("sync", out_d[:, k, :], acc0[:, 0:T], r=["acc0"], w=["out"])
        P.op("sync", None, r=["out"], w=["done"])
    P.emit()
    return nc


def _tT(v):
    return np.ascontiguousarray(v.reshape(-1, 128).T)


def prep_inputs(inp):
    f = np.float32
    x = np.asarray(inp["x"], f)[0]
    ctx = np.asarray(inp["ctx"], f)[0]
    common = {}
    cT = np.ascontiguousarray(ctx.T.reshape(8, 128, TC).transpose(1, 0, 2))
    cv = np.stack([np.asarray(inp["c"], f)[0], np.asarray(inp["c_ctx"], f)], axis=-1)
    common["cvec"] = np.ascontiguousarray(cv.reshape(8, 128, 2).transpose(1, 0, 2))
    q = 256
    om = (1.0 / (np.float32(10000.0) ** (np.arange(q, dtype=f) / f(q)))).astype(f)
    omega = np.ascontiguousarray(om.reshape(2, 128).T)
    cidx = (np.arange(T) % 64).astype(f)[None]
    common["tau"] = np.arange(T).astype(f)[None]
    common["ident"] = np.eye(128, dtype=f)
    for k in ("w_mod", "w_in", "w_glu", "w_proj_a", "w_proj_b", "w_out", "w_ffn_gate", "w_ffn_up", "w_ffn_down"):
        common[k] = np.ascontiguousarray(np.asarray(inp[k], f))

    def pl(v):
        v = np.asarray(v, f)
        return np.ascontiguousarray(v.reshape(DEPTH, -1, 128).transpose(2, 0, 1))

    common["bmodT"] = pl(inp["b_mod"])
    common["n1gT"] = pl(inp["norm1_g"])
    common["n2gT"] = pl(inp["norm2_g"])
    common["ssmdT"] = pl(inp["ssm_d"])
    common["dwbT"] = pl(inp["dw_bias"])
    common["cngT"] = pl(inp["conv_norm_g"])
    common["cnbT"] = pl(inp["conv_norm_b"])
    common["nfgT"] = _tT(np.asarray(inp["norm_f_g"], f))
    dk = np.asarray(inp["dw_kernel"], f)
    common["dwkT"] = np.ascontiguousarray(dk.reshape(DEPTH, 31, 8, 128).transpose(3, 0, 2, 1))
    lre = np.asarray(inp["ssm_lam_re"], f); lim = np.asarray(inp["ssm_lam_im"], f)
    ls = np.asarray(inp["ssm_log_step"], f)
    bre = np.asarray(inp["ssm_b_re"], f); bim = np.asarray(inp["ssm_b_im"], f)
    cre = np.asarray(inp["ssm_c_re"], f); cim = np.asarray(inp["ssm_c_im"], f)
    lsx = np.ascontiguousarray(np.broadcast_to(ls[..., None], lre.shape))

    def pview(a):
        return a.reshape(DEPTH, 2, 16, 128).transpose(3, 0, 1, 2).reshape(128, -1)
    common["P_l"] = np.ascontiguousarray(np.stack([pview(lre), pview(lim), pview(lsx)], axis=1))

    def eview(a):
        outa = np.zeros((DEPTH, 2, 128, 4, 128), f)
        for gi in range(8):
            for qq in range(4):
                for gl in range(2):
                    g = 8 * qq + 2 * (gi // 2) + gl
                    outa[:, :, gi * 16:(gi + 1) * 16, qq, gl * 64:(gl + 1) * 64] = a[:, :, g][:, :, None, :]
        return outa.reshape(DEPTH, 2, 128, 512)
    common["E_l"] = np.ascontiguousarray(np.stack([eview(lre), eview(lim), eview(lsx)], axis=2))

    def ebuild(b):
        outa = np.zeros((DEPTH, 2, 128, 4, 128), f)
        for gi in range(8):
            for qq in range(4):
                gl = gi % 2
                outa[:, :, gi * 16:(gi + 1) * 16, qq, gl * 64:(gl + 1) * 64] = b[:, :, 8 * qq + gi].transpose(0, 1, 3, 2)
        return outa.reshape(DEPTH, 2, 128, 512)
    common["E_b"] = np.ascontiguousarray(np.stack([ebuild(bre), ebuild(bim)], axis=2))

    def cbuild(cc):
        outa = np.zeros((DEPTH, 2, 128, 16, 32), f)
        for j in range(16):
            for gl in range(2):
                outa[:, :, gl * 64:(gl + 1) * 64, j, gl * 16:(gl + 1) * 16] = cc[:, :, 2 * j + gl].transpose(0, 1, 3, 2)
        return outa.reshape(DEPTH, 2, 128, 512)
    common["P_c"] = np.ascontiguousarray(np.stack([cbuild(cre), cbuild(cim)], axis=2))
    maps = []
    for ci in range(NCORES):
        m = dict(common)
        s = np.zeros((128, 24), f)
        s[:, ci] = 1.0
        if ci > 0:
            s[:, 8 + ci - 1] = 1.0
        if ci < NCORES - 1:
            s[:, 16 + ci + 1] = 1.0
        m["sel"] = s
        maps.append(m)
    first = []
    for ci in range(NCORES):
        xs = x[ci * T:(ci + 1) * T]
        first.append({
            "xT": np.ascontiguousarray(xs.T.reshape(8, 128, T).transpose(1, 0, 2)),
            "cT": cT, "omega": omega, "cidx": cidx,
            "ridx": ((ci * T + np.arange(T)) // 64).astype(f)[None],
        })
    return maps, first


_NC = {}
_DBG = None


def _prog(stage):
    if stage not in _NC:
        _NC[stage] = build(stage)
    return _NC[stage]


def _handover(res):
    pays = np.stack([np.asarray(res.results[ci]["pay_o"]) for ci in range(NCORES)], axis=1)
    nxt = []
    for ci in range(NCORES):
        r = res.results[ci]
        nxt.append({"xsp_i": r["xsp_o"], "hsp_i": r["hsp_o"], "u_i": r["u_o"], "vbp_i": r["vbp_o"],
                    "ccs_i": np.ascontiguousarray(pays)})
    return nxt


def kernel(**inputs):
    maps, first = prep_inputs(inputs)
    cores = list(range(NCORES))
    res = run_bass_kernel_spmd(_prog("A0"), [dict(maps[c], **first[c]) for c in cores], core_ids=cores)
    nxt = _handover(res)
    if _DBG is not None:
        _DBG["h1"] = nxt
    res = run_bass_kernel_spmd(_prog("B0A1"), [dict(maps[c], **nxt[c]) for c in cores], core_ids=cores)
    nxt = _handover(res)
    if _DBG is not None:
        _DBG["h2"] = nxt
    res = run_bass_kernel_spmd(_prog("B1"), [dict(maps[c], **nxt[c]) for c in cores], core_ids=cores)
    outs = []
    for ci in cores:
        o = np.asarray(res.results[ci]["outT"])
        outs.append(o.transpose(2, 1, 0).reshape(T, D))
    return np.concatenate(outs, axis=0)[None].astype(np.float32)
```

```python
import math
import numpy as np
import concourse.bass as bass
import concourse.mybir as mybir
from concourse.bass_utils import run_bass_kernel_spmd

F32 = mybir.dt.float32
BF16 = mybir.dt.bfloat16
ALU = mybir.AluOpType
AF = mybir.ActivationFunctionType

NCORES = 8
D = 1024
T = 2048
TC = 256
TT = T + TC
DEPTH = 2
FFN = 2816
WIN = 4608
PI = math.pi
MAGIC = 12582912.0
BLOCKS = [(0, 512), (512, 512), (1024, 512), (1536, 512), (2048, 256)]


class Prog:
    ENG = ("sync", "act", "dve", "pool", "pe")

    def __init__(self, nc):
        self.nc = nc
        self.ops = []
        self.lw = {}
        self.rd = {}

    def _rec(self, eng, fn, r, w, dma):
        i = len(self.ops)
        deps = {}
        for b in r:
            j = self.lw.get(b)
            if j is not None:
                deps[j] = deps.get(j, False) or True
        for b in w:
            j = self.lw.get(b)
            if j is not None:
                deps[j] = True
            for j in self.rd.get(b, ()):
                if j not in deps:
                    deps[j] = False
        for b in r:
            self.rd.setdefault(b, []).append(i)
        for b in w:
            self.lw[b] = i
            self.rd[b] = []
        keep = []
        for j, strong in deps.items():
            oj = self.ops[j]
            if oj["eng"] == eng and not oj["dma"] and not dma:
                if eng == "pe" or not strong:
                    continue
            keep.append(j)
        self.ops.append(dict(eng=eng, fn=fn, deps=keep, dma=dma, sig=None))
        return i

    def op(self, eng, fn, r=(), w=()):
        return self._rec(eng, fn, tuple(r), tuple(w), False)

    def dma(self, eng, out, in_, r=(), w=()):
        return self._rec(eng, lambda e: e.dma_start(out=out, in_=in_), tuple(r), tuple(w), True)

    def emit(self, ndma_sems=64):
        nc = self.nc
        waited_on = set()
        for o in self.ops:
            for j in o["deps"]:
                waited_on.add(j)
        esem = {e: nc.alloc_semaphore("s_" + e) for e in self.ENG}
        dsem = [nc.alloc_semaphore("d%d" % k) for k in range(ndma_sems)]
        ecnt = {e: 0 for e in self.ENG}
        dcnt = [0] * ndma_sems
        nd = 0
        for i, o in enumerate(self.ops):
            if (i in waited_on or o["dma"]) and o["fn"] is not None:
                if o["dma"]:
                    k = nd % ndma_sems
                    nd += 1
                    dcnt[k] += 16
                    o["sig"] = (("d", k), dsem[k], dcnt[k], 16)
                else:
                    ecnt[o["eng"]] += 1
                    o["sig"] = (("e", o["eng"]), esem[o["eng"]], ecnt[o["eng"]], 1)
        per = {e: [] for e in self.ENG}
        for o in self.ops:
            per[o["eng"]].append(o)
        ops = self.ops

        def run(eng_name, e):
            waited = {}
            for o in per[eng_name]:
                for j in sorted(o["deps"]):
                    s = ops[j]["sig"]
                    if s is None:
                        continue
                    ch, sem, val, _ = s
                    if waited.get(ch, 0) >= val:
                        continue
                    waited[ch] = val
                    e.wait_ge(sem, val)
                if o["fn"] is None:
                    continue
                if o["dma"] and o["sig"][2] > 16:
                    ch = o["sig"][0]
                    if waited.get(ch, 0) < o["sig"][2] - 16:
                        waited[ch] = o["sig"][2] - 16
                        e.wait_ge(o["sig"][1], o["sig"][2] - 16)
                ins = o["fn"](e)
                if o["sig"] is not None:
                    ins.then_inc(o["sig"][1], o["sig"][3])

        with nc.Block() as block:
            @block.sync
            def _(e):
                run("sync", e)

            @block.scalar
            def _(e):
                run("act", e)

            @block.vector
            def _(e):
                run("dve", e)

            @block.gpsimd
            def _(e):
                run("pool", e)

            @block.tensor
            def _(e):
                run("pe", e)


NPAY = 64 + 240 + 64
VP = 15 + T + 15 + 15 + TC + 15
VC0 = 15 + T + 15 + 15
HALF = TT // 3
S5_DIRS = (0, 1)


def build(stage):
    nc = bass.Bass("TRN2", target_bir_lowering=False)
    P = Prog(nc)

    def din(name, shape, dt=F32):
        return nc.dram_tensor(name, list(shape), dt, kind="ExternalInput").ap()

    def dout(name, shape, dt=F32):
        return nc.dram_tensor(name, list(shape), dt, kind="ExternalOutput").ap()

    first = stage == "A0"
    last = stage == "B1"
    if first:
        xT_d = din("xT", [128, 8, T])
        cT_d = din("cT", [128, 8, TC])
        omega_d = din("omega", [128, 2])
        ridx_d = din("ridx", [1, T])
        cidx_d = din("cidx", [1, T])
    else:
        xin_d = din("xsp_i", [128, 8, TT])
        hin_d = din("hsp_i", [128, 8, TT], BF16)
        uin_d = din("u_i", [128, 4, TT], BF16)
        vin_d = din("vbp_i", [128, 8, VP], BF16)
        ccs_d = din("ccs_i", [128, NCORES, NPAY])
        mods_i = din("mods_i", [128, 64, 2])
    cvec_d = din("cvec", [128, 8, 2])
    tau_d = din("tau", [1, T])
    sel_d = din("sel", [128, 24])
    w_mod_d = din("w_mod", [DEPTH, D, 6 * D])
    bmod_d = din("bmodT", [128, DEPTH, 48])
    n1g_d = din("n1gT", [128, DEPTH, 8])
    n2g_d = din("n2gT", [128, DEPTH, 8])
    w_in_d = din("w_in", [DEPTH, D, WIN])
    P_l_d = din("P_l", [128, 3, 64])
    P_c_d = din("P_c", [DEPTH, 2, 2, 128, 512])
    E_l_d = din("E_l", [DEPTH, 2, 3, 128, 512])
    E_b_d = din("E_b", [DEPTH, 2, 2, 128, 512])
    ssmd_d = din("ssmdT", [128, DEPTH, 4])
    w_glu_d = din("w_glu", [DEPTH, 512, 512])
    w_pa_d = din("w_proj_a", [DEPTH, 512, D])
    dwk_d = din("dwkT", [128, DEPTH, 8, 31])
    dwb_d = din("dwbT", [128, DEPTH, 8])
    cng_d = din("cngT", [128, DEPTH, 8])
    cnb_d = din("cnbT", [128, DEPTH, 8])
    w_pb_d = din("w_proj_b", [DEPTH, D, D])
    w_out_d = din("w_out", [DEPTH, D, D])
    w_fg_d = din("w_ffn_gate", [DEPTH, D, FFN])
    w_fu_d = din("w_ffn_up", [DEPTH, D, FFN])
    w_fd_d = din("w_ffn_down", [DEPTH, FFN, D])
    nfg_d = din("nfgT", [128, 8])
    ident_d = din("ident", [128, 128])
    if last:
        out_d = dout("outT", [128, 8, T])
    else:
        xsp_o = dout("xsp_o", [128, 8, TT])
        hsp_o = dout("hsp_o", [128, 8, TT], BF16)
        u_o = dout("u_o", [128, 4, TT], BF16)
        vbp_o = dout("vbp_o", [128, 8, VP], BF16)
        pay_o = dout("pay_o", [128, NPAY])
        mods_o = dout("mods_o", [128, 64, 2])

    _n = [0]

    def sb(name, shape, dt, off):
        _n[0] += 1
        return nc.alloc_sbuf_tensor_at("%s_%d" % (name, _n[0]), [128] + list(shape), dt, offset=off)

    O_PAR, O_W, O_RSTD, O_A, O_B, O_C, O_D, O_E = 16512, 28800, 41088, 50304, 124032, 160896, 179328, 217152
    END = 229376
    po = [O_PAR]

    def par(name, shape, dt=F32):
        sz = int(np.prod(shape)) * (4 if dt == F32 else 2)
        sz = (sz + 31) // 32 * 32
        t = sb(name, shape, dt, po[0])
        po[0] += sz
        assert po[0] <= O_W, po[0]
        return t

    modv = par("modv", [48, 2])
    gs1 = par("gs1", [8, 2])
    gs2 = par("gs2", [8, 2])
    bmod = par("bmod", [DEPTH, 48])
    n1g = par("n1g", [DEPTH, 8])
    n2g = par("n2g", [DEPTH, 8])
    nfg = par("nfg", [8])
    ssmd = par("ssmd", [DEPTH, 4])
    dwk = par("dwk", [DEPTH, 8, 31])
    dwb = par("dwb", [DEPTH, 8])
    cng = par("cng", [DEPTH, 8])
    cnb = par("cnb", [DEPTH, 8])
    omega = par("omega", [2])
    sel = par("sel", [24])
    csil = par("csil", [8, 2], BF16)
    cvec = par("cvec", [8, 2])
    ones_bf = par("ones_bf", [128], BF16)
    Pl = par("Pl", [3, 64])
    Pq = par("Pq", [14, 64])
    hst = par("hst", [10, 32])
    ini = par("ini", [4])
    e6 = par("e6", [8])
    halfpi = par("halfpi", [1])
    scr = par("scr", [1])
    ident = par("ident", [128], BF16)

    Wb = [sb("Wb%d" % i, [8, 384], BF16, O_W + i * 6144) for i in range(2)]
    rstd = sb("rstd", [TT], F32, O_RSTD)
    xT = sb("xT", [8, TT], F32, O_A)
    hT = sb("hT", [8, TT], BF16, O_B)
    uT = sb("uT", [4, TT], BF16, O_C)
    vbp = sb("vbp", [8, VP], BF16, O_D)
    acc0 = sb("acc0", [TT], F32, O_E)
    acc1 = sb("acc1", [TT], F32, O_B)
    o = O_A
    tau = sb("tau", [T], F32, o); o += 8192
    SN = sb("SN", [T], F32, o); o += 8192
    CS = sb("CS", [T], F32, o); o += 8192
    W1 = sb("W1", [TT], F32, o); o += 9216
    W2 = sb("W2", [TT], F32, o); o += 9216
    G1 = sb("G1", [TT], F32, o); o += 9216
    G2 = sb("G2", [TT], F32, o); o += 9216
    HHf = sb("HHf", [TT], F32, o)
    hre = sb("hre", [TT], BF16, o); o += 4608
    him = sb("him", [TT], BF16, o); o += 4608
    assert o <= O_B
    ET = [sb("ET%d" % i, [512], F32, O_A + 24576 + i * 2048) for i in range(16)]
    yacc = sb("yacc", [4, TT], F32, O_B)
    tabB = sb("tabB", [2, 2, 512], BF16, O_RSTD)
    tabC = sb("tabC", [2, 2, 512], BF16, O_RSTD + 4096)
    rmul = sb("rmul", [T], F32, O_E)
    ccs = sb("ccs", [NCORES, NPAY], F32, O_W)
    pay = sb("pay", [NPAY], F32, O_E)
    hal = sb("hal", [2, 8, 15], F32, O_E + 2048)
    ya = sb("ya", [4, TT], BF16, O_A)
    ya2 = sb("ya2", [4, TT], BF16, O_A + 18432)
    cvo = sb("cvo", [8, TT], BF16, O_A + 36864)
    gt0 = sb("gt0", [TT], F32, O_E)
    gt1 = sb("gt1", [TT], F32, O_C)
    dg = sb("dg", [31, 128], BF16, O_C + 9216)
    sqt = sb("sqt", [512], BF16, O_C + 9216 + 7936)
    mrg = sb("mrg", [8, TT], BF16, O_D)
    actT = sb("actT", [22, HALF], BF16, O_C)
    Wd = [sb("Wd%d" % i, [22, 128], BF16, O_C + 22 * HALF * 2 + i * 5632) for i in range(2)]
    assert O_C + 22 * HALF * 2 + 2 * 5632 <= O_E

    ps = [nc.alloc_psum_tensor("ps%d" % i, [128, 512], F32) for i in range(8)]

    def bkey(name, *idx):
        return (name,) + idx

    XK = [bkey("xT", k) for k in range(8)]
    ACTK = [bkey("actT", m) for m in range(22)]
    WDK = [bkey("Wd", 0), bkey("Wd", 1)]
    S5K = ["tau", "SN", "CS", "W1", "W2", "G1", "G2", "HH"]

    def tt(eng, out, a, b, op, r, w):
        P.op(eng, lambda e: e.tensor_tensor(out=out, in0=a, in1=b, op=op), r, w)

    def ts(eng, out, a, s1, s2, op0, op1, r, w):
        if op1 is None:
            P.op(eng, lambda e: e.tensor_scalar(out=out, in0=a, scalar1=s1, scalar2=None, op0=op0), r, w)
        else:
            P.op(eng, lambda e: e.tensor_scalar(out=out, in0=a, scalar1=s1, scalar2=s2, op0=op0, op1=op1), r, w)

    def stt(out, a, s, b, op0, op1, r, w):
        P.op("dve", lambda e: e.scalar_tensor_tensor(out=out, in0=a, scalar=s, in1=b, op0=op0, op1=op1), r, w)

    def act(out, a, func, r, w, bias=None, scale=None):
        kw = {}
        if bias is not None:
            kw["bias"] = bias
        if scale is not None:
            kw["scale"] = scale
        P.op("act", lambda e: e.activation(out=out, in_=a, func=func, **kw), r, w)

    def recip(out, a, r, w):
        P.op("dve", lambda e: e.reciprocal(out=out, in_=a), r, w)

    def cp(eng, out, a, r, w):
        P.op(eng, lambda e: e.tensor_copy(out=out, in_=a), r, w)

    def mset(eng, out, val, w):
        P.op(eng, lambda e: e.memset(out, val), (), w)

    def mm(out, lhsT, rhs, start, stop, r, w, tp=None):
        if tp is None:
            P.op("pe", lambda e: e.matmul(out, lhsT=lhsT, rhs=rhs, start=start, stop=stop), r, w)
        else:
            P.op("pe", lambda e: e.matmul(out, lhsT=lhsT, rhs=rhs, start=start, stop=stop, tile_position=tp), r, w)

    def scan(out, d0, d1, init, r, w):
        P.op("dve", lambda e: e.tensor_tensor_scan(out=out, data0=d0, data1=d1, initial=init, op0=ALU.mult,
                                                   op1=ALU.add), r, w)

    def barrier(keys):
        P.op("dve", lambda e: e.memset(scr[:], 0.0), (), list(keys))

    def ld(out, in_, r=(), w=(), eng="sync"):
        P.dma(eng, out, in_, r, w)

    WK = [bkey("Wb", 0), bkey("Wb", 1)]

    plist = [(bmod, bmod_d), (n1g, n1g_d), (n2g, n2g_d), (nfg, nfg_d), (ssmd, ssmd_d), (dwk, dwk_d),
             (dwb, dwb_d), (cng, cng_d), (cnb, cnb_d), (sel, sel_d), (cvec, cvec_d)]
    if first:
        plist.append((omega, omega_d))
    for t_, d_ in plist:
        ld(t_[:], d_, w=["par"])
    ld(Pl[:], P_l_d, w=["Pl"])
    P.dma("pool", ident[:], ident_d, w=["par"])
    mset("dve", ones_bf[:], 1.0, ["par"])
    mset("dve", halfpi[:], PI / 2, ["par"])
    mset("dve", e6[:], 1e-6, ["par"])
    act(csil[:], cvec[:], AF.Silu, ["par"], ["csil"])

    def sincos(eng, s_out, c_out, u, tmp, r, w):
        ts(eng, tmp, u, MAGIC, None, ALU.add, None, r, w)
        ts(eng, tmp, tmp, -MAGIC, None, ALU.add, None, w, w)
        tt(eng, tmp, u, tmp, ALU.subtract, list(r) + list(w), w)
        act(c_out, tmp, AF.Abs, w, w)
        act(c_out, c_out, AF.Sin, list(w) + ["par"], w, bias=halfpi[:, 0:1], scale=-2 * PI)
        act(s_out, tmp, AF.Sin, w, w, scale=2 * PI)

    def disc(lre, lim, ls, r_o, cth, sth, dt_o, th_o, tmp, rk, wk):
        act(dt_o, ls, AF.Exp, rk, wk)
        tt("dve", r_o, lre, dt_o, ALU.mult, list(rk) + list(wk), wk)
        tt("dve", th_o, lim, dt_o, ALU.mult, list(rk) + list(wk), wk)
        ts("dve", th_o, th_o, 1.0 / (2 * PI), None, ALU.mult, None, wk, wk)
        sincos("dve", sth, cth, th_o, tmp, wk, wk)

    PQ = lambda i: Pq[:, i, :]
    disc(Pl[:, 0, :], Pl[:, 1, :], Pl[:, 2, :], PQ(7), PQ(2), PQ(3), PQ(6), PQ(1), PQ(8), ["Pl", "par"], ["Pq"])
    act(PQ(0), PQ(7), AF.Exp, ["Pq"], ["Pq"])
    act(PQ(9), PQ(7), AF.Exp, ["Pq"], ["Pq"], scale=float(T))
    ts("dve", PQ(10), PQ(1), float(T), None, ALU.mult, None, ["Pq"], ["Pq"])
    sincos("dve", PQ(11), PQ(12), PQ(10), PQ(8), ["Pq"], ["Pq"])
    tt("dve", PQ(4), PQ(9), PQ(12), ALU.mult, ["Pq"], ["Pq"])
    tt("dve", PQ(5), PQ(9), PQ(11), ALU.mult, ["Pq"], ["Pq"])

    if first:
        for k in range(8):
            ld(xT[:, k, 0:T], xT_d[:, k, :], w=[XK[k]])
            ld(xT[:, k, T:TT], cT_d[:, k, :], w=[XK[k]])
        for k in range(8):
            idx_d = ridx_d if k < 4 else cidx_d
            if k in (0, 4):
                ld(acc0[:, 0:T], idx_d[0:1, :].partition_broadcast(128), w=["acc0"])
            ph = (PI / 2 if (k // 2) % 2 == 1 else 0.0)
            ts("dve", acc1[:, 0:T], acc0[:, 0:T], omega[:, k % 2:k % 2 + 1], ph, ALU.mult, ALU.add,
               ["acc0", "par"], ["hT"])
            ts("dve", rstd[:, 0:T], acc1[:, 0:T], 1.0 / (2 * PI), MAGIC, ALU.mult, ALU.add, ["hT"], ["rstd"])
            ts("dve", rstd[:, 0:T], rstd[:, 0:T], -MAGIC, None, ALU.add, None, ["rstd"], ["rstd"])
            stt(acc1[:, 0:T], rstd[:, 0:T], -2 * PI, acc1[:, 0:T], ALU.mult, ALU.add, ["rstd", "hT"], ["hT"])
            act(acc1[:, 0:T], acc1[:, 0:T], AF.Sin, ["hT", "par"], ["hT"])
            tt("dve", xT[:, k, 0:T], xT[:, k, 0:T], acc1[:, 0:T], ALU.add, ["hT", XK[k]], [XK[k]])

    wslot = [0]

    def wload(Wd_ap, KT, c0, cw, wbufs=None, wkey="Wb"):
        wbufs = wbufs or Wb
        s = wslot[0] % 2
        wslot[0] += 1
        wb = wbufs[s]
        src = Wd_ap[:, c0:c0 + cw].rearrange("(k p) f -> p k f", p=128)
        P.dma("pool", wb[:, 0:KT, 0:cw], src, w=[bkey(wkey, s)])
        return wb, bkey(wkey, s)

    pscnt = [0]

    def stream_proj(Wd_ap, KT, ncols, inT, inkeys, evac, col0=0, blocks=BLOCKS, chunk=384):
        nchunk = (ncols + chunk - 1) // chunk
        for c in range(nchunk):
            cw = min(chunk, ncols - c * chunk)
            wb, wk = wload(Wd_ap, KT, col0 + c * chunk, cw)
            for mi in range(cw // 128):
                m = c * (chunk // 128) + mi
                for bi, (b0, bn) in enumerate(blocks):
                    pi = pscnt[0] % 4
                    pscnt[0] += 1
                    for k in range(KT):
                        mm(ps[pi][:, 0:bn], wb[:, k, mi * 128:(mi + 1) * 128], inT[:, k, b0:b0 + bn],
                           k == 0, k == KT - 1, [wk] + list(inkeys), [bkey("ps", pi)])
                    evac(m, bi, b0, bn, ps[pi][:, 0:bn], bkey("ps", pi))

    def rms_stats():
        for bi, (b0, bn) in enumerate(BLOCKS):
            for k in range(8):
                act(hT[:, k, b0:b0 + bn], xT[:, k, b0:b0 + bn], AF.Square, [XK[k]], ["hT"])
            for k in range(8):
                mm(ps[4][:, 0:bn], ones_bf[:, :], hT[:, k, b0:b0 + bn], k == 0, k == 7, ["hT", "par"], [bkey("ps", 4)])
            act(rstd[:, b0:b0 + bn], ps[4][:, 0:bn], AF.Sqrt, [bkey("ps", 4), "par"], ["rstd"], bias=e6[:, 0:1],
                scale=1.0 / D)
            recip(rstd[:, b0:b0 + bn], rstd[:, b0:b0 + bn], ["rstd"], ["rstd"])

    def modulate(gsv, shoff):
        for k in range(8):
            for (c0, c1, mc) in ((0, T, 0), (T, TT, 1)):
                tt("dve", acc0[:, c0:c1], xT[:, k, c0:c1], rstd[:, c0:c1], ALU.mult, [XK[k], "rstd"], ["acc0"])
                act(hT[:, k, c0:c1], acc0[:, c0:c1], AF.Identity, ["acc0", "modv"], ["hT"],
                    bias=modv[:, shoff + k, mc:mc + 1], scale=gsv[:, k, mc:mc + 1])

    def mod_vectors(l):
        for c in range(16):
            wb, wk = wload(w_mod_d[l], 8, c * 384, 384)
            for mi in range(3):
                m = c * 3 + mi
                for k in range(8):
                    mm(ps[5][:, 2 * m:2 * m + 2], wb[:, k, mi * 128:(mi + 1) * 128], csil[:, k, :], k == 0, k == 7,
                       [wk, "csil"], [bkey("ps", 5)])
        for c2 in range(2):
            tt("dve", modv[:, :, c2], ps[5][:, c2:96:2], bmod[:, l, :], ALU.add, [bkey("ps", 5), "par"], ["modv"])
        for (gsv, ng, off) in ((gs1, n1g, 8), (gs2, n2g, 32)):
            for c2 in range(2):
                stt(gsv[:, :, c2], modv[:, off:off + 8, c2], 1.0, ng[:, l, :], ALU.add, ALU.mult,
                    ["modv", "par"], ["modv"])

    def s5_tables(l):
        barrier(["rstd", "tabB", "tabC", "W1", "W2", "G1", "G2", "ET"])
        for d in S5_DIRS:
            lre, lim, ls, bre, bim = ET[0], ET[1], ET[2], ET[3], ET[4]
            for i, tdst in enumerate((lre, lim, ls)):
                ld(tdst[:], E_l_d[l, d, i], w=["ET"])
            for i, tdst in enumerate((bre, bim)):
                ld(tdst[:], E_b_d[l, d, i], w=["ET"])
            r_, cth, sth, dt_, th_, tmp = ET[5], ET[6], ET[7], ET[8], ET[9], ET[10]
            disc(lre[:], lim[:], ls[:], r_[:], cth[:], sth[:], dt_[:], th_[:], tmp[:], ["ET", "par"], ["ET"])
            act(r_[:], r_[:], AF.Exp, ["ET"], ["ET"])
            A_, Bn, den, fre, fim, t1 = ET[11], ET[12], ET[13], ET[14], ET[15], ET[8]
            tt("dve", A_[:], r_[:], cth[:], ALU.mult, ["ET"], ["ET"])
            ts("dve", A_[:], A_[:], -1.0, None, ALU.add, None, ["ET"], ["ET"])
            tt("dve", Bn[:], r_[:], sth[:], ALU.mult, ["ET"], ["ET"])
            tt("dve", den[:], lre[:], lre[:], ALU.mult, ["ET"], ["ET"])
            tt("dve", t1[:], lim[:], lim[:], ALU.mult, ["ET"], ["ET"])
            tt("dve", den[:], den[:], t1[:], ALU.add, ["ET"], ["ET"])
            recip(den[:], den[:], ["ET"], ["ET"])
            tt("dve", fre[:], A_[:], lre[:], ALU.mult, ["ET"], ["ET"])
            tt("dve", t1[:], Bn[:], lim[:], ALU.mult, ["ET"], ["ET"])
            tt("dve", fre[:], fre[:], t1[:], ALU.add, ["ET"], ["ET"])
            tt("dve", fre[:], fre[:], den[:], ALU.mult, ["ET"], ["ET"])
            tt("dve", fim[:], Bn[:], lre[:], ALU.mult, ["ET"], ["ET"])
            tt("dve", t1[:], A_[:], lim[:], ALU.mult, ["ET"], ["ET"])
            tt("dve", fim[:], fim[:], t1[:], ALU.subtract, ["ET"], ["ET"])
            tt("dve", fim[:], fim[:], den[:], ALU.mult, ["ET"], ["ET"])
            t2 = ET[9]
            tt("dve", t1[:], fre[:], bre[:], ALU.mult, ["ET"], ["ET"])
            tt("dve", t2[:], fim[:], bim[:], ALU.mult, ["ET"], ["ET"])
            tt("dve", tabB[:, d, 0, :], t1[:], t2[:], ALU.subtract, ["ET"], ["tabB"])
            tt("dve", t1[:], fre[:], bim[:], ALU.mult, ["ET"], ["ET"])
            tt("dve", t2[:], fim[:], bre[:], ALU.mult, ["ET"], ["ET"])
            tt("dve", tabB[:, d, 1, :], t1[:], t2[:], ALU.add, ["ET"], ["tabB"])
            ld(ET[0][:], P_c_d[l, d, 0], w=["ET"])
            ld(ET[1][:], P_c_d[l, d, 1], w=["ET"])
            act(tabC[:, d, 0, :], ET[0][:], AF.Copy, ["ET"], ["tabC"])
            act(tabC[:, d, 1, :], ET[1][:], AF.Copy, ["ET"], ["tabC"], scale=-1.0)
        barrier(["W1", "W2", "G1", "G2", "ET"])

    def rot(o1, o2, x1, x2, tmp, cs, sn, r, w, tkey):
        tt("pool", o1, x1, cs, ALU.mult, r, w)
        tt("dve", tmp, x2, sn, ALU.mult, r, [tkey])
        tt("dve", o1, o1, tmp, ALU.add, list(w) + [tkey], w)
        tt("pool", o2, x1, sn, ALU.mult, list(r) + list(w) + [tkey], w)
        tt("dve", tmp, x2, cs, ALU.mult, list(r) + list(w), [tkey])
        tt("pool", o2, o2, tmp, ALU.subtract, list(w) + [tkey], w)

    SEGS = ((T, TT, TC), (0, T, T))

    def s5_tile(l, d, j, phase):
        c = (l * 2 + d) * 16 + j
        dj = d * 16 + j
        q, a = j // 4, j % 4
        pcol = lambda i: Pq[:, i, c:c + 1]
        rev = d == 1

        def tab(t_, n):
            return t_[:, n - 1::-1] if rev and n > 1 else (t_[:, 0:n] if not rev else t_[:, 0:1])

        def tabn(t_, n):
            if not rev:
                return t_[:, 0:n]
            return t_[:, n - 1::-1] if n > 1 else t_[:, 0:1]

        ts("pool", SN[:], tau[:], pcol(1), MAGIC, ALU.mult, ALU.add, ["tau", "Pq"], ["SN"])
        ts("pool", SN[:], SN[:], 1.0, -MAGIC, ALU.mult, ALU.add, ["SN"], ["SN"])
        stt(SN[:], tau[:], pcol(1), SN[:], ALU.mult, ALU.subtract, ["tau", "Pq", "SN"], ["SN"])
        act(CS[:], SN[:], AF.Abs, ["SN"], ["CS"])
        act(CS[:], CS[:], AF.Sin, ["CS", "par"], ["CS"], bias=halfpi[:, 0:1], scale=-2 * PI)
        act(SN[:], SN[:], AF.Sin, ["SN"], ["SN"], scale=2 * PI)
        ts("pool", rmul[:], tau[:], 0.0, pcol(0), ALU.mult, ALU.add, ["tau", "Pq"], ["acc0"])
        for bi, (b0, bn) in enumerate(BLOCKS):
            p0 = 2 * (bi % 2)
            for ri in range(2):
                mm(ps[p0 + ri][:, 0:bn], tabB[32 * a:32 * a + 32, d, ri, q * 128:(q + 1) * 128],
                   uT[32 * a:32 * a + 32, q, b0:b0 + bn], True, True, ["tabB", "uT", "rstd"], [bkey("ps", p0 + ri)],
                   tp=(32 * a, 0))
            act(G1[:, b0:b0 + bn], ps[p0][:, 0:bn], AF.Copy, [bkey("ps", p0)], ["G1"])
            act(G2[:, b0:b0 + bn], ps[p0 + 1][:, 0:bn], AF.Copy, [bkey("ps", p0 + 1)], ["G2"])
        for (c0, c1, n) in SEGS:
            rot(W1[:, c0:c1], W2[:, c0:c1], G1[:, c0:c1], G2[:, c0:c1], HHf[:, c0:c1], tabn(CS, n), tabn(SN, n),
                ["G1", "G2", "CS", "SN"], ["W1", "W2"], "HH")
        if phase == 2:
            hr, hi_ = hst[:, 4, dj:dj + 1], hst[:, 5, dj:dj + 1]
            tt("dve", ini[:, 0:1], hr, pcol(2), ALU.mult, ["hst", "Pq"], ["ini"])
            tt("dve", ini[:, 2:3], hi_, pcol(3), ALU.mult, ["hst", "Pq"], ["ini"])
            tt("dve", ini[:, 0:1], ini[:, 0:1], ini[:, 2:3], ALU.subtract, ["ini"], ["ini"])
            tt("dve", ini[:, 1:2], hr, pcol(3), ALU.mult, ["hst", "Pq", "ini"], ["ini"])
            tt("dve", ini[:, 2:3], hi_, pcol(2), ALU.mult, ["hst", "Pq", "ini"], ["ini"])
            tt("dve", ini[:, 1:2], ini[:, 1:2], ini[:, 2:3], ALU.add, ["ini"], ["ini"])
            ts("dve", ini[:, 1:2], ini[:, 1:2], -1.0, None, ALU.mult, None, ["ini"], ["ini"])
        for (c0, c1, n) in SEGS:
            for gi, (Gx, Wx, gk, wk) in enumerate(((G1, W1, "G1", "W1"), (G2, W2, "G2", "W2"))):
                if phase == 2 and c0 == 0:
                    init = ini[:, gi:gi + 1]
                else:
                    init = 0.0
                if not rev:
                    scan(Gx[:, c0:c1], rmul[:, 0:n], Wx[:, c0:c1], init, [wk, "acc0", "ini"], [gk])
                else:
                    lo = c0 - 1 if c0 > 0 else None
                    scan(Gx[:, c1 - 1:lo:-1], rmul[:, 0:n], Wx[:, c1 - 1:lo:-1], init, [wk, "acc0", "ini"], [gk])
        if phase == 1:
            for (c0, c1, n), row in zip(SEGS, (0, 2)):
                ecol = c1 - 1 if not rev else c0
                tcol = n - 1
                g1, g2 = G1[:, ecol:ecol + 1], G2[:, ecol:ecol + 1]
                cs_, sn_ = CS[:, tcol:tcol + 1], SN[:, tcol:tcol + 1]
                hr, hi_ = hst[:, row, dj:dj + 1], hst[:, row + 1, dj:dj + 1]
                tt("dve", hr, g1, cs_, ALU.mult, ["G1", "CS"], ["hst"])
                tt("dve", ini[:, 3:4], g2, sn_, ALU.mult, ["G2", "SN"], ["ini"])
                tt("dve", hr, hr, ini[:, 3:4], ALU.add, ["hst", "ini"], ["hst"])
                tt("dve", hi_, g1, sn_, ALU.mult, ["G1", "SN"], ["hst"])
                tt("dve", ini[:, 3:4], g2, cs_, ALU.mult, ["G2", "CS", "hst"], ["ini"])
                tt("dve", hi_, hi_, ini[:, 3:4], ALU.subtract, ["hst", "ini"], ["hst"])
            return
        for (c0, c1, n) in SEGS:
            cs_, sn_ = tabn(CS, n), tabn(SN, n)
            tt("pool", W1[:, c0:c1], G1[:, c0:c1], cs_, ALU.mult, ["G1", "CS"], ["W1"])
            tt("dve", W2[:, c0:c1], G2[:, c0:c1], sn_, ALU.mult, ["G2", "SN"], ["W2"])
            tt("pool", hre[:, c0:c1], W1[:, c0:c1], W2[:, c0:c1], ALU.add, ["W1", "W2"], ["HH"])
            tt("pool", W1[:, c0:c1], G1[:, c0:c1], sn_, ALU.mult, ["G1", "SN", "HH"], ["W1"])
            tt("dve", W2[:, c0:c1], G2[:, c0:c1], cs_, ALU.mult, ["G2", "CS", "HH"], ["W2"])
            tt("pool", him[:, c0:c1], W1[:, c0:c1], W2[:, c0:c1], ALU.subtract, ["W1", "W2"], ["HH"])
        for bi, (b0, bn) in enumerate(BLOCKS):
            pi = 4 + bi % 2
            mm(ps[pi][32 * a:32 * a + 32, 0:bn], tabC[:, d, 0, j * 32:(j + 1) * 32], hre[:, b0:b0 + bn], True, False,
               ["tabC", "HH", "rstd"], [bkey("ps", pi)], tp=(0, 32 * a))
            mm(ps[pi][32 * a:32 * a + 32, 0:bn], tabC[:, d, 1, j * 32:(j + 1) * 32], him[:, b0:b0 + bn], False, True,
               ["tabC", "HH", "rstd"], [bkey("ps", pi)], tp=(0, 32 * a))
            ydst = yacc[32 * a:32 * a + 32, q, b0:b0 + bn]
            if d == S5_DIRS[0]:
                cp("dve", ydst, ps[pi][32 * a:32 * a + 32, 0:bn], [bkey("ps", pi)], ["yacc"])
            else:
                tt("dve", ydst, ydst, ps[pi][32 * a:32 * a + 32, 0:bn], ALU.add, [bkey("ps", pi), "yacc"], ["yacc"])

    def layer_A(l):
        mod_vectors(l)
        rms_stats()
        modulate(gs1, 0)
        def ev_u(m, bi, b0, bn, p_, pk):
            act(uT[:, m, b0:b0 + bn], p_, AF.Copy, [pk], ["uT"])
        stream_proj(w_in_d[l], 8, 512, hT, ["hT"], ev_u)
        mset("pool", vbp[:, :, 0:15], 0.0, ["vbp"])
        mset("pool", vbp[:, :, 15 + T:VC0], 0.0, ["vbp"])
        mset("pool", vbp[:, :, VC0 + TC:VP], 0.0, ["vbp"])
        for m in range(8):
            wa, wka = wload(w_in_d[l], 8, 512 + m * 128, 128)
            wg, wkg = wload(w_in_d[l], 8, 1536 + m * 128, 128)
            for bi, (b0, bn) in enumerate(BLOCKS):
                pa, pg = (bi % 2) * 2, (bi % 2) * 2 + 1
                for k in range(8):
                    mm(ps[pg][:, 0:bn], wg[:, k, 0:128], hT[:, k, b0:b0 + bn], k == 0, k == 7, [wkg, "hT"], [bkey("ps", pg)])
                for k in range(8):
                    mm(ps[pa][:, 0:bn], wa[:, k, 0:128], hT[:, k, b0:b0 + bn], k == 0, k == 7, [wka, "hT"], [bkey("ps", pa)])
                act(acc0[:, 0:bn], ps[pg][:, 0:bn], AF.Sigmoid, [bkey("ps", pg)], ["acc0"])
                vc = 15 + b0 if b0 < T else VC0
                tt("dve", vbp[:, m, vc:vc + bn], ps[pa][:, 0:bn], acc0[:, 0:bn], ALU.mult, [bkey("ps", pa), "acc0"], ["vbp"])
        for k in range(8):
            ld(xsp_o[:, k, :], xT[:, k, :], r=[XK[k]], w=["xsp"])
        ld(hsp_o[:, :, :], hT[:, :, :], r=["hT"], w=["hsp"])
        ld(u_o[:, :, :], uT[:, :, :], r=["uT"], w=["uo"])
        barrier(XK + S5K + ["hT", "yacc"])
        ld(tau[:], tau_d[0:1, :].partition_broadcast(128), w=["tau"])
        s5_tables(l)
        for j in range(16):
            for d in S5_DIRS:
                s5_tile(l, d, j, 1)
        barrier(["acc0", "pay"])
        mset("dve", pay[:], 0.0, ["pay"])
        cp("dve", pay[:, 0:32], hst[:, 2, :], ["hst"], ["pay"])
        cp("dve", pay[:, 32:64], hst[:, 3, :], ["hst"], ["pay"])
        cp("dve", pay[:, 304:336], hst[:, 0, :], ["hst"], ["pay"])
        cp("dve", pay[:, 336:368], hst[:, 1, :], ["hst"], ["pay"])
        pe_ = pay[:, 64:304].rearrange("p (q s f) -> p q s f", q=8, s=2)
        cp("dve", pe_[:, :, 0, :], vbp[:, :, 15:30], ["vbp"], ["pay"])
        cp("dve", pe_[:, :, 1, :], vbp[:, :, 15 + T - 15:15 + T], ["vbp"], ["pay"])
        ld(pay_o[:, :], pay[:], r=["pay"], w=["payo"])
        ld(mods_o[:, 0:48, :], modv[:], r=["modv"], w=["payo"])
        ld(mods_o[:, 48:56, :], gs1[:], r=["modv"], w=["payo"])
        ld(mods_o[:, 56:64, :], gs2[:], r=["modv"], w=["payo"])
        ld(vbp_o[:, :, :], vbp[:, :, :], r=["vbp"], w=["vo"])
        P.op("sync", None, r=["xsp", "hsp", "uo", "payo", "vo"], w=["doneA"])

    def layer_B(l, x_src):
        ld(modv[:], mods_i[:, 0:48, :], w=["modv"])
        ld(gs1[:], mods_i[:, 48:56, :], w=["modv"])
        ld(gs2[:], mods_i[:, 56:64, :], w=["modv"])
        ld(uT[:, :, :], uin_d[:, :, :], w=["uT"])
        ld(vbp[:, :, :], vin_d[:, :, :], w=["vbp"])
        P.dma("sync", ccs[:, :, :], ccs_d[:, :, :], w=WK + ["ccs"])
        ld(tau[:], tau_d[0:1, :].partition_broadcast(128), w=["tau"])
        HR, HI, T1, T2 = hst[:, 6, 0:16], hst[:, 7, 0:16], hst[:, 8, 0:16], hst[:, 9, 0:16]
        for d in S5_DIRS:
            c0 = (l * 2 + d) * 16
            LR, LI = Pq[:, 4, c0:c0 + 16], Pq[:, 5, c0:c0 + 16]
            SR, SI = hst[:, 4, d * 16:(d + 1) * 16], hst[:, 5, d * 16:(d + 1) * 16]
            cp("dve", HR, ccs[:, 0, 304 + d * 16:304 + d * 16 + 16], ["ccs"], ["hst"])
            cp("dve", HI, ccs[:, 0, 336 + d * 16:336 + d * 16 + 16], ["ccs"], ["hst"])
            order = list(range(NCORES)) if d == 0 else list(range(NCORES - 1, -1, -1))
            for n_, ci in enumerate(order):
                sc_ = sel[:, ci:ci + 1]
                if n_ == 0:
                    ts("dve", SR, HR, sc_, None, ALU.mult, None, ["hst", "par"], ["hst"])
                    ts("dve", SI, HI, sc_, None, ALU.mult, None, ["hst", "par"], ["hst"])
                else:
                    stt(SR, HR, sc_, SR, ALU.mult, ALU.add, ["hst", "par"], ["hst"])
                    stt(SI, HI, sc_, SI, ALU.mult, ALU.add, ["hst", "par"], ["hst"])
                if n_ == NCORES - 1:
                    break
                lr_, li_ = ccs[:, ci, d * 16:d * 16 + 16], ccs[:, ci, 32 + d * 16:32 + d * 16 + 16]
                tt("dve", T1, LR, HR, ALU.mult, ["hst", "Pq"], ["hst"])
                tt("dve", T2, LI, HI, ALU.mult, ["hst", "Pq"], ["hst"])
                tt("dve", T1, T1, T2, ALU.subtract, ["hst"], ["hst"])
                tt("dve", T2, LI, HR, ALU.mult, ["hst", "Pq"], ["hst"])
                tt("dve", HR, T1, lr_, ALU.add, ["hst", "ccs"], ["hst"])
                tt("dve", T1, LR, HI, ALU.mult, ["hst", "Pq"], ["hst"])
                tt("dve", T1, T1, T2, ALU.add, ["hst"], ["hst"])
                tt("dve", HI, T1, li_, ALU.add, ["hst", "ccs"], ["hst"])
        barrier(["acc0", "hal"])
        for side, so in ((0, 16), (1, 8)):
            hs = hal[:, side, :, :]
            for ci in range(NCORES):
                e_ = ccs[:, ci, 64:304].rearrange("p (q s f) -> p q s f", q=8, s=2)[:, :, 1 - side, :]
                sc_ = sel[:, 8 + side * 8 + ci: 8 + side * 8 + ci + 1]
                if ci == 0:
                    ts("dve", hs, e_, sc_, None, ALU.mult, None, ["ccs", "par"], ["hal"])
                else:
                    stt(hs, e_, sc_, hs, ALU.mult, ALU.add, ["ccs", "par", "hal"], ["hal"])
        cp("dve", vbp[:, :, 0:15], hal[:, 0, :, :], ["hal"], ["vbp"])
        cp("dve", vbp[:, :, 15 + T:15 + T + 15], hal[:, 1, :, :], ["hal"], ["vbp"])
        barrier(S5K + ["yacc", "hT", "hal", "acc0", "pay"] + XK)
        s5_tables(l)
        for j in range(16):
            for d in S5_DIRS:
                s5_tile(l, d, j, 2)
        barrier(S5K + ["ya", "ya2", "cvo", "acc0", "gt0", "tabB", "tabC", "rstd"] + XK)
        gtmp = sb("gtmp", [TT], F32, O_A + 36864)
        for q in range(4):
            stt(gt0[:], uT[:, q, :], ssmd[:, l, q:q + 1], yacc[:, q, :], ALU.mult, ALU.add, ["uT", "par", "yacc"], ["gt0"])
            act(gtmp[:], gt0[:], AF.Square, ["gt0"], ["cvo"])
            ts("pool", gtmp[:], gtmp[:], 0.044715, 1.0, ALU.mult, ALU.add, ["cvo"], ["cvo"])
            tt("pool", gtmp[:], gtmp[:], gt0[:], ALU.mult, ["cvo", "gt0"], ["cvo"])
            act(gtmp[:], gtmp[:], AF.Sigmoid, ["cvo"], ["cvo"], scale=1.5957691216057308)
            tt("dve", ya[:, q, :], gt0[:], gtmp[:], ALU.mult, ["gt0", "cvo"], ["ya"])

        def ev_glu(m, bi, b0, bn, p_, pk):
            act(gt0[:, 0:bn], p_, AF.Sigmoid, [pk], ["gt0"])
            tt("dve", ya2[:, m, b0:b0 + bn], ya[:, m, b0:b0 + bn], gt0[:, 0:bn], ALU.mult, ["ya", "gt0"], ["ya2"])
        stream_proj(w_glu_d[l], 4, 512, ya, ["ya"], ev_glu)
        barrier(["uT", "dg", "gt1", "sqt"])
        for q in range(8):
            for jt in range(31):
                ts("pool", dg[:, jt, :], ident[:, :], dwk[:, l, q, jt:jt + 1], 0.0, ALU.mult, ALU.add, ["par"], ["dg"])
            for (o0, v0, n) in ((0, 0, 512), (512, 512, 512), (1024, 1024, 512), (1536, 1536, 512), (T, VC0 - 15, TC)):
                pi = pscnt[0] % 4
                pscnt[0] += 1
                for jt in range(31):
                    mm(ps[pi][:, 0:n], dg[:, jt, :], vbp[:, q, v0 + jt:v0 + jt + n], jt == 0, jt == 30, ["dg", "vbp"],
                       [bkey("ps", pi)])
                act(cvo[:, q, o0:o0 + n], ps[pi][:, 0:n], AF.Identity, [bkey("ps", pi), "par"], ["cvo"],
                    bias=dwb[:, l, q:q + 1])
        for bi, (b0, bn) in enumerate(BLOCKS):
            for q in range(8):
                act(sqt[:, 0:bn], cvo[:, q, b0:b0 + bn], AF.Square, ["cvo"], ["sqt"])
                mm(ps[4][:, 0:bn], ones_bf[:, :], cvo[:, q, b0:b0 + bn], q == 0, q == 7, ["cvo", "par"], [bkey("ps", 4)])
                mm(ps[5][:, 0:bn], ones_bf[:, :], sqt[:, 0:bn], q == 0, q == 7, ["sqt", "par"], [bkey("ps", 5)])
            mean, rs_ = gt0[:, 0:bn], gt0[:, 512:512 + bn]
            ts("dve", mean, ps[4][:, 0:bn], 1.0 / D, None, ALU.mult, None, [bkey("ps", 4)], ["gt0"])
            tt("dve", rs_, mean, mean, ALU.mult, ["gt0"], ["gt0"])
            stt(rs_, ps[5][:, 0:bn], 1.0 / D, rs_, ALU.mult, ALU.subtract, [bkey("ps", 5), "gt0"], ["gt0"])
            act(rs_, rs_, AF.Sqrt, ["gt0", "par"], ["gt0"], bias=e6[:, 0:1])
            recip(rs_, rs_, ["gt0"], ["gt0"])
            for q in range(8):
                t_ = gt0[:, 1024:1024 + bn]
                tt("dve", t_, cvo[:, q, b0:b0 + bn], mean, ALU.subtract, ["cvo", "gt0"], ["gt0"])
                tt("dve", t_, t_, rs_, ALU.mult, ["gt0"], ["gt0"])
                act(cvo[:, q, b0:b0 + bn], t_, AF.Silu, ["gt0", "par"], ["cvo"], bias=cnb[:, l, q:q + 1],
                    scale=cng[:, l, q:q + 1])
        barrier(["yacc", "hT", "mrg", "vbp"])
        ld(hT[:, :, :], hin_d[:, :, :], w=["hT"])
        for m in range(8):
            wga, kga = wload(w_in_d[l], 8, 2560 + m * 128, 128)
            wgb, kgb = wload(w_in_d[l], 8, 3584 + m * 128, 128)
            for bi, (b0, bn) in enumerate(BLOCKS):
                for k in range(8):
                    mm(ps[0][:, 0:bn], wga[:, k, 0:128], hT[:, k, b0:b0 + bn], k == 0, k == 7, [kga, "hT"], [bkey("ps", 0)])
                for k in range(8):
                    mm(ps[1][:, 0:bn], wgb[:, k, 0:128], hT[:, k, b0:b0 + bn], k == 0, k == 7, [kgb, "hT"], [bkey("ps", 1)])
                act(gt0[:, b0:b0 + bn], ps[0][:, 0:bn], AF.Sigmoid, [bkey("ps", 0)], ["gt0"])
                act(gt1[:, b0:b0 + bn], ps[1][:, 0:bn], AF.Sigmoid, [bkey("ps", 1)], ["gt1"])
            wpa, kpa = wload(w_pa_d[l], 4, m * 128, 128)
            wpb, kpb = wload(w_pb_d[l], 8, m * 128, 128)
            for bi, (b0, bn) in enumerate(BLOCKS):
                for k in range(4):
                    mm(ps[2][:, 0:bn], wpa[:, k, 0:128], ya2[:, k, b0:b0 + bn], k == 0, k == 3, [kpa, "ya2"], [bkey("ps", 2)])
                for k in range(8):
                    mm(ps[3][:, 0:bn], wpb[:, k, 0:128], cvo[:, k, b0:b0 + bn], k == 0, k == 7, [kpb, "cvo"], [bkey("ps", 3)])
                tt("dve", gt0[:, b0:b0 + bn], gt0[:, b0:b0 + bn], ps[2][:, 0:bn], ALU.mult, ["gt0", bkey("ps", 2)], ["gt0"])
                tt("dve", gt1[:, b0:b0 + bn], gt1[:, b0:b0 + bn], ps[3][:, 0:bn], ALU.mult, ["gt1", bkey("ps", 3)], ["gt1"])
                tt("pool", mrg[:, m, b0:b0 + bn], gt0[:, b0:b0 + bn], gt1[:, b0:b0 + bn], ALU.add, ["gt0", "gt1"], ["mrg"])
        barrier(["ya", "ya2", "cvo"] + XK)
        for k in range(8):
            ld(xT[:, k, :], x_src[:, k, :], w=[XK[k]])

        def ev_out(m, bi, b0, bn, p_, pk):
            mc = 1 if b0 >= T else 0
            stt(xT[:, m, b0:b0 + bn], p_, modv[:, 16 + m, mc:mc + 1], xT[:, m, b0:b0 + bn], ALU.mult, ALU.add,
                [pk, "modv", XK[m]], [XK[m]])
        stream_proj(w_out_d[l], 8, D, mrg, ["mrg"], ev_out)
        barrier(["acc0", "gt0", "rstd", "tabB", "tabC", "hT", "uT", "gt1", "dg", "sqt", "mrg", "vbp"] + ACTK + WDK)
        rms_stats()
        modulate(gs2, 24)
        hb = [(0, 512), (512, 256)]
        for part in range(3):
            h0 = part * HALF
            hTh = hT[:, :, h0:h0 + HALF]

            def ev_gate(m, bi, b0, bn, p_, pk):
                act(actT[:, m, b0:b0 + bn], p_, AF.Silu, [pk], [bkey("actT", m)])

            def ev_up(m, bi, b0, bn, p_, pk):
                tt("dve", actT[:, m, b0:b0 + bn], actT[:, m, b0:b0 + bn], p_, ALU.mult, [pk, bkey("actT", m)],
                   [bkey("actT", m)])
            stream_proj(w_fg_d[l], 8, FFN, hTh, ["hT"], ev_gate, blocks=hb)
            stream_proj(w_fu_d[l], 8, FFN, hTh, ["hT"], ev_up, blocks=hb)
            for m in range(8):
                wb, wk = wload(w_fd_d[l], 22, m * 128, 128, wbufs=Wd, wkey="Wd")
                for bi, (b0, bn) in enumerate(hb):
                    pi = pscnt[0] % 4
                    pscnt[0] += 1
                    for k in range(22):
                        mm(ps[pi][:, 0:bn], wb[:, k, :], actT[:, k, b0:b0 + bn], k == 0, k == 21,
                           [wk, bkey("actT", k)], [bkey("ps", pi)])
                    t0 = h0 + b0
                    segs = [(t0, T, 0), (T, t0 + bn, 1)] if t0 < T < t0 + bn else [(t0, t0 + bn, 0 if t0 + bn <= T else 1)]
                    for (s0, s1, mc) in segs:
                        stt(xT[:, m, s0:s1], ps[pi][:, s0 - t0:s1 - t0], modv[:, 40 + m, mc:mc + 1], xT[:, m, s0:s1],
                            ALU.mult, ALU.add, [bkey("ps", pi), "modv", XK[m]], [XK[m]])
        barrier(["hT", "uT", "vbp", "mrg", "acc0", "gt0", "gt1", "dg", "sqt"] + ACTK + WDK)

    if stage == "A0":
        layer_A(0)
    elif stage == "B0A1":
        layer_B(0, xin_d)
        layer_A(1)
    else:
        layer_B(1, xin_d)
        rms_stats()
        for k in range(8):
            tt("dve", acc0[:, 0:T], xT[:, k, 0:T], rstd[:, 0:T], ALU.mult, [XK[k], "rstd"], ["acc0"])
            ts("dve", acc0[:, 0:T], acc0[:, 0:T], nfg[:, k:k + 1], None, ALU.mult, None, ["acc0", "par"], ["acc0"])
            P.dma("sync", out_d[:, k, :], acc0[:, 0:T], r=["acc0"], w=["out"])
        P.op("sync", None, r=["out"], w=["done"])
    P.emit()
    return nc


def _tT(v):
    return np.ascontiguousarray(v.reshape(-1, 128).T)


def prep_inputs(inp):
    f = np.float32
    x = np.asarray(inp["x"], f)[0]
    ctx = np.asarray(inp["ctx"], f)[0]
    common = {}
    cT = np.ascontiguousarray(ctx.T.reshape(8, 128, TC).transpose(1, 0, 2))
    cv = np.stack([np.asarray(inp["c"], f)[0], np.asarray(inp["c_ctx"], f)], axis=-1)
    common["cvec"] = np.ascontiguousarray(cv.reshape(8, 128, 2).transpose(1, 0, 2))
    q = 256
    om = (1.0 / (np.float32(10000.0) ** (np.arange(q, dtype=f) / f(q)))).astype(f)
    omega = np.ascontiguousarray(om.reshape(2, 128).T)
    cidx = (np.arange(T) % 64).astype(f)[None]
    common["tau"] = np.arange(T).astype(f)[None]
    common["ident"] = np.eye(128, dtype=f)
    for k in ("w_mod", "w_in", "w_glu", "w_proj_a", "w_proj_b", "w_out", "w_ffn_gate", "w_ffn_up", "w_ffn_down"):
        common[k] = np.ascontiguousarray(np.asarray(inp[k], f))

    def pl(v):
        v = np.asarray(v, f)
        return np.ascontiguousarray(v.reshape(DEPTH, -1, 128).transpose(2, 0, 1))

    common["bmodT"] = pl(inp["b_mod"])
    common["n1gT"] = pl(inp["norm1_g"])
    common["n2gT"] = pl(inp["norm2_g"])
    common["ssmdT"] = pl(inp["ssm_d"])
    common["dwbT"] = pl(inp["dw_bias"])
    common["cngT"] = pl(inp["conv_norm_g"])
    common["cnbT"] = pl(inp["conv_norm_b"])
    common["nfgT"] = _tT(np.asarray(inp["norm_f_g"], f))
    dk = np.asarray(inp["dw_kernel"], f)
    common["dwkT"] = np.ascontiguousarray(dk.reshape(DEPTH, 31, 8, 128).transpose(3, 0, 2, 1))
    lre = np.asarray(inp["ssm_lam_re"], f); lim = np.asarray(inp["ssm_lam_im"], f)
    ls = np.asarray(inp["ssm_log_step"], f)
    bre = np.asarray(inp["ssm_b_re"], f); bim = np.asarray(inp["ssm_b_im"], f)
    cre = np.asarray(inp["ssm_c_re"], f); cim = np.asarray(inp["ssm_c_im"], f)
    lsx = np.ascontiguousarray(np.broadcast_to(ls[..., None], lre.shape))

    def pview(a):
        return a.reshape(DEPTH, 2, 16, 128).transpose(3, 0, 1, 2).reshape(128, -1)
    common["P_l"] = np.ascontiguousarray(np.stack([pview(lre), pview(lim), pview(lsx)], axis=1))

    def eview(a):
        outa = np.zeros((DEPTH, 2, 128, 4, 128), f)
        for gi in range(8):
            for qq in range(4):
                for gl in range(2):
                    g = 8 * qq + 2 * (gi // 2) + gl
                    outa[:, :, gi * 16:(gi + 1) * 16, qq, gl * 64:(gl + 1) * 64] = a[:, :, g][:, :, None, :]
        return outa.reshape(DEPTH, 2, 128, 512)
    common["E_l"] = np.ascontiguousarray(np.stack([eview(lre), eview(lim), eview(lsx)], axis=2))

    def ebuild(b):
        outa = np.zeros((DEPTH, 2, 128, 4, 128), f)
        for gi in range(8):
            for qq in range(4):
                gl = gi % 2
                outa[:, :, gi * 16:(gi + 1) * 16, qq, gl * 64:(gl + 1) * 64] = b[:, :, 8 * qq + gi].transpose(0, 1, 3, 2)
        return outa.reshape(DEPTH, 2, 128, 512)
    common["E_b"] = np.ascontiguousarray(np.stack([ebuild(bre), ebuild(bim)], axis=2))

    def cbuild(cc):
        outa = np.zeros((DEPTH, 2, 128, 16, 32), f)
        for j in range(16):
            for gl in range(2):
                outa[:, :, gl * 64:(gl + 1) * 64, j, gl * 16:(gl + 1) * 16] = cc[:, :, 2 * j + gl].transpose(0, 1, 3, 2)
        return outa.reshape(DEPTH, 2, 128, 512)
    common["P_c"] = np.ascontiguousarray(np.stack([cbuild(cre), cbuild(cim)], axis=2))
    maps = []
    for ci in range(NCORES):
        m = dict(common)
        s = np.zeros((128, 24), f)
        s[:, ci] = 1.0
        if ci > 0:
            s[:, 8 + ci - 1] = 1.0
        if ci < NCORES - 1:
            s[:, 16 + ci + 1] = 1.0
        m["sel"] = s
        maps.append(m)
    first = []
    for ci in range(NCORES):
        xs = x[ci * T:(ci + 1) * T]
        first.append({
            "xT": np.ascontiguousarray(xs.T.reshape(8, 128, T).transpose(1, 0, 2)),
            "cT": cT, "omega": omega, "cidx": cidx,
            "ridx": ((ci * T + np.arange(T)) // 64).astype(f)[None],
        })
    return maps, first


_NC = {}
_DBG = None


def _prog(stage):
    if stage not in _NC:
        _NC[stage] = build(stage)
    return _NC[stage]


def _handover(res):
    pays = np.stack([np.asarray(res.results[ci]["pay_o"]) for ci in range(NCORES)], axis=1)
    nxt = []
    for ci in range(NCORES):
        r = res.results[ci]
        nxt.append({"xsp_i": r["xsp_o"], "hsp_i": r["hsp_o"], "u_i": r["u_o"], "vbp_i": r["vbp_o"],
                    "mods_i": r["mods_o"], "ccs_i": np.ascontiguousarray(pays)})
    return nxt


def kernel(**inputs):
    maps, first = prep_inputs(inputs)
    cores = list(range(NCORES))
    res = run_bass_kernel_spmd(_prog("A0"), [dict(maps[c], **first[c]) for c in cores], core_ids=cores)
    nxt = _handover(res)
    if _DBG is not None:
        _DBG["h1"] = nxt
    res = run_bass_kernel_spmd(_prog("B0A1"), [dict(maps[c], **nxt[c]) for c in cores], core_ids=cores)
    nxt = _handover(res)
    if _DBG is not None:
        _DBG["h2"] = nxt
    res = run_bass_kernel_spmd(_prog("B1"), [dict(maps[c], **nxt[c]) for c in cores], core_ids=cores)
    outs = []
    for ci in cores:
        o = np.asarray(res.results[ci]["outT"])
        outs.append(o.transpose(2, 1, 0).reshape(T, D))
    return np.concatenate(outs, axis=0)[None].astype(np.float32)
```

```python
import math
import numpy as np
import concourse.bass as bass
import concourse.mybir as mybir
from concourse.bass_utils import run_bass_kernel_spmd

F32 = mybir.dt.float32
BF16 = mybir.dt.bfloat16
ALU = mybir.AluOpType
AF = mybir.ActivationFunctionType

NCORES = 8
D = 1024
T = 2048
TC = 256
TT = T + TC
DEPTH = 2
FFN = 2816
WIN = 4608
PI = math.pi
MAGIC = 12582912.0
BLOCKS = [(0, 512), (512, 512), (1024, 512), (1536, 512), (2048, 256)]


class Prog:
    ENG = ("sync", "act", "dve", "pool", "pe")

    def __init__(self, nc):
        self.nc = nc
        self.ops = []
        self.lw = {}
        self.rd = {}

    def _rec(self, eng, fn, r, w, dma):
        i = len(self.ops)
        deps = {}
        for b in r:
            j = self.lw.get(b)
            if j is not None:
                deps[j] = deps.get(j, False) or True
        for b in w:
            j = self.lw.get(b)
            if j is not None:
                deps[j] = True
            for j in self.rd.get(b, ()):
                if j not in deps:
                    deps[j] = False
        for b in r:
            self.rd.setdefault(b, []).append(i)
        for b in w:
            self.lw[b] = i
            self.rd[b] = []
        keep = []
        for j, strong in deps.items():
            oj = self.ops[j]
            if oj["eng"] == eng and not oj["dma"] and not dma:
                if eng == "pe" or not strong:
                    continue
            keep.append(j)
        self.ops.append(dict(eng=eng, fn=fn, deps=keep, dma=dma, sig=None))
        return i

    def op(self, eng, fn, r=(), w=()):
        return self._rec(eng, fn, tuple(r), tuple(w), False)

    def dma(self, eng, out, in_, r=(), w=()):
        return self._rec(eng, lambda e: e.dma_start(out=out, in_=in_), tuple(r), tuple(w), True)

    def emit(self, ndma_sems=64):
        nc = self.nc
        waited_on = set()
        for o in self.ops:
            for j in o["deps"]:
                waited_on.add(j)
        esem = {e: nc.alloc_semaphore("s_" + e) for e in self.ENG}
        dsem = [nc.alloc_semaphore("d%d" % k) for k in range(ndma_sems)]
        ecnt = {e: 0 for e in self.ENG}
        dcnt = [0] * ndma_sems
        nd = 0
        for i, o in enumerate(self.ops):
            if (i in waited_on or o["dma"]) and o["fn"] is not None:
                if o["dma"]:
                    k = nd % ndma_sems
                    nd += 1
                    dcnt[k] += 16
                    o["sig"] = (("d", k), dsem[k], dcnt[k], 16)
                else:
                    ecnt[o["eng"]] += 1
                    o["sig"] = (("e", o["eng"]), esem[o["eng"]], ecnt[o["eng"]], 1)
        per = {e: [] for e in self.ENG}
        for o in self.ops:
            per[o["eng"]].append(o)
        ops = self.ops

        def run(eng_name, e):
            waited = {}
            for o in per[eng_name]:
                for j in sorted(o["deps"]):
                    s = ops[j]["sig"]
                    if s is None:
                        continue
                    ch, sem, val, _ = s
                    if waited.get(ch, 0) >= val:
                        continue
                    waited[ch] = val
                    e.wait_ge(sem, val)
                if o["fn"] is None:
                    continue
                if o["dma"] and o["sig"][2] > 16:
                    ch = o["sig"][0]
                    if waited.get(ch, 0) < o["sig"][2] - 16:
                        waited[ch] = o["sig"][2] - 16
                        e.wait_ge(o["sig"][1], o["sig"][2] - 16)
                ins = o["fn"](e)
                if o["sig"] is not None:
                    ins.then_inc(o["sig"][1], o["sig"][3])

        with nc.Block() as block:
            @block.sync
            def _(e):
                run("sync", e)

            @block.scalar
            def _(e):
                run("act", e)

            @block.vector
            def _(e):
                run("dve", e)

            @block.gpsimd
            def _(e):
                run("pool", e)

            @block.tensor
            def _(e):
                run("pe", e)


NPAY = 64 + 240 + 64
VP = 15 + T + 15 + 15 + TC + 15
VC0 = 15 + T + 15 + 15
HALF = TT // 3
S5_DIRS = (0, 1)


def build(stage):
    nc = bass.Bass("TRN2", target_bir_lowering=False)
    P = Prog(nc)

    def din(name, shape, dt=F32):
        return nc.dram_tensor(name, list(shape), dt, kind="ExternalInput").ap()

    def dout(name, shape, dt=F32):
        return nc.dram_tensor(name, list(shape), dt, kind="ExternalOutput").ap()

    first = stage == "A0"
    last = stage == "B1"
    if first:
        xT_d = din("xT", [128, 8, T])
        cT_d = din("cT", [128, 8, TC])
        omega_d = din("omega", [128, 2])
        ridx_d = din("ridx", [1, T])
        cidx_d = din("cidx", [1, T])
    else:
        xin_d = din("xsp_i", [128, 8, TT])
        hin_d = din("hsp_i", [128, 8, TT], BF16)
        uin_d = din("u_i", [128, 4, TT], BF16)
        vin_d = din("vbp_i", [128, 8, VP], BF16)
        ccs_d = din("ccs_i", [128, NCORES, NPAY])
        mods_i = din("mods_i", [128, 64, 2])
    cvec_d = din("cvec", [128, 8, 2])
    tau_d = din("tau", [1, T])
    sel_d = din("sel", [128, 24])
    w_mod_d = din("w_mod", [DEPTH, D, 6 * D])
    bmod_d = din("bmodT", [128, DEPTH, 48])
    n1g_d = din("n1gT", [128, DEPTH, 8])
    n2g_d = din("n2gT", [128, DEPTH, 8])
    w_in_d = din("w_in", [DEPTH, D, WIN])
    P_l_d = din("P_l", [128, 3, 64])
    P_c_d = din("P_c", [DEPTH, 2, 2, 128, 512])
    E_l_d = din("E_l", [DEPTH, 2, 3, 128, 512])
    E_b_d = din("E_b", [DEPTH, 2, 2, 128, 512])
    ssmd_d = din("ssmdT", [128, DEPTH, 4])
    w_glu_d = din("w_glu", [DEPTH, 512, 512])
    w_pa_d = din("w_proj_a", [DEPTH, 512, D])
    dwk_d = din("dwkT", [128, DEPTH, 8, 31])
    dwb_d = din("dwbT", [128, DEPTH, 8])
    cng_d = din("cngT", [128, DEPTH, 8])
    cnb_d = din("cnbT", [128, DEPTH, 8])
    w_pb_d = din("w_proj_b", [DEPTH, D, D])
    w_out_d = din("w_out", [DEPTH, D, D])
    w_fg_d = din("w_ffn_gate", [DEPTH, D, FFN])
    w_fu_d = din("w_ffn_up", [DEPTH, D, FFN])
    w_fd_d = din("w_ffn_down", [DEPTH, FFN, D])
    nfg_d = din("nfgT", [128, 8])
    ident_d = din("ident", [128, 128])
    if last:
        out_d = dout("outT", [128, 8, T])
    else:
        xsp_o = dout("xsp_o", [128, 8, TT])
        hsp_o = dout("hsp_o", [128, 8, TT], BF16)
        u_o = dout("u_o", [128, 4, TT], BF16)
        vbp_o = dout("vbp_o", [128, 8, VP], BF16)
        pay_o = dout("pay_o", [128, NPAY])
        mods_o = dout("mods_o", [128, 64, 2])

    _n = [0]

    def sb(name, shape, dt, off):
        _n[0] += 1
        return nc.alloc_sbuf_tensor_at("%s_%d" % (name, _n[0]), [128] + list(shape), dt, offset=off)

    O_PAR, O_W, O_RSTD, O_A, O_B, O_C, O_D, O_E = 16512, 28800, 41088, 50304, 124032, 160896, 179328, 217152
    END = 229376
    po = [O_PAR]

    def par(name, shape, dt=F32):
        sz = int(np.prod(shape)) * (4 if dt == F32 else 2)
        sz = (sz + 31) // 32 * 32
        t = sb(name, shape, dt, po[0])
        po[0] += sz
        assert po[0] <= O_W, po[0]
        return t

    modv = par("modv", [48, 2])
    gs1 = par("gs1", [8, 2])
    gs2 = par("gs2", [8, 2])
    bmod = par("bmod", [DEPTH, 48])
    n1g = par("n1g", [DEPTH, 8])
    n2g = par("n2g", [DEPTH, 8])
    nfg = par("nfg", [8])
    ssmd = par("ssmd", [DEPTH, 4])
    dwk = par("dwk", [DEPTH, 8, 31])
    dwb = par("dwb", [DEPTH, 8])
    cng = par("cng", [DEPTH, 8])
    cnb = par("cnb", [DEPTH, 8])
    omega = par("omega", [2])
    sel = par("sel", [24])
    csil = par("csil", [8, 2], BF16)
    cvec = par("cvec", [8, 2])
    ones_bf = par("ones_bf", [128], BF16)
    Pl = par("Pl", [3, 64])
    Pq = par("Pq", [14, 64])
    hst = par("hst", [10, 32])
    ini = par("ini", [4])
    e6 = par("e6", [8])
    halfpi = par("halfpi", [1])
    scr = par("scr", [1])
    ident = par("ident", [128], BF16)

    Wb = [sb("Wb%d" % i, [8, 384], BF16, O_W + i * 6144) for i in range(2)]
    rstd = sb("rstd", [TT], F32, O_RSTD)
    xT = sb("xT", [8, TT], F32, O_A)
    hT = sb("hT", [8, TT], BF16, O_B)
    uT = sb("uT", [4, TT], BF16, O_C)
    vbp = sb("vbp", [8, VP], BF16, O_D)
    acc0 = sb("acc0", [TT], F32, O_E)
    acc1 = sb("acc1", [TT], F32, O_B)
    o = O_A
    tau = sb("tau", [T], F32, o); o += 8192
    SN = sb("SN", [T], F32, o); o += 8192
    CS = sb("CS", [T], F32, o); o += 8192
    W1 = sb("W1", [TT], F32, o); o += 9216
    W2 = sb("W2", [TT], F32, o); o += 9216
    G1 = sb("G1", [TT], F32, o); o += 9216
    G2 = sb("G2", [TT], F32, o); o += 9216
    HHf = sb("HHf", [TT], F32, o)
    hre = sb("hre", [TT], BF16, o); o += 4608
    him = sb("him", [TT], BF16, o); o += 4608
    assert o <= O_B
    ET = [sb("ET%d" % i, [512], F32, O_A + 24576 + i * 2048) for i in range(16)]
    yacc = sb("yacc", [4, TT], F32, O_B)
    SN2 = sb("SN2", [T], F32, O_B)
    CS2 = sb("CS2", [T], F32, O_B + 8192)
    tabB = sb("tabB", [2, 2, 512], BF16, O_RSTD)
    tabC = sb("tabC", [2, 2, 512], BF16, O_RSTD + 4096)
    rmul = sb("rmul", [T], F32, O_E)
    ccs = sb("ccs", [NCORES, NPAY], F32, O_W)
    pay = sb("pay", [NPAY], F32, O_E)
    hal = sb("hal", [2, 8, 15], F32, O_E + 2048)
    ya = sb("ya", [4, TT], BF16, O_A)
    ya2 = sb("ya2", [4, TT], BF16, O_A + 18432)
    cvo = sb("cvo", [8, TT], BF16, O_A + 36864)
    gt0 = sb("gt0", [TT], F32, O_E)
    gt1 = sb("gt1", [TT], F32, O_C)
    dg = sb("dg", [31, 128], BF16, O_C + 9216)
    sqt = sb("sqt", [512], BF16, O_C + 9216 + 7936)
    mrg = sb("mrg", [8, TT], BF16, O_D)
    actT = sb("actT", [22, HALF], BF16, O_C)
    Wd = [sb("Wd%d" % i, [22, 128], BF16, O_C + 22 * HALF * 2 + i * 5632) for i in range(2)]
    assert O_C + 22 * HALF * 2 + 2 * 5632 <= O_E

    ps = [nc.alloc_psum_tensor("ps%d" % i, [128, 512], F32) for i in range(8)]

    def bkey(name, *idx):
        return (name,) + idx

    XK = [bkey("xT", k) for k in range(8)]
    ACTK = [bkey("actT", m) for m in range(22)]
    WDK = [bkey("Wd", 0), bkey("Wd", 1)]
    S5K = ["tau", "SN", "CS", "SN2", "CS2", "W1", "W2", "G1", "G2", "HH"]

    def tt(eng, out, a, b, op, r, w):
        P.op(eng, lambda e: e.tensor_tensor(out=out, in0=a, in1=b, op=op), r, w)

    def ts(eng, out, a, s1, s2, op0, op1, r, w):
        if op1 is None:
            P.op(eng, lambda e: e.tensor_scalar(out=out, in0=a, scalar1=s1, scalar2=None, op0=op0), r, w)
        else:
            P.op(eng, lambda e: e.tensor_scalar(out=out, in0=a, scalar1=s1, scalar2=s2, op0=op0, op1=op1), r, w)

    def stt(out, a, s, b, op0, op1, r, w):
        P.op("dve", lambda e: e.scalar_tensor_tensor(out=out, in0=a, scalar=s, in1=b, op0=op0, op1=op1), r, w)

    def act(out, a, func, r, w, bias=None, scale=None):
        kw = {}
        if bias is not None:
            kw["bias"] = bias
        if scale is not None:
            kw["scale"] = scale
        P.op("act", lambda e: e.activation(out=out, in_=a, func=func, **kw), r, w)

    def recip(out, a, r, w):
        P.op("dve", lambda e: e.reciprocal(out=out, in_=a), r, w)

    def cp(eng, out, a, r, w):
        P.op(eng, lambda e: e.tensor_copy(out=out, in_=a), r, w)

    def mset(eng, out, val, w):
        P.op(eng, lambda e: e.memset(out, val), (), w)

    def mm(out, lhsT, rhs, start, stop, r, w, tp=None):
        if tp is None:
            P.op("pe", lambda e: e.matmul(out, lhsT=lhsT, rhs=rhs, start=start, stop=stop), r, w)
        else:
            P.op("pe", lambda e: e.matmul(out, lhsT=lhsT, rhs=rhs, start=start, stop=stop, tile_position=tp), r, w)

    def scan(out, d0, d1, init, r, w):
        P.op("dve", lambda e: e.tensor_tensor_scan(out=out, data0=d0, data1=d1, initial=init, op0=ALU.mult,
                                                   op1=ALU.add), r, w)

    def barrier(keys):
        P.op("dve", lambda e: e.memset(scr[:], 0.0), (), list(keys))

    def ld(out, in_, r=(), w=(), eng="sync"):
        P.dma(eng, out, in_, r, w)

    WK = [bkey("Wb", 0), bkey("Wb", 1)]

    plist = [(bmod, bmod_d), (n1g, n1g_d), (n2g, n2g_d), (nfg, nfg_d), (ssmd, ssmd_d), (dwk, dwk_d),
             (dwb, dwb_d), (cng, cng_d), (cnb, cnb_d), (sel, sel_d), (cvec, cvec_d)]
    if first:
        plist.append((omega, omega_d))
    for t_, d_ in plist:
        ld(t_[:], d_, w=["par"])
    ld(Pl[:], P_l_d, w=["Pl"])
    P.dma("pool", ident[:], ident_d, w=["par"])
    mset("dve", ones_bf[:], 1.0, ["par"])
    mset("dve", halfpi[:], PI / 2, ["par"])
    mset("dve", e6[:], 1e-6, ["par"])
    act(csil[:], cvec[:], AF.Silu, ["par"], ["csil"])

    def sincos(eng, s_out, c_out, u, tmp, r, w):
        ts(eng, tmp, u, MAGIC, None, ALU.add, None, r, w)
        ts(eng, tmp, tmp, -MAGIC, None, ALU.add, None, w, w)
        tt(eng, tmp, u, tmp, ALU.subtract, list(r) + list(w), w)
        act(c_out, tmp, AF.Abs, w, w)
        act(c_out, c_out, AF.Sin, list(w) + ["par"], w, bias=halfpi[:, 0:1], scale=-2 * PI)
        act(s_out, tmp, AF.Sin, w, w, scale=2 * PI)

    def disc(lre, lim, ls, r_o, cth, sth, dt_o, th_o, tmp, rk, wk):
        act(dt_o, ls, AF.Exp, rk, wk)
        tt("dve", r_o, lre, dt_o, ALU.mult, list(rk) + list(wk), wk)
        tt("dve", th_o, lim, dt_o, ALU.mult, list(rk) + list(wk), wk)
        ts("dve", th_o, th_o, 1.0 / (2 * PI), None, ALU.mult, None, wk, wk)
        sincos("dve", sth, cth, th_o, tmp, wk, wk)

    PQ = lambda i: Pq[:, i, :]
    disc(Pl[:, 0, :], Pl[:, 1, :], Pl[:, 2, :], PQ(7), PQ(2), PQ(3), PQ(6), PQ(1), PQ(8), ["Pl", "par"], ["Pq"])
    act(PQ(0), PQ(7), AF.Exp, ["Pq"], ["Pq"])
    act(PQ(9), PQ(7), AF.Exp, ["Pq"], ["Pq"], scale=float(T))
    ts("dve", PQ(10), PQ(1), float(T), None, ALU.mult, None, ["Pq"], ["Pq"])
    sincos("dve", PQ(11), PQ(12), PQ(10), PQ(8), ["Pq"], ["Pq"])
    tt("dve", PQ(4), PQ(9), PQ(12), ALU.mult, ["Pq"], ["Pq"])
    tt("dve", PQ(5), PQ(9), PQ(11), ALU.mult, ["Pq"], ["Pq"])

    if first:
        for k in range(8):
            ld(xT[:, k, 0:T], xT_d[:, k, :], w=[XK[k]])
            ld(xT[:, k, T:TT], cT_d[:, k, :], w=[XK[k]])
        for k in range(8):
            idx_d = ridx_d if k < 4 else cidx_d
            if k in (0, 4):
                ld(acc0[:, 0:T], idx_d[0:1, :].partition_broadcast(128), w=["acc0"])
            ph = (PI / 2 if (k // 2) % 2 == 1 else 0.0)
            ts("dve", acc1[:, 0:T], acc0[:, 0:T], omega[:, k % 2:k % 2 + 1], ph, ALU.mult, ALU.add,
               ["acc0", "par"], ["hT"])
            ts("dve", rstd[:, 0:T], acc1[:, 0:T], 1.0 / (2 * PI), MAGIC, ALU.mult, ALU.add, ["hT"], ["rstd"])
            ts("dve", rstd[:, 0:T], rstd[:, 0:T], -MAGIC, None, ALU.add, None, ["rstd"], ["rstd"])
            stt(acc1[:, 0:T], rstd[:, 0:T], -2 * PI, acc1[:, 0:T], ALU.mult, ALU.add, ["rstd", "hT"], ["hT"])
            act(acc1[:, 0:T], acc1[:, 0:T], AF.Sin, ["hT", "par"], ["hT"])
            tt("dve", xT[:, k, 0:T], xT[:, k, 0:T], acc1[:, 0:T], ALU.add, ["hT", XK[k]], [XK[k]])

    wslot = [0]

    def wload(Wd_ap, KT, c0, cw, wbufs=None, wkey="Wb"):
        wbufs = wbufs or Wb
        s = wslot[0] % 2
        wslot[0] += 1
        wb = wbufs[s]
        src = Wd_ap[:, c0:c0 + cw].rearrange("(k p) f -> p k f", p=128)
        P.dma("pool", wb[:, 0:KT, 0:cw], src, w=[bkey(wkey, s)])
        return wb, bkey(wkey, s)

    pscnt = [0]

    def stream_proj(Wd_ap, KT, ncols, inT, inkeys, evac, col0=0, blocks=BLOCKS, chunk=384):
        nchunk = (ncols + chunk - 1) // chunk
        for c in range(nchunk):
            cw = min(chunk, ncols - c * chunk)
            wb, wk = wload(Wd_ap, KT, col0 + c * chunk, cw)
            for mi in range(cw // 128):
                m = c * (chunk // 128) + mi
                for bi, (b0, bn) in enumerate(blocks):
                    pi = pscnt[0] % 4
                    pscnt[0] += 1
                    for k in range(KT):
                        mm(ps[pi][:, 0:bn], wb[:, k, mi * 128:(mi + 1) * 128], inT[:, k, b0:b0 + bn],
                           k == 0, k == KT - 1, [wk] + list(inkeys), [bkey("ps", pi)])
                    evac(m, bi, b0, bn, ps[pi][:, 0:bn], bkey("ps", pi))

    def rms_stats():
        for bi, (b0, bn) in enumerate(BLOCKS):
            for k in range(8):
                act(hT[:, k, b0:b0 + bn], xT[:, k, b0:b0 + bn], AF.Square, [XK[k]], ["hT"])
            for k in range(8):
                mm(ps[4][:, 0:bn], ones_bf[:, :], hT[:, k, b0:b0 + bn], k == 0, k == 7, ["hT", "par"], [bkey("ps", 4)])
            act(rstd[:, b0:b0 + bn], ps[4][:, 0:bn], AF.Sqrt, [bkey("ps", 4), "par"], ["rstd"], bias=e6[:, 0:1],
                scale=1.0 / D)
            recip(rstd[:, b0:b0 + bn], rstd[:, b0:b0 + bn], ["rstd"], ["rstd"])

    def modulate(gsv, shoff):
        for k in range(8):
            for (c0, c1, mc) in ((0, T, 0), (T, TT, 1)):
                tt("dve", acc0[:, c0:c1], xT[:, k, c0:c1], rstd[:, c0:c1], ALU.mult, [XK[k], "rstd"], ["acc0"])
                act(hT[:, k, c0:c1], acc0[:, c0:c1], AF.Identity, ["acc0", "modv"], ["hT"],
                    bias=modv[:, shoff + k, mc:mc + 1], scale=gsv[:, k, mc:mc + 1])

    def mod_vectors(l):
        for c in range(16):
            wb, wk = wload(w_mod_d[l], 8, c * 384, 384)
            for mi in range(3):
                m = c * 3 + mi
                for k in range(8):
                    mm(ps[5][:, 2 * m:2 * m + 2], wb[:, k, mi * 128:(mi + 1) * 128], csil[:, k, :], k == 0, k == 7,
                       [wk, "csil"], [bkey("ps", 5)])
        for c2 in range(2):
            tt("dve", modv[:, :, c2], ps[5][:, c2:96:2], bmod[:, l, :], ALU.add, [bkey("ps", 5), "par"], ["modv"])
        for (gsv, ng, off) in ((gs1, n1g, 8), (gs2, n2g, 32)):
            for c2 in range(2):
                stt(gsv[:, :, c2], modv[:, off:off + 8, c2], 1.0, ng[:, l, :], ALU.add, ALU.mult,
                    ["modv", "par"], ["modv"])

    def s5_tables(l):
        barrier(["rstd", "tabB", "tabC", "W1", "W2", "G1", "G2", "ET"])
        for d in S5_DIRS:
            lre, lim, ls, bre, bim = ET[0], ET[1], ET[2], ET[3], ET[4]
            for i, tdst in enumerate((lre, lim, ls)):
                ld(tdst[:], E_l_d[l, d, i], w=["ET"])
            for i, tdst in enumerate((bre, bim)):
                ld(tdst[:], E_b_d[l, d, i], w=["ET"])
            r_, cth, sth, dt_, th_, tmp = ET[5], ET[6], ET[7], ET[8], ET[9], ET[10]
            disc(lre[:], lim[:], ls[:], r_[:], cth[:], sth[:], dt_[:], th_[:], tmp[:], ["ET", "par"], ["ET"])
            act(r_[:], r_[:], AF.Exp, ["ET"], ["ET"])
            A_, Bn, den, fre, fim, t1 = ET[11], ET[12], ET[13], ET[14], ET[15], ET[8]
            tt("dve", A_[:], r_[:], cth[:], ALU.mult, ["ET"], ["ET"])
            ts("dve", A_[:], A_[:], -1.0, None, ALU.add, None, ["ET"], ["ET"])
            tt("dve", Bn[:], r_[:], sth[:], ALU.mult, ["ET"], ["ET"])
            tt("dve", den[:], lre[:], lre[:], ALU.mult, ["ET"], ["ET"])
            tt("dve", t1[:], lim[:], lim[:], ALU.mult, ["ET"], ["ET"])
            tt("dve", den[:], den[:], t1[:], ALU.add, ["ET"], ["ET"])
            recip(den[:], den[:], ["ET"], ["ET"])
            tt("dve", fre[:], A_[:], lre[:], ALU.mult, ["ET"], ["ET"])
            tt("dve", t1[:], Bn[:], lim[:], ALU.mult, ["ET"], ["ET"])
            tt("dve", fre[:], fre[:], t1[:], ALU.add, ["ET"], ["ET"])
            tt("dve", fre[:], fre[:], den[:], ALU.mult, ["ET"], ["ET"])
            tt("dve", fim[:], Bn[:], lre[:], ALU.mult, ["ET"], ["ET"])
            tt("dve", t1[:], A_[:], lim[:], ALU.mult, ["ET"], ["ET"])
            tt("dve", fim[:], fim[:], t1[:], ALU.subtract, ["ET"], ["ET"])
            tt("dve", fim[:], fim[:], den[:], ALU.mult, ["ET"], ["ET"])
            t2 = ET[9]
            tt("dve", t1[:], fre[:], bre[:], ALU.mult, ["ET"], ["ET"])
            tt("dve", t2[:], fim[:], bim[:], ALU.mult, ["ET"], ["ET"])
            tt("dve", tabB[:, d, 0, :], t1[:], t2[:], ALU.subtract, ["ET"], ["tabB"])
            tt("dve", t1[:], fre[:], bim[:], ALU.mult, ["ET"], ["ET"])
            tt("dve", t2[:], fim[:], bre[:], ALU.mult, ["ET"], ["ET"])
            tt("dve", tabB[:, d, 1, :], t1[:], t2[:], ALU.add, ["ET"], ["tabB"])
            ld(ET[0][:], P_c_d[l, d, 0], w=["ET"])
            ld(ET[1][:], P_c_d[l, d, 1], w=["ET"])
            act(tabC[:, d, 0, :], ET[0][:], AF.Copy, ["ET"], ["tabC"])
            act(tabC[:, d, 1, :], ET[1][:], AF.Copy, ["ET"], ["tabC"], scale=-1.0)
        barrier(["W1", "W2", "G1", "G2", "ET"])

    def rot(o1, o2, x1, x2, tmp, cs, sn, r, w, tkey):
        tt("pool", o1, x1, cs, ALU.mult, r, w)
        tt("dve", tmp, x2, sn, ALU.mult, r, [tkey])
        tt("dve", o1, o1, tmp, ALU.add, list(w) + [tkey], w)
        tt("pool", o2, x1, sn, ALU.mult, list(r) + list(w) + [tkey], w)
        tt("dve", tmp, x2, cs, ALU.mult, list(r) + list(w), [tkey])
        tt("pool", o2, o2, tmp, ALU.subtract, list(w) + [tkey], w)

    SEGS = ((T, TT, TC), (0, T, T))

    tilecnt = [0]

    def s5_tile(l, d, j, phase):
        par_ = tilecnt[0] % 2 if phase == 1 else 0
        tilecnt[0] += 1
        SN_, CS_ = (SN, CS) if par_ == 0 else (SN2, CS2)
        snk, csk = ("SN", "CS") if par_ == 0 else ("SN2", "CS2")
        return _s5_tile(l, d, j, phase, SN_, CS_, snk, csk)

    def _s5_tile(l, d, j, phase, SN, CS, snk, csk):
        c = (l * 2 + d) * 16 + j
        dj = d * 16 + j
        q, a = j // 4, j % 4
        pcol = lambda i: Pq[:, i, c:c + 1]
        rev = d == 1

        def tab(t_, n):
            return t_[:, n - 1::-1] if rev and n > 1 else (t_[:, 0:n] if not rev else t_[:, 0:1])

        def tabn(t_, n):
            if not rev:
                return t_[:, 0:n]
            return t_[:, n - 1::-1] if n > 1 else t_[:, 0:1]

        ts("pool", SN[:], tau[:], pcol(1), MAGIC, ALU.mult, ALU.add, ["tau", "Pq"], [snk])
        ts("pool", SN[:], SN[:], 1.0, -MAGIC, ALU.mult, ALU.add, [snk], [snk])
        stt(SN[:], tau[:], pcol(1), SN[:], ALU.mult, ALU.subtract, ["tau", "Pq", snk], [snk])
        act(CS[:], SN[:], AF.Abs, [snk], [csk])
        act(CS[:], CS[:], AF.Sin, [csk, "par"], [csk], bias=halfpi[:, 0:1], scale=-2 * PI)
        act(SN[:], SN[:], AF.Sin, [snk], [snk], scale=2 * PI)
        ts("pool", rmul[:], tau[:], 0.0, pcol(0), ALU.mult, ALU.add, ["tau", "Pq"], ["acc0"])
        for bi, (b0, bn) in enumerate(BLOCKS):
            p0 = 2 * (bi % 2)
            for ri in range(2):
                mm(ps[p0 + ri][:, 0:bn], tabB[32 * a:32 * a + 32, d, ri, q * 128:(q + 1) * 128],
                   uT[32 * a:32 * a + 32, q, b0:b0 + bn], True, True, ["tabB", "uT", "rstd"], [bkey("ps", p0 + ri)],
                   tp=(32 * a, 0))
            act(G1[:, b0:b0 + bn], ps[p0][:, 0:bn], AF.Copy, [bkey("ps", p0)], ["G1"])
            act(G2[:, b0:b0 + bn], ps[p0 + 1][:, 0:bn], AF.Copy, [bkey("ps", p0 + 1)], ["G2"])
        for (c0, c1, n) in SEGS:
            rot(W1[:, c0:c1], W2[:, c0:c1], G1[:, c0:c1], G2[:, c0:c1], HHf[:, c0:c1], tabn(CS, n), tabn(SN, n),
                ["G1", "G2", csk, snk], ["W1", "W2"], "HH")
        if phase == 2:
            hr, hi_ = hst[:, 4, dj:dj + 1], hst[:, 5, dj:dj + 1]
            tt("dve", ini[:, 0:1], hr, pcol(2), ALU.mult, ["hst", "Pq"], ["ini"])
            tt("dve", ini[:, 2:3], hi_, pcol(3), ALU.mult, ["hst", "Pq"], ["ini"])
            tt("dve", ini[:, 0:1], ini[:, 0:1], ini[:, 2:3], ALU.subtract, ["ini"], ["ini"])
            tt("dve", ini[:, 1:2], hr, pcol(3), ALU.mult, ["hst", "Pq", "ini"], ["ini"])
            tt("dve", ini[:, 2:3], hi_, pcol(2), ALU.mult, ["hst", "Pq", "ini"], ["ini"])
            tt("dve", ini[:, 1:2], ini[:, 1:2], ini[:, 2:3], ALU.add, ["ini"], ["ini"])
            ts("dve", ini[:, 1:2], ini[:, 1:2], -1.0, None, ALU.mult, None, ["ini"], ["ini"])
        for (c0, c1, n) in SEGS:
            for gi, (Gx, Wx, gk, wk) in enumerate(((G1, W1, "G1", "W1"), (G2, W2, "G2", "W2"))):
                if phase == 2 and c0 == 0:
                    init = ini[:, gi:gi + 1]
                else:
                    init = 0.0
                if not rev:
                    scan(Gx[:, c0:c1], rmul[:, 0:n], Wx[:, c0:c1], init, [wk, "acc0", "ini"], [gk])
                else:
                    lo = c0 - 1 if c0 > 0 else None
                    scan(Gx[:, c1 - 1:lo:-1], rmul[:, 0:n], Wx[:, c1 - 1:lo:-1], init, [wk, "acc0", "ini"], [gk])
        if phase == 1:
            for (c0, c1, n), row in zip(SEGS, (0, 2)):
                ecol = c1 - 1 if not rev else c0
                tcol = n - 1
                g1, g2 = G1[:, ecol:ecol + 1], G2[:, ecol:ecol + 1]
                cs_, sn_ = CS[:, tcol:tcol + 1], SN[:, tcol:tcol + 1]
                hr, hi_ = hst[:, row, dj:dj + 1], hst[:, row + 1, dj:dj + 1]
                tt("dve", hr, g1, cs_, ALU.mult, ["G1", csk], ["hst"])
                tt("dve", ini[:, 3:4], g2, sn_, ALU.mult, ["G2", snk], ["ini"])
                tt("dve", hr, hr, ini[:, 3:4], ALU.add, ["hst", "ini"], ["hst"])
                tt("dve", hi_, g1, sn_, ALU.mult, ["G1", snk], ["hst"])
                tt("dve", ini[:, 3:4], g2, cs_, ALU.mult, ["G2", csk, "hst"], ["ini"])
                tt("dve", hi_, hi_, ini[:, 3:4], ALU.subtract, ["hst", "ini"], ["hst"])
            return
        for (c0, c1, n) in SEGS:
            cs_, sn_ = tabn(CS, n), tabn(SN, n)
            tt("pool", W1[:, c0:c1], G1[:, c0:c1], cs_, ALU.mult, ["G1", csk], ["W1"])
            tt("dve", W2[:, c0:c1], G2[:, c0:c1], sn_, ALU.mult, ["G2", snk], ["W2"])
            tt("dve", hre[:, c0:c1], W1[:, c0:c1], W2[:, c0:c1], ALU.add, ["W1", "W2"], ["HH"])
            tt("pool", W1[:, c0:c1], G1[:, c0:c1], sn_, ALU.mult, ["G1", snk, "HH"], ["W1"])
            tt("dve", W2[:, c0:c1], G2[:, c0:c1], cs_, ALU.mult, ["G2", csk, "HH"], ["W2"])
            tt("pool", him[:, c0:c1], W1[:, c0:c1], W2[:, c0:c1], ALU.subtract, ["W1", "W2"], ["HH"])
        for bi, (b0, bn) in enumerate(BLOCKS):
            pi = 4 + bi % 2
            mm(ps[pi][32 * a:32 * a + 32, 0:bn], tabC[:, d, 0, j * 32:(j + 1) * 32], hre[:, b0:b0 + bn], True, False,
               ["tabC", "HH", "rstd"], [bkey("ps", pi)], tp=(0, 32 * a))
            mm(ps[pi][32 * a:32 * a + 32, 0:bn], tabC[:, d, 1, j * 32:(j + 1) * 32], him[:, b0:b0 + bn], False, True,
               ["tabC", "HH", "rstd"], [bkey("ps", pi)], tp=(0, 32 * a))
            ydst = yacc[32 * a:32 * a + 32, q, b0:b0 + bn]
            if d == S5_DIRS[0]:
                cp("dve", ydst, ps[pi][32 * a:32 * a + 32, 0:bn], [bkey("ps", pi)], ["yacc"])
            else:
                tt("dve", ydst, ydst, ps[pi][32 * a:32 * a + 32, 0:bn], ALU.add, [bkey("ps", pi), "yacc"], ["yacc"])

    def layer_A(l):
        mod_vectors(l)
        rms_stats()
        modulate(gs1, 0)
        def ev_u(m, bi, b0, bn, p_, pk):
            act(uT[:, m, b0:b0 + bn], p_, AF.Copy, [pk], ["uT"])
        stream_proj(w_in_d[l], 8, 512, hT, ["hT"], ev_u)
        mset("pool", vbp[:, :, 0:15], 0.0, ["vbp"])
        mset("pool", vbp[:, :, 15 + T:VC0], 0.0, ["vbp"])
        mset("pool", vbp[:, :, VC0 + TC:VP], 0.0, ["vbp"])
        for m in range(8):
            wa, wka = wload(w_in_d[l], 8, 512 + m * 128, 128)
            wg, wkg = wload(w_in_d[l], 8, 1536 + m * 128, 128)
            for bi, (b0, bn) in enumerate(BLOCKS):
                pa, pg = (bi % 2) * 2, (bi % 2) * 2 + 1
                for k in range(8):
                    mm(ps[pg][:, 0:bn], wg[:, k, 0:128], hT[:, k, b0:b0 + bn], k == 0, k == 7, [wkg, "hT"], [bkey("ps", pg)])
                for k in range(8):
                    mm(ps[pa][:, 0:bn], wa[:, k, 0:128], hT[:, k, b0:b0 + bn], k == 0, k == 7, [wka, "hT"], [bkey("ps", pa)])
                act(acc0[:, 0:bn], ps[pg][:, 0:bn], AF.Sigmoid, [bkey("ps", pg)], ["acc0"])
                vc = 15 + b0 if b0 < T else VC0
                tt("dve", vbp[:, m, vc:vc + bn], ps[pa][:, 0:bn], acc0[:, 0:bn], ALU.mult, [bkey("ps", pa), "acc0"], ["vbp"])
        for k in range(8):
            ld(xsp_o[:, k, :], xT[:, k, :], r=[XK[k]], w=["xsp"])
        ld(hsp_o[:, :, :], hT[:, :, :], r=["hT"], w=["hsp"])
        ld(u_o[:, :, :], uT[:, :, :], r=["uT"], w=["uo"])
        barrier(XK + S5K + ["hT", "yacc"])
        ld(tau[:], tau_d[0:1, :].partition_broadcast(128), w=["tau"])
        s5_tables(l)
        for j in range(16):
            for d in S5_DIRS:
                s5_tile(l, d, j, 1)
        barrier(["acc0", "pay"])
        mset("dve", pay[:], 0.0, ["pay"])
        cp("dve", pay[:, 0:32], hst[:, 2, :], ["hst"], ["pay"])
        cp("dve", pay[:, 32:64], hst[:, 3, :], ["hst"], ["pay"])
        cp("dve", pay[:, 304:336], hst[:, 0, :], ["hst"], ["pay"])
        cp("dve", pay[:, 336:368], hst[:, 1, :], ["hst"], ["pay"])
        pe_ = pay[:, 64:304].rearrange("p (q s f) -> p q s f", q=8, s=2)
        cp("dve", pe_[:, :, 0, :], vbp[:, :, 15:30], ["vbp"], ["pay"])
        cp("dve", pe_[:, :, 1, :], vbp[:, :, 15 + T - 15:15 + T], ["vbp"], ["pay"])
        ld(pay_o[:, :], pay[:], r=["pay"], w=["payo"])
        ld(mods_o[:, 0:48, :], modv[:], r=["modv"], w=["payo"])
        ld(mods_o[:, 48:56, :], gs1[:], r=["modv"], w=["payo"])
        ld(mods_o[:, 56:64, :], gs2[:], r=["modv"], w=["payo"])
        ld(vbp_o[:, :, :], vbp[:, :, :], r=["vbp"], w=["vo"])
        P.op("sync", None, r=["xsp", "hsp", "uo", "payo", "vo"], w=["doneA"])

    def layer_B(l, x_src):
        ld(modv[:], mods_i[:, 0:48, :], w=["modv"])
        ld(gs1[:], mods_i[:, 48:56, :], w=["modv"])
        ld(gs2[:], mods_i[:, 56:64, :], w=["modv"])
        ld(uT[:, :, :], uin_d[:, :, :], w=["uT"])
        ld(vbp[:, :, :], vin_d[:, :, :], w=["vbp"])
        P.dma("sync", ccs[:, :, :], ccs_d[:, :, :], w=WK + ["ccs"])
        ld(tau[:], tau_d[0:1, :].partition_broadcast(128), w=["tau"])
        HR, HI, T1, T2 = hst[:, 6, 0:16], hst[:, 7, 0:16], hst[:, 8, 0:16], hst[:, 9, 0:16]
        for d in S5_DIRS:
            c0 = (l * 2 + d) * 16
            LR, LI = Pq[:, 4, c0:c0 + 16], Pq[:, 5, c0:c0 + 16]
            SR, SI = hst[:, 4, d * 16:(d + 1) * 16], hst[:, 5, d * 16:(d + 1) * 16]
            cp("dve", HR, ccs[:, 0, 304 + d * 16:304 + d * 16 + 16], ["ccs"], ["hst"])
            cp("dve", HI, ccs[:, 0, 336 + d * 16:336 + d * 16 + 16], ["ccs"], ["hst"])
            order = list(range(NCORES)) if d == 0 else list(range(NCORES - 1, -1, -1))
            for n_, ci in enumerate(order):
                sc_ = sel[:, ci:ci + 1]
                if n_ == 0:
                    ts("dve", SR, HR, sc_, None, ALU.mult, None, ["hst", "par"], ["hst"])
                    ts("dve", SI, HI, sc_, None, ALU.mult, None, ["hst", "par"], ["hst"])
                else:
                    stt(SR, HR, sc_, SR, ALU.mult, ALU.add, ["hst", "par"], ["hst"])
                    stt(SI, HI, sc_, SI, ALU.mult, ALU.add, ["hst", "par"], ["hst"])
                if n_ == NCORES - 1:
                    break
                lr_, li_ = ccs[:, ci, d * 16:d * 16 + 16], ccs[:, ci, 32 + d * 16:32 + d * 16 + 16]
                tt("dve", T1, LR, HR, ALU.mult, ["hst", "Pq"], ["hst"])
                tt("dve", T2, LI, HI, ALU.mult, ["hst", "Pq"], ["hst"])
                tt("dve", T1, T1, T2, ALU.subtract, ["hst"], ["hst"])
                tt("dve", T2, LI, HR, ALU.mult, ["hst", "Pq"], ["hst"])
                tt("dve", HR, T1, lr_, ALU.add, ["hst", "ccs"], ["hst"])
                tt("dve", T1, LR, HI, ALU.mult, ["hst", "Pq"], ["hst"])
                tt("dve", T1, T1, T2, ALU.add, ["hst"], ["hst"])
                tt("dve", HI, T1, li_, ALU.add, ["hst", "ccs"], ["hst"])
        barrier(["acc0", "hal"])
        for side, so in ((0, 16), (1, 8)):
            hs = hal[:, side, :, :]
            for ci in range(NCORES):
                e_ = ccs[:, ci, 64:304].rearrange("p (q s f) -> p q s f", q=8, s=2)[:, :, 1 - side, :]
                sc_ = sel[:, 8 + side * 8 + ci: 8 + side * 8 + ci + 1]
                if ci == 0:
                    ts("dve", hs, e_, sc_, None, ALU.mult, None, ["ccs", "par"], ["hal"])
                else:
                    stt(hs, e_, sc_, hs, ALU.mult, ALU.add, ["ccs", "par", "hal"], ["hal"])
        cp("dve", vbp[:, :, 0:15], hal[:, 0, :, :], ["hal"], ["vbp"])
        cp("dve", vbp[:, :, 15 + T:15 + T + 15], hal[:, 1, :, :], ["hal"], ["vbp"])
        barrier(S5K + ["yacc", "hT", "hal", "acc0", "pay"] + XK)
        s5_tables(l)
        for j in range(16):
            for d in S5_DIRS:
                s5_tile(l, d, j, 2)
        barrier(S5K + ["ya", "ya2", "cvo", "acc0", "gt0", "tabB", "tabC", "rstd"] + XK)
        gtmp = sb("gtmp", [TT], F32, O_A + 36864)
        for q in range(4):
            stt(gt0[:], uT[:, q, :], ssmd[:, l, q:q + 1], yacc[:, q, :], ALU.mult, ALU.add, ["uT", "par", "yacc"], ["gt0"])
            act(gtmp[:], gt0[:], AF.Square, ["gt0"], ["cvo"])
            ts("pool", gtmp[:], gtmp[:], 0.044715, 1.0, ALU.mult, ALU.add, ["cvo"], ["cvo"])
            tt("pool", gtmp[:], gtmp[:], gt0[:], ALU.mult, ["cvo", "gt0"], ["cvo"])
            act(gtmp[:], gtmp[:], AF.Sigmoid, ["cvo"], ["cvo"], scale=1.5957691216057308)
            tt("dve", ya[:, q, :], gt0[:], gtmp[:], ALU.mult, ["gt0", "cvo"], ["ya"])

        def ev_glu(m, bi, b0, bn, p_, pk):
            act(gt0[:, 0:bn], p_, AF.Sigmoid, [pk], ["gt0"])
            tt("dve", ya2[:, m, b0:b0 + bn], ya[:, m, b0:b0 + bn], gt0[:, 0:bn], ALU.mult, ["ya", "gt0"], ["ya2"])
        stream_proj(w_glu_d[l], 4, 512, ya, ["ya"], ev_glu)
        barrier(["uT", "dg", "gt1", "sqt"])
        for q in range(8):
            for jt in range(31):
                ts("pool", dg[:, jt, :], ident[:, :], dwk[:, l, q, jt:jt + 1], 0.0, ALU.mult, ALU.add, ["par"], ["dg"])
            for (o0, v0, n) in ((0, 0, 512), (512, 512, 512), (1024, 1024, 512), (1536, 1536, 512), (T, VC0 - 15, TC)):
                pi = pscnt[0] % 4
                pscnt[0] += 1
                for jt in range(31):
                    mm(ps[pi][:, 0:n], dg[:, jt, :], vbp[:, q, v0 + jt:v0 + jt + n], jt == 0, jt == 30, ["dg", "vbp"],
                       [bkey("ps", pi)])
                act(cvo[:, q, o0:o0 + n], ps[pi][:, 0:n], AF.Identity, [bkey("ps", pi), "par"], ["cvo"],
                    bias=dwb[:, l, q:q + 1])
        for bi, (b0, bn) in enumerate(BLOCKS):
            for q in range(8):
                act(sqt[:, 0:bn], cvo[:, q, b0:b0 + bn], AF.Square, ["cvo"], ["sqt"])
                mm(ps[4][:, 0:bn], ones_bf[:, :], cvo[:, q, b0:b0 + bn], q == 0, q == 7, ["cvo", "par"], [bkey("ps", 4)])
                mm(ps[5][:, 0:bn], ones_bf[:, :], sqt[:, 0:bn], q == 0, q == 7, ["sqt", "par"], [bkey("ps", 5)])
            mean, rs_ = gt0[:, 0:bn], gt0[:, 512:512 + bn]
            ts("dve", mean, ps[4][:, 0:bn], 1.0 / D, None, ALU.mult, None, [bkey("ps", 4)], ["gt0"])
            tt("dve", rs_, mean, mean, ALU.mult, ["gt0"], ["gt0"])
            stt(rs_, ps[5][:, 0:bn], 1.0 / D, rs_, ALU.mult, ALU.subtract, [bkey("ps", 5), "gt0"], ["gt0"])
            act(rs_, rs_, AF.Sqrt, ["gt0", "par"], ["gt0"], bias=e6[:, 0:1])
            recip(rs_, rs_, ["gt0"], ["gt0"])
            for q in range(8):
                t_ = gt0[:, 1024:1024 + bn]
                tt("dve", t_, cvo[:, q, b0:b0 + bn], mean, ALU.subtract, ["cvo", "gt0"], ["gt0"])
                tt("dve", t_, t_, rs_, ALU.mult, ["gt0"], ["gt0"])
                act(cvo[:, q, b0:b0 + bn], t_, AF.Silu, ["gt0", "par"], ["cvo"], bias=cnb[:, l, q:q + 1],
                    scale=cng[:, l, q:q + 1])
        barrier(["yacc", "hT", "mrg", "vbp"])
        ld(hT[:, :, :], hin_d[:, :, :], w=["hT"])
        for m in range(8):
            wga, kga = wload(w_in_d[l], 8, 2560 + m * 128, 128)
            wgb, kgb = wload(w_in_d[l], 8, 3584 + m * 128, 128)
            for bi, (b0, bn) in enumerate(BLOCKS):
                for k in range(8):
                    mm(ps[0][:, 0:bn], wga[:, k, 0:128], hT[:, k, b0:b0 + bn], k == 0, k == 7, [kga, "hT"], [bkey("ps", 0)])
                for k in range(8):
                    mm(ps[1][:, 0:bn], wgb[:, k, 0:128], hT[:, k, b0:b0 + bn], k == 0, k == 7, [kgb, "hT"], [bkey("ps", 1)])
                act(gt0[:, b0:b0 + bn], ps[0][:, 0:bn], AF.Sigmoid, [bkey("ps", 0)], ["gt0"])
                act(gt1[:, b0:b0 + bn], ps[1][:, 0:bn], AF.Sigmoid, [bkey("ps", 1)], ["gt1"])
            wpa, kpa = wload(w_pa_d[l], 4, m * 128, 128)
            wpb, kpb = wload(w_pb_d[l], 8, m * 128, 128)
            for bi, (b0, bn) in enumerate(BLOCKS):
                for k in range(4):
                    mm(ps[2][:, 0:bn], wpa[:, k, 0:128], ya2[:, k, b0:b0 + bn], k == 0, k == 3, [kpa, "ya2"], [bkey("ps", 2)])
                for k in range(8):
                    mm(ps[3][:, 0:bn], wpb[:, k, 0:128], cvo[:, k, b0:b0 + bn], k == 0, k == 7, [kpb, "cvo"], [bkey("ps", 3)])
                tt("dve", gt0[:, b0:b0 + bn], gt0[:, b0:b0 + bn], ps[2][:, 0:bn], ALU.mult, ["gt0", bkey("ps", 2)], ["gt0"])
                tt("dve", gt1[:, b0:b0 + bn], gt1[:, b0:b0 + bn], ps[3][:, 0:bn], ALU.mult, ["gt1", bkey("ps", 3)], ["gt1"])
                tt("pool", mrg[:, m, b0:b0 + bn], gt0[:, b0:b0 + bn], gt1[:, b0:b0 + bn], ALU.add, ["gt0", "gt1"], ["mrg"])
        barrier(["ya", "ya2", "cvo"] + XK)
        for k in range(8):
            ld(xT[:, k, :], x_src[:, k, :], w=[XK[k]])

        def ev_out(m, bi, b0, bn, p_, pk):
            mc = 1 if b0 >= T else 0
            stt(xT[:, m, b0:b0 + bn], p_, modv[:, 16 + m, mc:mc + 1], xT[:, m, b0:b0 + bn], ALU.mult, ALU.add,
                [pk, "modv", XK[m]], [XK[m]])
        stream_proj(w_out_d[l], 8, D, mrg, ["mrg"], ev_out)
        barrier(["acc0", "gt0", "rstd", "tabB", "tabC", "hT", "uT", "gt1", "dg", "sqt", "mrg", "vbp"] + ACTK + WDK)
        rms_stats()
        modulate(gs2, 24)
        hb = [(0, 512), (512, 256)]
        for part in range(3):
            h0 = part * HALF
            hTh = hT[:, :, h0:h0 + HALF]

            def ev_gate(m, bi, b0, bn, p_, pk):
                act(actT[:, m, b0:b0 + bn], p_, AF.Silu, [pk], [bkey("actT", m)])

            def ev_up(m, bi, b0, bn, p_, pk):
                tt("dve", actT[:, m, b0:b0 + bn], actT[:, m, b0:b0 + bn], p_, ALU.mult, [pk, bkey("actT", m)],
                   [bkey("actT", m)])
            stream_proj(w_fg_d[l], 8, FFN, hTh, ["hT"], ev_gate, blocks=hb)
            stream_proj(w_fu_d[l], 8, FFN, hTh, ["hT"], ev_up, blocks=hb)
            for m in range(8):
                wb, wk = wload(w_fd_d[l], 22, m * 128, 128, wbufs=Wd, wkey="Wd")
                for bi, (b0, bn) in enumerate(hb):
                    pi = pscnt[0] % 4
                    pscnt[0] += 1
                    for k in range(22):
                        mm(ps[pi][:, 0:bn], wb[:, k, :], actT[:, k, b0:b0 + bn], k == 0, k == 21,
                           [wk, bkey("actT", k)], [bkey("ps", pi)])
                    t0 = h0 + b0
                    segs = [(t0, T, 0), (T, t0 + bn, 1)] if t0 < T < t0 + bn else [(t0, t0 + bn, 0 if t0 + bn <= T else 1)]
                    for (s0, s1, mc) in segs:
                        stt(xT[:, m, s0:s1], ps[pi][:, s0 - t0:s1 - t0], modv[:, 40 + m, mc:mc + 1], xT[:, m, s0:s1],
                            ALU.mult, ALU.add, [bkey("ps", pi), "modv", XK[m]], [XK[m]])
        barrier(["hT", "uT", "vbp", "mrg", "acc0", "gt0", "gt1", "dg", "sqt"] + ACTK + WDK)

    if stage == "A0":
        layer_A(0)
    elif stage == "B0A1":
        layer_B(0, xin_d)
        layer_A(1)
    else:
        layer_B(1, xin_d)
        rms_stats()
        for k in range(8):
            tt("dve", acc0[:, 0:T], xT[:, k, 0:T], rstd[:, 0:T], ALU.mult, [XK[k], "rstd"], ["acc0"])
            ts("dve", acc0[:, 0:T], acc0[:, 0:T], nfg[:, k:k + 1], None, ALU.mult, None, ["acc0", "par"], ["acc0"])
            P.dma("sync", out_d[:, k, :], acc0[:, 0:T], r=["acc0"], w=["out"])
        P.op("sync", None, r=["out"], w=["done"])
    P.emit()
    return nc


def _tT(v):
    return np.ascontiguousarray(v.reshape(-1, 128).T)


def prep_inputs(inp):
    f = np.float32
    x = np.asarray(inp["x"], f)[0]
    ctx = np.asarray(inp["ctx"], f)[0]
    common = {}
    cT = np.ascontiguousarray(ctx.T.reshape(8, 128, TC).transpose(1, 0, 2))
    cv = np.stack([np.asarray(inp["c"], f)[0], np.asarray(inp["c_ctx"], f)], axis=-1)
    common["cvec"] = np.ascontiguousarray(cv.reshape(8, 128, 2).transpose(1, 0, 2))
    q = 256
    om = (1.0 / (np.float32(10000.0) ** (np.arange(q, dtype=f) / f(q)))).astype(f)
    omega = np.ascontiguousarray(om.reshape(2, 128).T)
    cidx = (np.arange(T) % 64).astype(f)[None]
    common["tau"] = np.arange(T).astype(f)[None]
    common["ident"] = np.eye(128, dtype=f)
    for k in ("w_mod", "w_in", "w_glu", "w_proj_a", "w_proj_b", "w_out", "w_ffn_gate", "w_ffn_up", "w_ffn_down"):
        common[k] = np.ascontiguousarray(np.asarray(inp[k], f))

    def pl(v):
        v = np.asarray(v, f)
        return np.ascontiguousarray(v.reshape(DEPTH, -1, 128).transpose(2, 0, 1))

    common["bmodT"] = pl(inp["b_mod"])
    common["n1gT"] = pl(inp["norm1_g"])
    common["n2gT"] = pl(inp["norm2_g"])
    common["ssmdT"] = pl(inp["ssm_d"])
    common["dwbT"] = pl(inp["dw_bias"])
    common["cngT"] = pl(inp["conv_norm_g"])
    common["cnbT"] = pl(inp["conv_norm_b"])
    common["nfgT"] = _tT(np.asarray(inp["norm_f_g"], f))
    dk = np.asarray(inp["dw_kernel"], f)
    common["dwkT"] = np.ascontiguousarray(dk.reshape(DEPTH, 31, 8, 128).transpose(3, 0, 2, 1))
    lre = np.asarray(inp["ssm_lam_re"], f); lim = np.asarray(inp["ssm_lam_im"], f)
    ls = np.asarray(inp["ssm_log_step"], f)
    bre = np.asarray(inp["ssm_b_re"], f); bim = np.asarray(inp["ssm_b_im"], f)
    cre = np.asarray(inp["ssm_c_re"], f); cim = np.asarray(inp["ssm_c_im"], f)
    lsx = np.ascontiguousarray(np.broadcast_to(ls[..., None], lre.shape))

    def pview(a):
        return a.reshape(DEPTH, 2, 16, 128).transpose(3, 0, 1, 2).reshape(128, -1)
    common["P_l"] = np.ascontiguousarray(np.stack([pview(lre), pview(lim), pview(lsx)], axis=1))

    def eview(a):
        outa = np.zeros((DEPTH, 2, 128, 4, 128), f)
        for gi in range(8):
            for qq in range(4):
                for gl in range(2):
                    g = 8 * qq + 2 * (gi // 2) + gl
                    outa[:, :, gi * 16:(gi + 1) * 16, qq, gl * 64:(gl + 1) * 64] = a[:, :, g][:, :, None, :]
        return outa.reshape(DEPTH, 2, 128, 512)
    common["E_l"] = np.ascontiguousarray(np.stack([eview(lre), eview(lim), eview(lsx)], axis=2))

    def ebuild(b):
        outa = np.zeros((DEPTH, 2, 128, 4, 128), f)
        for gi in range(8):
            for qq in range(4):
                gl = gi % 2
                outa[:, :, gi * 16:(gi + 1) * 16, qq, gl * 64:(gl + 1) * 64] = b[:, :, 8 * qq + gi].transpose(0, 1, 3, 2)
        return outa.reshape(DEPTH, 2, 128, 512)
    common["E_b"] = np.ascontiguousarray(np.stack([ebuild(bre), ebuild(bim)], axis=2))

    def cbuild(cc):
        outa = np.zeros((DEPTH, 2, 128, 16, 32), f)
        for j in range(16):
            for gl in range(2):
                outa[:, :, gl * 64:(gl + 1) * 64, j, gl * 16:(gl + 1) * 16] = cc[:, :, 2 * j + gl].transpose(0, 1, 3, 2)
        return outa.reshape(DEPTH, 2, 128, 512)
    common["P_c"] = np.ascontiguousarray(np.stack([cbuild(cre), cbuild(cim)], axis=2))
    maps = []
    for ci in range(NCORES):
        m = dict(common)
        s = np.zeros((128, 24), f)
        s[:, ci] = 1.0
        if ci > 0:
            s[:, 8 + ci - 1] = 1.0
        if ci < NCORES - 1:
            s[:, 16 + ci + 1] = 1.0
        m["sel"] = s
        maps.append(m)
    first = []
    for ci in range(NCORES):
        xs = x[ci * T:(ci + 1) * T]
        first.append({
            "xT": np.ascontiguousarray(xs.T.reshape(8, 128, T).transpose(1, 0, 2)),
            "cT": cT, "omega": omega, "cidx": cidx,
            "ridx": ((ci * T + np.arange(T)) // 64).astype(f)[None],
        })
    return maps, first


_NC = {}
_DBG = None


def _prog(stage):
    if stage not in _NC:
        _NC[stage] = build(stage)
    return _NC[stage]


def _handover(res):
    pays = np.stack([np.asarray(res.results[ci]["pay_o"]) for ci in range(NCORES)], axis=1)
    nxt = []
    for ci in range(NCORES):
        r = res.results[ci]
        nxt.append({"xsp_i": r["xsp_o"], "hsp_i": r["hsp_o"], "u_i": r["u_o"], "vbp_i": r["vbp_o"],
                    "mods_i": r["mods_o"], "ccs_i": np.ascontiguousarray(pays)})
    return nxt


def kernel(**inputs):
    maps, first = prep_inputs(inputs)
    cores = list(range(NCORES))
    res = run_bass_kernel_spmd(_prog("A0"), [dict(maps[c], **first[c]) for c in cores], core_ids=cores)
    nxt = _handover(res)
    if _DBG is not None:
        _DBG["h1"] = nxt
    res = run_bass_kernel_spmd(_prog("B0A1"), [dict(maps[c], **nxt[c]) for c in cores], core_ids=cores)
    nxt = _handover(res)
    if _DBG is not None:
        _DBG["h2"] = nxt
    res = run_bass_kernel_spmd(_prog("B1"), [dict(maps[c], **nxt[c]) for c in cores], core_ids=cores)
    outs = []
    for ci in cores:
        o = np.asarray(res.results[ci]["outT"])
        outs.append(o.transpose(2, 1, 0).reshape(T, D))
    return np.concatenate(outs, axis=0)[None].astype(np.float32)
```

```python
import math
import numpy as np
import concourse.bass as bass
import concourse.mybir as mybir
from concourse.bass_utils import run_bass_kernel_spmd

F32 = mybir.dt.float32
BF16 = mybir.dt.bfloat16
ALU = mybir.AluOpType
AF = mybir.ActivationFunctionType

NCORES = 8
D = 1024
T = 2048
TC = 256
TT = T + TC
DEPTH = 2
FFN = 2816
WIN = 4608
PI = math.pi
MAGIC = 12582912.0
BLOCKS = [(0, 512), (512, 512), (1024, 512), (1536, 512), (2048, 256)]


class Prog:
    ENG = ("sync", "act", "dve", "pool", "pe")

    def __init__(self, nc):
        self.nc = nc
        self.ops = []
        self.lw = {}
        self.rd = {}

    def _rec(self, eng, fn, r, w, dma):
        i = len(self.ops)
        deps = {}
        for b in r:
            j = self.lw.get(b)
            if j is not None:
                deps[j] = deps.get(j, False) or True
        for b in w:
            j = self.lw.get(b)
            if j is not None:
                deps[j] = True
            for j in self.rd.get(b, ()):
                if j not in deps:
                    deps[j] = False
        for b in r:
            self.rd.setdefault(b, []).append(i)
        for b in w:
            self.lw[b] = i
            self.rd[b] = []
        keep = []
        for j, strong in deps.items():
            oj = self.ops[j]
            if oj["eng"] == eng and not oj["dma"] and not dma:
                if eng == "pe":
                    continue
            keep.append(j)
        self.ops.append(dict(eng=eng, fn=fn, deps=keep, dma=dma, sig=None))
        return i

    def op(self, eng, fn, r=(), w=()):
        return self._rec(eng, fn, tuple(r), tuple(w), False)

    def dma(self, eng, out, in_, r=(), w=()):
        return self._rec(eng, lambda e: e.dma_start(out=out, in_=in_), tuple(r), tuple(w), True)

    def emit(self, ndma_sems=64):
        nc = self.nc
        waited_on = set()
        for o in self.ops:
            for j in o["deps"]:
                waited_on.add(j)
        esem = {e: nc.alloc_semaphore("s_" + e) for e in self.ENG}
        dsem = [nc.alloc_semaphore("d%d" % k) for k in range(ndma_sems)]
        ecnt = {e: 0 for e in self.ENG}
        dcnt = [0] * ndma_sems
        nd = 0
        for i, o in enumerate(self.ops):
            if (i in waited_on or o["dma"]) and o["fn"] is not None:
                if o["dma"]:
                    k = nd % ndma_sems
                    nd += 1
                    dcnt[k] += 16
                    o["sig"] = (("d", k), dsem[k], dcnt[k], 16)
                else:
                    ecnt[o["eng"]] += 1
                    o["sig"] = (("e", o["eng"]), esem[o["eng"]], ecnt[o["eng"]], 1)
        per = {e: [] for e in self.ENG}
        for o in self.ops:
            per[o["eng"]].append(o)
        ops = self.ops

        def run(eng_name, e):
            waited = {}
            for o in per[eng_name]:
                for j in sorted(o["deps"]):
                    s = ops[j]["sig"]
                    if s is None:
                        continue
                    ch, sem, val, _ = s
                    if waited.get(ch, 0) >= val:
                        continue
                    waited[ch] = val
                    e.wait_ge(sem, val)
                if o["fn"] is None:
                    continue
                if o["dma"] and o["sig"][2] > 16:
                    ch = o["sig"][0]
                    if waited.get(ch, 0) < o["sig"][2] - 16:
                        waited[ch] = o["sig"][2] - 16
                        e.wait_ge(o["sig"][1], o["sig"][2] - 16)
                ins = o["fn"](e)
                if o["sig"] is not None:
                    ins.then_inc(o["sig"][1], o["sig"][3])

        with nc.Block() as block:
            @block.sync
            def _(e):
                run("sync", e)

            @block.scalar
            def _(e):
                run("act", e)

            @block.vector
            def _(e):
                run("dve", e)

            @block.gpsimd
            def _(e):
                run("pool", e)

            @block.tensor
            def _(e):
                run("pe", e)


NPAY = 64 + 240 + 64
VP = 15 + T + 15 + 15 + TC + 15
VC0 = 15 + T + 15 + 15
HALF = TT // 3
S5_DIRS = (0, 1)


def build(stage):
    nc = bass.Bass("TRN2", target_bir_lowering=False)
    P = Prog(nc)

    def din(name, shape, dt=F32):
        return nc.dram_tensor(name, list(shape), dt, kind="ExternalInput").ap()

    def dout(name, shape, dt=F32):
        return nc.dram_tensor(name, list(shape), dt, kind="ExternalOutput").ap()

    first = stage == "A0"
    last = stage == "B1"
    if first:
        xT_d = din("xT", [128, 8, T])
        cT_d = din("cT", [128, 8, TC])
        omega_d = din("omega", [128, 2])
        ridx_d = din("ridx", [1, T])
        cidx_d = din("cidx", [1, T])
    else:
        xin_d = din("xsp_i", [128, 8, TT])
        hin_d = din("hsp_i", [128, 8, TT], BF16)
        uin_d = din("u_i", [128, 4, TT], BF16)
        vin_d = din("vbp_i", [128, 8, VP], BF16)
        ccs_d = din("ccs_i", [128, NCORES, NPAY])
        mods_i = din("mods_i", [128, 64, 2])
    cvec_d = din("cvec", [128, 8, 2])
    tau_d = din("tau", [1, T])
    sel_d = din("sel", [128, 24])
    w_mod_d = din("w_mod", [DEPTH, D, 6 * D])
    bmod_d = din("bmodT", [128, DEPTH, 48])
    n1g_d = din("n1gT", [128, DEPTH, 8])
    n2g_d = din("n2gT", [128, DEPTH, 8])
    w_in_d = din("w_in", [DEPTH, D, WIN])
    P_l_d = din("P_l", [128, 3, 64])
    P_c_d = din("P_c", [DEPTH, 2, 2, 128, 512])
    E_l_d = din("E_l", [DEPTH, 2, 3, 128, 512])
    E_b_d = din("E_b", [DEPTH, 2, 2, 128, 512])
    ssmd_d = din("ssmdT", [128, DEPTH, 4])
    w_glu_d = din("w_glu", [DEPTH, 512, 512])
    w_pa_d = din("w_proj_a", [DEPTH, 512, D])
    dwk_d = din("dwkT", [128, DEPTH, 8, 31])
    dwb_d = din("dwbT", [128, DEPTH, 8])
    cng_d = din("cngT", [128, DEPTH, 8])
    cnb_d = din("cnbT", [128, DEPTH, 8])
    w_pb_d = din("w_proj_b", [DEPTH, D, D])
    w_out_d = din("w_out", [DEPTH, D, D])
    w_fg_d = din("w_ffn_gate", [DEPTH, D, FFN])
    w_fu_d = din("w_ffn_up", [DEPTH, D, FFN])
    w_fd_d = din("w_ffn_down", [DEPTH, FFN, D])
    nfg_d = din("nfgT", [128, 8])
    ident_d = din("ident", [128, 128])
    if last:
        out_d = dout("outT", [128, 8, T])
    else:
        xsp_o = dout("xsp_o", [128, 8, TT])
        hsp_o = dout("hsp_o", [128, 8, TT], BF16)
        u_o = dout("u_o", [128, 4, TT], BF16)
        vbp_o = dout("vbp_o", [128, 8, VP], BF16)
        pay_o = dout("pay_o", [128, NPAY])
        mods_o = dout("mods_o", [128, 64, 2])

    _n = [0]

    def sb(name, shape, dt, off):
        _n[0] += 1
        return nc.alloc_sbuf_tensor_at("%s_%d" % (name, _n[0]), [128] + list(shape), dt, offset=off)

    O_PAR, O_W, O_RSTD, O_A, O_B, O_C, O_D, O_E = 16512, 28800, 41088, 50304, 124032, 160896, 179328, 217152
    END = 229376
    po = [O_PAR]

    def par(name, shape, dt=F32):
        sz = int(np.prod(shape)) * (4 if dt == F32 else 2)
        sz = (sz + 31) // 32 * 32
        t = sb(name, shape, dt, po[0])
        po[0] += sz
        assert po[0] <= O_W, po[0]
        return t

    modv = par("modv", [48, 2])
    gs1 = par("gs1", [8, 2])
    gs2 = par("gs2", [8, 2])
    bmod = par("bmod", [DEPTH, 48])
    n1g = par("n1g", [DEPTH, 8])
    n2g = par("n2g", [DEPTH, 8])
    nfg = par("nfg", [8])
    ssmd = par("ssmd", [DEPTH, 4])
    dwk = par("dwk", [DEPTH, 8, 31])
    dwb = par("dwb", [DEPTH, 8])
    cng = par("cng", [DEPTH, 8])
    cnb = par("cnb", [DEPTH, 8])
    omega = par("omega", [2])
    sel = par("sel", [24])
    csil = par("csil", [8, 2], BF16)
    cvec = par("cvec", [8, 2])
    ones_bf = par("ones_bf", [128], BF16)
    Pl = par("Pl", [3, 64])
    Pq = par("Pq", [14, 64])
    hst = par("hst", [10, 32])
    ini = par("ini", [4])
    e6 = par("e6", [8])
    halfpi = par("halfpi", [1])
    scr = par("scr", [1])
    ident = par("ident", [128], BF16)

    Wb = [sb("Wb%d" % i, [8, 384], BF16, O_W + i * 6144) for i in range(2)]
    rstd = sb("rstd", [TT], F32, O_RSTD)
    xT = sb("xT", [8, TT], F32, O_A)
    hT = sb("hT", [8, TT], BF16, O_B)
    uT = sb("uT", [4, TT], BF16, O_C)
    vbp = sb("vbp", [8, VP], BF16, O_D)
    acc0 = sb("acc0", [TT], F32, O_E)
    acc1 = sb("acc1", [TT], F32, O_B)
    o = O_A
    tau = sb("tau", [T], F32, o); o += 8192
    SN = sb("SN", [T], F32, o); o += 8192
    CS = sb("CS", [T], F32, o); o += 8192
    W1 = sb("W1", [TT], F32, o); o += 9216
    W2 = sb("W2", [TT], F32, o); o += 9216
    G1 = sb("G1", [TT], F32, o); o += 9216
    G2 = sb("G2", [TT], F32, o); o += 9216
    HHf = sb("HHf", [TT], F32, o)
    hre = sb("hre", [TT], BF16, o); o += 4608
    him = sb("him", [TT], BF16, o); o += 4608
    assert o <= O_B
    ET = [sb("ET%d" % i, [512], F32, O_A + 24576 + i * 2048) for i in range(16)]
    yacc = sb("yacc", [4, TT], F32, O_B)
    SN2 = sb("SN2", [T], F32, O_B)
    CS2 = sb("CS2", [T], F32, O_B + 8192)
    tabB = sb("tabB", [2, 2, 512], BF16, O_RSTD)
    tabC = sb("tabC", [2, 2, 512], BF16, O_RSTD + 4096)
    rmul = sb("rmul", [T], F32, O_E)
    ccs = sb("ccs", [NCORES, NPAY], F32, O_W)
    pay = sb("pay", [NPAY], F32, O_E)
    hal = sb("hal", [2, 8, 15], F32, O_E + 2048)
    ya = sb("ya", [4, TT], BF16, O_A)
    ya2 = sb("ya2", [4, TT], BF16, O_A + 18432)
    cvo = sb("cvo", [8, TT], BF16, O_A + 36864)
    gt0 = sb("gt0", [TT], F32, O_E)
    gt1 = sb("gt1", [TT], F32, O_C)
    dg = sb("dg", [31, 128], BF16, O_C + 9216)
    sqt = sb("sqt", [512], BF16, O_C + 9216 + 7936)
    mrg = sb("mrg", [8, TT], BF16, O_D)
    actT = sb("actT", [22, HALF], BF16, O_C)
    Wd = [sb("Wd%d" % i, [22, 128], BF16, O_C + 22 * HALF * 2 + i * 5632) for i in range(2)]
    assert O_C + 22 * HALF * 2 + 2 * 5632 <= O_E

    ps = [nc.alloc_psum_tensor("ps%d" % i, [128, 512], F32) for i in range(8)]

    def bkey(name, *idx):
        return (name,) + idx

    XK = [bkey("xT", k) for k in range(8)]
    ACTK = [bkey("actT", m) for m in range(22)]
    WDK = [bkey("Wd", 0), bkey("Wd", 1)]
    S5K = ["tau", "SN", "CS", "SN2", "CS2", "W1", "W2", "G1", "G2", "HH"]

    def tt(eng, out, a, b, op, r, w):
        P.op(eng, lambda e: e.tensor_tensor(out=out, in0=a, in1=b, op=op), r, w)

    def ts(eng, out, a, s1, s2, op0, op1, r, w):
        if op1 is None:
            P.op(eng, lambda e: e.tensor_scalar(out=out, in0=a, scalar1=s1, scalar2=None, op0=op0), r, w)
        else:
            P.op(eng, lambda e: e.tensor_scalar(out=out, in0=a, scalar1=s1, scalar2=s2, op0=op0, op1=op1), r, w)

    def stt(out, a, s, b, op0, op1, r, w):
        P.op("dve", lambda e: e.scalar_tensor_tensor(out=out, in0=a, scalar=s, in1=b, op0=op0, op1=op1), r, w)

    def act(out, a, func, r, w, bias=None, scale=None):
        kw = {}
        if bias is not None:
            kw["bias"] = bias
        if scale is not None:
            kw["scale"] = scale
        P.op("act", lambda e: e.activation(out=out, in_=a, func=func, **kw), r, w)

    def recip(out, a, r, w):
        P.op("dve", lambda e: e.reciprocal(out=out, in_=a), r, w)

    def cp(eng, out, a, r, w):
        P.op(eng, lambda e: e.tensor_copy(out=out, in_=a), r, w)

    def mset(eng, out, val, w):
        P.op(eng, lambda e: e.memset(out, val), (), w)

    def mm(out, lhsT, rhs, start, stop, r, w, tp=None):
        if tp is None:
            P.op("pe", lambda e: e.matmul(out, lhsT=lhsT, rhs=rhs, start=start, stop=stop), r, w)
        else:
            P.op("pe", lambda e: e.matmul(out, lhsT=lhsT, rhs=rhs, start=start, stop=stop, tile_position=tp), r, w)

    def scan(out, d0, d1, init, r, w):
        P.op("dve", lambda e: e.tensor_tensor_scan(out=out, data0=d0, data1=d1, initial=init, op0=ALU.mult,
                                                   op1=ALU.add), r, w)

    def barrier(keys):
        P.op("dve", lambda e: e.memset(scr[:], 0.0), (), list(keys))

    def ld(out, in_, r=(), w=(), eng="sync"):
        P.dma(eng, out, in_, r, w)

    WK = [bkey("Wb", 0), bkey("Wb", 1)]

    plist = [(bmod, bmod_d), (n1g, n1g_d), (n2g, n2g_d), (nfg, nfg_d), (ssmd, ssmd_d), (dwk, dwk_d),
             (dwb, dwb_d), (cng, cng_d), (cnb, cnb_d), (sel, sel_d), (cvec, cvec_d)]
    if first:
        plist.append((omega, omega_d))
    for t_, d_ in plist:
        ld(t_[:], d_, w=["par"])
    ld(Pl[:], P_l_d, w=["Pl"])
    P.dma("pool", ident[:], ident_d, w=["par"])
    mset("dve", ones_bf[:], 1.0, ["par"])
    mset("dve", halfpi[:], PI / 2, ["par"])
    mset("dve", e6[:], 1e-6, ["par"])
    act(csil[:], cvec[:], AF.Silu, ["par"], ["csil"])

    def sincos(eng, s_out, c_out, u, tmp, r, w):
        ts(eng, tmp, u, MAGIC, None, ALU.add, None, r, w)
        ts(eng, tmp, tmp, -MAGIC, None, ALU.add, None, w, w)
        tt(eng, tmp, u, tmp, ALU.subtract, list(r) + list(w), w)
        act(c_out, tmp, AF.Abs, w, w)
        act(c_out, c_out, AF.Sin, list(w) + ["par"], w, bias=halfpi[:, 0:1], scale=-2 * PI)
        act(s_out, tmp, AF.Sin, w, w, scale=2 * PI)

    def disc(lre, lim, ls, r_o, cth, sth, dt_o, th_o, tmp, rk, wk):
        act(dt_o, ls, AF.Exp, rk, wk)
        tt("dve", r_o, lre, dt_o, ALU.mult, list(rk) + list(wk), wk)
        tt("dve", th_o, lim, dt_o, ALU.mult, list(rk) + list(wk), wk)
        ts("dve", th_o, th_o, 1.0 / (2 * PI), None, ALU.mult, None, wk, wk)
        sincos("dve", sth, cth, th_o, tmp, wk, wk)

    PQ = lambda i: Pq[:, i, :]
    disc(Pl[:, 0, :], Pl[:, 1, :], Pl[:, 2, :], PQ(7), PQ(2), PQ(3), PQ(6), PQ(1), PQ(8), ["Pl", "par"], ["Pq"])
    act(PQ(0), PQ(7), AF.Exp, ["Pq"], ["Pq"])
    act(PQ(9), PQ(7), AF.Exp, ["Pq"], ["Pq"], scale=float(T))
    ts("dve", PQ(10), PQ(1), float(T), None, ALU.mult, None, ["Pq"], ["Pq"])
    sincos("dve", PQ(11), PQ(12), PQ(10), PQ(8), ["Pq"], ["Pq"])
    tt("dve", PQ(4), PQ(9), PQ(12), ALU.mult, ["Pq"], ["Pq"])
    tt("dve", PQ(5), PQ(9), PQ(11), ALU.mult, ["Pq"], ["Pq"])

    if first:
        for k in range(8):
            ld(xT[:, k, 0:T], xT_d[:, k, :], w=[XK[k]])
            ld(xT[:, k, T:TT], cT_d[:, k, :], w=[XK[k]])
        for k in range(8):
            idx_d = ridx_d if k < 4 else cidx_d
            if k in (0, 4):
                ld(acc0[:, 0:T], idx_d[0:1, :].partition_broadcast(128), w=["acc0"])
            ph = (PI / 2 if (k // 2) % 2 == 1 else 0.0)
            ts("dve", acc1[:, 0:T], acc0[:, 0:T], omega[:, k % 2:k % 2 + 1], ph, ALU.mult, ALU.add,
               ["acc0", "par"], ["hT"])
            ts("dve", rstd[:, 0:T], acc1[:, 0:T], 1.0 / (2 * PI), MAGIC, ALU.mult, ALU.add, ["hT"], ["rstd"])
            ts("dve", rstd[:, 0:T], rstd[:, 0:T], -MAGIC, None, ALU.add, None, ["rstd"], ["rstd"])
            stt(acc1[:, 0:T], rstd[:, 0:T], -2 * PI, acc1[:, 0:T], ALU.mult, ALU.add, ["rstd", "hT"], ["hT"])
            act(acc1[:, 0:T], acc1[:, 0:T], AF.Sin, ["hT", "par"], ["hT"])
            tt("dve", xT[:, k, 0:T], xT[:, k, 0:T], acc1[:, 0:T], ALU.add, ["hT", XK[k]], [XK[k]])

    wslot = [0]

    def wload(Wd_ap, KT, c0, cw, wbufs=None, wkey="Wb"):
        wbufs = wbufs or Wb
        s = wslot[0] % 2
        wslot[0] += 1
        wb = wbufs[s]
        src = Wd_ap[:, c0:c0 + cw].rearrange("(k p) f -> p k f", p=128)
        P.dma("pool", wb[:, 0:KT, 0:cw], src, w=[bkey(wkey, s)])
        return wb, bkey(wkey, s)

    pscnt = [0]

    def stream_proj(Wd_ap, KT, ncols, inT, inkeys, evac, col0=0, blocks=BLOCKS, chunk=384):
        nchunk = (ncols + chunk - 1) // chunk
        for c in range(nchunk):
            cw = min(chunk, ncols - c * chunk)
            wb, wk = wload(Wd_ap, KT, col0 + c * chunk, cw)
            for mi in range(cw // 128):
                m = c * (chunk // 128) + mi
                for bi, (b0, bn) in enumerate(blocks):
                    pi = pscnt[0] % 4
                    pscnt[0] += 1
                    for k in range(KT):
                        mm(ps[pi][:, 0:bn], wb[:, k, mi * 128:(mi + 1) * 128], inT[:, k, b0:b0 + bn],
                           k == 0, k == KT - 1, [wk] + list(inkeys), [bkey("ps", pi)])
                    evac(m, bi, b0, bn, ps[pi][:, 0:bn], bkey("ps", pi))

    def rms_stats():
        for bi, (b0, bn) in enumerate(BLOCKS):
            for k in range(8):
                act(hT[:, k, b0:b0 + bn], xT[:, k, b0:b0 + bn], AF.Square, [XK[k]], ["hT"])
            for k in range(8):
                mm(ps[4][:, 0:bn], ones_bf[:, :], hT[:, k, b0:b0 + bn], k == 0, k == 7, ["hT", "par"], [bkey("ps", 4)])
            act(rstd[:, b0:b0 + bn], ps[4][:, 0:bn], AF.Sqrt, [bkey("ps", 4), "par"], ["rstd"], bias=e6[:, 0:1],
                scale=1.0 / D)
            recip(rstd[:, b0:b0 + bn], rstd[:, b0:b0 + bn], ["rstd"], ["rstd"])

    def modulate(gsv, shoff):
        for k in range(8):
            for (c0, c1, mc) in ((0, T, 0), (T, TT, 1)):
                tt("dve", acc0[:, c0:c1], xT[:, k, c0:c1], rstd[:, c0:c1], ALU.mult, [XK[k], "rstd"], ["acc0"])
                act(hT[:, k, c0:c1], acc0[:, c0:c1], AF.Identity, ["acc0", "modv"], ["hT"],
                    bias=modv[:, shoff + k, mc:mc + 1], scale=gsv[:, k, mc:mc + 1])

    def mod_vectors(l):
        for c in range(16):
            wb, wk = wload(w_mod_d[l], 8, c * 384, 384)
            for mi in range(3):
                m = c * 3 + mi
                for k in range(8):
                    mm(ps[5][:, 2 * m:2 * m + 2], wb[:, k, mi * 128:(mi + 1) * 128], csil[:, k, :], k == 0, k == 7,
                       [wk, "csil"], [bkey("ps", 5)])
        for c2 in range(2):
            tt("dve", modv[:, :, c2], ps[5][:, c2:96:2], bmod[:, l, :], ALU.add, [bkey("ps", 5), "par"], ["modv"])
        for (gsv, ng, off) in ((gs1, n1g, 8), (gs2, n2g, 32)):
            for c2 in range(2):
                stt(gsv[:, :, c2], modv[:, off:off + 8, c2], 1.0, ng[:, l, :], ALU.add, ALU.mult,
                    ["modv", "par"], ["modv"])

    def s5_tables(l):
        barrier(["rstd", "tabB", "tabC", "W1", "W2", "G1", "G2", "ET"])
        for d in S5_DIRS:
            lre, lim, ls, bre, bim = ET[0], ET[1], ET[2], ET[3], ET[4]
            for i, tdst in enumerate((lre, lim, ls)):
                ld(tdst[:], E_l_d[l, d, i], w=["ET"])
            for i, tdst in enumerate((bre, bim)):
                ld(tdst[:], E_b_d[l, d, i], w=["ET"])
            r_, cth, sth, dt_, th_, tmp = ET[5], ET[6], ET[7], ET[8], ET[9], ET[10]
            disc(lre[:], lim[:], ls[:], r_[:], cth[:], sth[:], dt_[:], th_[:], tmp[:], ["ET", "par"], ["ET"])
            act(r_[:], r_[:], AF.Exp, ["ET"], ["ET"])
            A_, Bn, den, fre, fim, t1 = ET[11], ET[12], ET[13], ET[14], ET[15], ET[8]
            tt("dve", A_[:], r_[:], cth[:], ALU.mult, ["ET"], ["ET"])
            ts("dve", A_[:], A_[:], -1.0, None, ALU.add, None, ["ET"], ["ET"])
            tt("dve", Bn[:], r_[:], sth[:], ALU.mult, ["ET"], ["ET"])
            tt("dve", den[:], lre[:], lre[:], ALU.mult, ["ET"], ["ET"])
            tt("dve", t1[:], lim[:], lim[:], ALU.mult, ["ET"], ["ET"])
            tt("dve", den[:], den[:], t1[:], ALU.add, ["ET"], ["ET"])
            recip(den[:], den[:], ["ET"], ["ET"])
            tt("dve", fre[:], A_[:], lre[:], ALU.mult, ["ET"], ["ET"])
            tt("dve", t1[:], Bn[:], lim[:], ALU.mult, ["ET"], ["ET"])
            tt("dve", fre[:], fre[:], t1[:], ALU.add, ["ET"], ["ET"])
            tt("dve", fre[:], fre[:], den[:], ALU.mult, ["ET"], ["ET"])
            tt("dve", fim[:], Bn[:], lre[:], ALU.mult, ["ET"], ["ET"])
            tt("dve", t1[:], A_[:], lim[:], ALU.mult, ["ET"], ["ET"])
            tt("dve", fim[:], fim[:], t1[:], ALU.subtract, ["ET"], ["ET"])
            tt("dve", fim[:], fim[:], den[:], ALU.mult, ["ET"], ["ET"])
            t2 = ET[9]
            tt("dve", t1[:], fre[:], bre[:], ALU.mult, ["ET"], ["ET"])
            tt("dve", t2[:], fim[:], bim[:], ALU.mult, ["ET"], ["ET"])
            tt("dve", tabB[:, d, 0, :], t1[:], t2[:], ALU.subtract, ["ET"], ["tabB"])
            tt("dve", t1[:], fre[:], bim[:], ALU.mult, ["ET"], ["ET"])
            tt("dve", t2[:], fim[:], bre[:], ALU.mult, ["ET"], ["ET"])
            tt("dve", tabB[:, d, 1, :], t1[:], t2[:], ALU.add, ["ET"], ["tabB"])
            ld(ET[0][:], P_c_d[l, d, 0], w=["ET"])
            ld(ET[1][:], P_c_d[l, d, 1], w=["ET"])
            act(tabC[:, d, 0, :], ET[0][:], AF.Copy, ["ET"], ["tabC"])
            act(tabC[:, d, 1, :], ET[1][:], AF.Copy, ["ET"], ["tabC"], scale=-1.0)
        barrier(["W1", "W2", "G1", "G2", "ET"])

    def rot(o1, o2, x1, x2, tmp, cs, sn, r, w, tkey):
        tt("pool", o1, x1, cs, ALU.mult, r, w)
        tt("dve", tmp, x2, sn, ALU.mult, r, [tkey])
        tt("dve", o1, o1, tmp, ALU.add, list(w) + [tkey], w)
        tt("pool", o2, x1, sn, ALU.mult, list(r) + list(w) + [tkey], w)
        tt("dve", tmp, x2, cs, ALU.mult, list(r) + list(w), [tkey])
        tt("pool", o2, o2, tmp, ALU.subtract, list(w) + [tkey], w)

    SEGS = ((T, TT, TC), (0, T, T))

    tilecnt = [0]

    def s5_tile(l, d, j, phase):
        par_ = tilecnt[0] % 2 if phase == 1 else 0
        tilecnt[0] += 1
        SN_, CS_ = (SN, CS) if par_ == 0 else (SN2, CS2)
        snk, csk = ("SN", "CS") if par_ == 0 else ("SN2", "CS2")
        return _s5_tile(l, d, j, phase, SN_, CS_, snk, csk)

    def _s5_tile(l, d, j, phase, SN, CS, snk, csk):
        c = (l * 2 + d) * 16 + j
        dj = d * 16 + j
        q, a = j // 4, j % 4
        pcol = lambda i: Pq[:, i, c:c + 1]
        rev = d == 1

        def tab(t_, n):
            return t_[:, n - 1::-1] if rev and n > 1 else (t_[:, 0:n] if not rev else t_[:, 0:1])

        def tabn(t_, n):
            if not rev:
                return t_[:, 0:n]
            return t_[:, n - 1::-1] if n > 1 else t_[:, 0:1]

        ts("pool", SN[:], tau[:], pcol(1), MAGIC, ALU.mult, ALU.add, ["tau", "Pq"], [snk])
        ts("pool", SN[:], SN[:], 1.0, -MAGIC, ALU.mult, ALU.add, [snk], [snk])
        stt(SN[:], tau[:], pcol(1), SN[:], ALU.mult, ALU.subtract, ["tau", "Pq", snk], [snk])
        act(CS[:], SN[:], AF.Abs, [snk], [csk])
        act(CS[:], CS[:], AF.Sin, [csk, "par"], [csk], bias=halfpi[:, 0:1], scale=-2 * PI)
        act(SN[:], SN[:], AF.Sin, [snk], [snk], scale=2 * PI)
        ts("pool", rmul[:], tau[:], 0.0, pcol(0), ALU.mult, ALU.add, ["tau", "Pq"], ["acc0"])
        for bi, (b0, bn) in enumerate(BLOCKS):
            p0 = 2 * (bi % 2)
            for ri in range(2):
                mm(ps[p0 + ri][:, 0:bn], tabB[32 * a:32 * a + 32, d, ri, q * 128:(q + 1) * 128],
                   uT[32 * a:32 * a + 32, q, b0:b0 + bn], True, True, ["tabB", "uT", "rstd"], [bkey("ps", p0 + ri)],
                   tp=(32 * a, 0))
            act(G1[:, b0:b0 + bn], ps[p0][:, 0:bn], AF.Copy, [bkey("ps", p0)], ["G1"])
            act(G2[:, b0:b0 + bn], ps[p0 + 1][:, 0:bn], AF.Copy, [bkey("ps", p0 + 1)], ["G2"])
        for (c0, c1, n) in SEGS:
            rot(W1[:, c0:c1], W2[:, c0:c1], G1[:, c0:c1], G2[:, c0:c1], HHf[:, c0:c1], tabn(CS, n), tabn(SN, n),
                ["G1", "G2", csk, snk], ["W1", "W2"], "HH")
        if phase == 2:
            hr, hi_ = hst[:, 4, dj:dj + 1], hst[:, 5, dj:dj + 1]
            tt("dve", ini[:, 0:1], hr, pcol(2), ALU.mult, ["hst", "Pq"], ["ini"])
            tt("dve", ini[:, 2:3], hi_, pcol(3), ALU.mult, ["hst", "Pq"], ["ini"])
            tt("dve", ini[:, 0:1], ini[:, 0:1], ini[:, 2:3], ALU.subtract, ["ini"], ["ini"])
            tt("dve", ini[:, 1:2], hr, pcol(3), ALU.mult, ["hst", "Pq", "ini"], ["ini"])
            tt("dve", ini[:, 2:3], hi_, pcol(2), ALU.mult, ["hst", "Pq", "ini"], ["ini"])
            tt("dve", ini[:, 1:2], ini[:, 1:2], ini[:, 2:3], ALU.add, ["ini"], ["ini"])
            ts("dve", ini[:, 1:2], ini[:, 1:2], -1.0, None, ALU.mult, None, ["ini"], ["ini"])
        for (c0, c1, n) in SEGS:
            for gi, (Gx, Wx, gk, wk) in enumerate(((G1, W1, "G1", "W1"), (G2, W2, "G2", "W2"))):
                if phase == 2 and c0 == 0:
                    init = ini[:, gi:gi + 1]
                else:
                    init = 0.0
                if not rev:
                    scan(Gx[:, c0:c1], rmul[:, 0:n], Wx[:, c0:c1], init, [wk, "acc0", "ini"], [gk])
                else:
                    lo = c0 - 1 if c0 > 0 else None
                    scan(Gx[:, c1 - 1:lo:-1], rmul[:, 0:n], Wx[:, c1 - 1:lo:-1], init, [wk, "acc0", "ini"], [gk])
        if phase == 1:
            for (c0, c1, n), row in zip(SEGS, (0, 2)):
                ecol = c1 - 1 if not rev else c0
                tcol = n - 1
                g1, g2 = G1[:, ecol:ecol + 1], G2[:, ecol:ecol + 1]
                cs_, sn_ = CS[:, tcol:tcol + 1], SN[:, tcol:tcol + 1]
                hr, hi_ = hst[:, row, dj:dj + 1], hst[:, row + 1, dj:dj + 1]
                tt("dve", hr, g1, cs_, ALU.mult, ["G1", csk], ["hst"])
                tt("dve", ini[:, 3:4], g2, sn_, ALU.mult, ["G2", snk], ["ini"])
                tt("dve", hr, hr, ini[:, 3:4], ALU.add, ["hst", "ini"], ["hst"])
                tt("dve", hi_, g1, sn_, ALU.mult, ["G1", snk], ["hst"])
                tt("dve", ini[:, 3:4], g2, cs_, ALU.mult, ["G2", csk, "hst"], ["ini"])
                tt("dve", hi_, hi_, ini[:, 3:4], ALU.subtract, ["hst", "ini"], ["hst"])
            return
        for (c0, c1, n) in SEGS:
            cs_, sn_ = tabn(CS, n), tabn(SN, n)
            tt("pool", W1[:, c0:c1], G1[:, c0:c1], cs_, ALU.mult, ["G1", csk], ["W1"])
            tt("dve", W2[:, c0:c1], G2[:, c0:c1], sn_, ALU.mult, ["G2", snk], ["W2"])
            tt("dve", hre[:, c0:c1], W1[:, c0:c1], W2[:, c0:c1], ALU.add, ["W1", "W2"], ["HH"])
            tt("pool", W1[:, c0:c1], G1[:, c0:c1], sn_, ALU.mult, ["G1", snk, "HH"], ["W1"])
            tt("dve", W2[:, c0:c1], G2[:, c0:c1], cs_, ALU.mult, ["G2", csk, "HH"], ["W2"])
            tt("pool", him[:, c0:c1], W1[:, c0:c1], W2[:, c0:c1], ALU.subtract, ["W1", "W2"], ["HH"])
        for bi, (b0, bn) in enumerate(BLOCKS):
            pi = 4 + bi % 2
            mm(ps[pi][32 * a:32 * a + 32, 0:bn], tabC[:, d, 0, j * 32:(j + 1) * 32], hre[:, b0:b0 + bn], True, False,
               ["tabC", "HH", "rstd"], [bkey("ps", pi)], tp=(0, 32 * a))
            mm(ps[pi][32 * a:32 * a + 32, 0:bn], tabC[:, d, 1, j * 32:(j + 1) * 32], him[:, b0:b0 + bn], False, True,
               ["tabC", "HH", "rstd"], [bkey("ps", pi)], tp=(0, 32 * a))
            ydst = yacc[32 * a:32 * a + 32, q, b0:b0 + bn]
            if d == S5_DIRS[0]:
                cp("dve", ydst, ps[pi][32 * a:32 * a + 32, 0:bn], [bkey("ps", pi)], ["yacc"])
            else:
                tt("dve", ydst, ydst, ps[pi][32 * a:32 * a + 32, 0:bn], ALU.add, [bkey("ps", pi), "yacc"], ["yacc"])

    def layer_A(l):
        mod_vectors(l)
        rms_stats()
        modulate(gs1, 0)
        def ev_u(m, bi, b0, bn, p_, pk):
            act(uT[:, m, b0:b0 + bn], p_, AF.Copy, [pk], ["uT"])
        stream_proj(w_in_d[l], 8, 512, hT, ["hT"], ev_u)
        mset("pool", vbp[:, :, 0:15], 0.0, ["vbp"])
        mset("pool", vbp[:, :, 15 + T:VC0], 0.0, ["vbp"])
        mset("pool", vbp[:, :, VC0 + TC:VP], 0.0, ["vbp"])
        for m in range(8):
            wa, wka = wload(w_in_d[l], 8, 512 + m * 128, 128)
            wg, wkg = wload(w_in_d[l], 8, 1536 + m * 128, 128)
            for bi, (b0, bn) in enumerate(BLOCKS):
                pa, pg = (bi % 2) * 2, (bi % 2) * 2 + 1
                for k in range(8):
                    mm(ps[pg][:, 0:bn], wg[:, k, 0:128], hT[:, k, b0:b0 + bn], k == 0, k == 7, [wkg, "hT"], [bkey("ps", pg)])
                for k in range(8):
                    mm(ps[pa][:, 0:bn], wa[:, k, 0:128], hT[:, k, b0:b0 + bn], k == 0, k == 7, [wka, "hT"], [bkey("ps", pa)])
                act(acc0[:, 0:bn], ps[pg][:, 0:bn], AF.Sigmoid, [bkey("ps", pg)], ["acc0"])
                vc = 15 + b0 if b0 < T else VC0
                tt("dve", vbp[:, m, vc:vc + bn], ps[pa][:, 0:bn], acc0[:, 0:bn], ALU.mult, [bkey("ps", pa), "acc0"], ["vbp"])
        for k in range(8):
            ld(xsp_o[:, k, :], xT[:, k, :], r=[XK[k]], w=["xsp"])
        ld(hsp_o[:, :, :], hT[:, :, :], r=["hT"], w=["hsp"])
        ld(u_o[:, :, :], uT[:, :, :], r=["uT"], w=["uo"])
        barrier(XK + S5K + ["hT", "yacc"])
        ld(tau[:], tau_d[0:1, :].partition_broadcast(128), w=["tau"])
        s5_tables(l)
        for j in range(16):
            for d in S5_DIRS:
                s5_tile(l, d, j, 1)
        barrier(["acc0", "pay"])
        mset("dve", pay[:], 0.0, ["pay"])
        cp("dve", pay[:, 0:32], hst[:, 2, :], ["hst"], ["pay"])
        cp("dve", pay[:, 32:64], hst[:, 3, :], ["hst"], ["pay"])
        cp("dve", pay[:, 304:336], hst[:, 0, :], ["hst"], ["pay"])
        cp("dve", pay[:, 336:368], hst[:, 1, :], ["hst"], ["pay"])
        pe_ = pay[:, 64:304].rearrange("p (q s f) -> p q s f", q=8, s=2)
        cp("dve", pe_[:, :, 0, :], vbp[:, :, 15:30], ["vbp"], ["pay"])
        cp("dve", pe_[:, :, 1, :], vbp[:, :, 15 + T - 15:15 + T], ["vbp"], ["pay"])
        ld(pay_o[:, :], pay[:], r=["pay"], w=["payo"])
        ld(mods_o[:, 0:48, :], modv[:], r=["modv"], w=["payo"])
        ld(mods_o[:, 48:56, :], gs1[:], r=["modv"], w=["payo"])
        ld(mods_o[:, 56:64, :], gs2[:], r=["modv"], w=["payo"])
        ld(vbp_o[:, :, :], vbp[:, :, :], r=["vbp"], w=["vo"])
        P.op("sync", None, r=["xsp", "hsp", "uo", "payo", "vo"], w=["doneA"])

    def layer_B(l, x_src):
        ld(modv[:], mods_i[:, 0:48, :], w=["modv"])
        ld(gs1[:], mods_i[:, 48:56, :], w=["modv"])
        ld(gs2[:], mods_i[:, 56:64, :], w=["modv"])
        ld(uT[:, :, :], uin_d[:, :, :], w=["uT"])
        ld(vbp[:, :, :], vin_d[:, :, :], w=["vbp"])
        P.dma("sync", ccs[:, :, :], ccs_d[:, :, :], w=WK + ["ccs"])
        ld(tau[:], tau_d[0:1, :].partition_broadcast(128), w=["tau"])
        HR, HI, T1, T2 = hst[:, 6, 0:16], hst[:, 7, 0:16], hst[:, 8, 0:16], hst[:, 9, 0:16]
        for d in S5_DIRS:
            c0 = (l * 2 + d) * 16
            LR, LI = Pq[:, 4, c0:c0 + 16], Pq[:, 5, c0:c0 + 16]
            SR, SI = hst[:, 4, d * 16:(d + 1) * 16], hst[:, 5, d * 16:(d + 1) * 16]
            cp("dve", HR, ccs[:, 0, 304 + d * 16:304 + d * 16 + 16], ["ccs"], ["hst"])
            cp("dve", HI, ccs[:, 0, 336 + d * 16:336 + d * 16 + 16], ["ccs"], ["hst"])
            order = list(range(NCORES)) if d == 0 else list(range(NCORES - 1, -1, -1))
            for n_, ci in enumerate(order):
                sc_ = sel[:, ci:ci + 1]
                if n_ == 0:
                    ts("dve", SR, HR, sc_, None, ALU.mult, None, ["hst", "par"], ["hst"])
                    ts("dve", SI, HI, sc_, None, ALU.mult, None, ["hst", "par"], ["hst"])
                else:
                    stt(SR, HR, sc_, SR, ALU.mult, ALU.add, ["hst", "par"], ["hst"])
                    stt(SI, HI, sc_, SI, ALU.mult, ALU.add, ["hst", "par"], ["hst"])
                if n_ == NCORES - 1:
                    break
                lr_, li_ = ccs[:, ci, d * 16:d * 16 + 16], ccs[:, ci, 32 + d * 16:32 + d * 16 + 16]
                tt("dve", T1, LR, HR, ALU.mult, ["hst", "Pq"], ["hst"])
                tt("dve", T2, LI, HI, ALU.mult, ["hst", "Pq"], ["hst"])
                tt("dve", T1, T1, T2, ALU.subtract, ["hst"], ["hst"])
                tt("dve", T2, LI, HR, ALU.mult, ["hst", "Pq"], ["hst"])
                tt("dve", HR, T1, lr_, ALU.add, ["hst", "ccs"], ["hst"])
                tt("dve", T1, LR, HI, ALU.mult, ["hst", "Pq"], ["hst"])
                tt("dve", T1, T1, T2, ALU.add, ["hst"], ["hst"])
                tt("dve", HI, T1, li_, ALU.add, ["hst", "ccs"], ["hst"])
        barrier(["acc0", "hal"])
        for side, so in ((0, 16), (1, 8)):
            hs = hal[:, side, :, :]
            for ci in range(NCORES):
                e_ = ccs[:, ci, 64:304].rearrange("p (q s f) -> p q s f", q=8, s=2)[:, :, 1 - side, :]
                sc_ = sel[:, 8 + side * 8 + ci: 8 + side * 8 + ci + 1]
                if ci == 0:
                    ts("dve", hs, e_, sc_, None, ALU.mult, None, ["ccs", "par"], ["hal"])
                else:
                    stt(hs, e_, sc_, hs, ALU.mult, ALU.add, ["ccs", "par", "hal"], ["hal"])
        cp("dve", vbp[:, :, 0:15], hal[:, 0, :, :], ["hal"], ["vbp"])
        cp("dve", vbp[:, :, 15 + T:15 + T + 15], hal[:, 1, :, :], ["hal"], ["vbp"])
        barrier(S5K + ["yacc", "hT", "hal", "acc0", "pay"] + XK)
        s5_tables(l)
        for j in range(16):
            for d in S5_DIRS:
                s5_tile(l, d, j, 2)
        barrier(S5K + ["ya", "ya2", "cvo", "acc0", "gt0", "tabB", "tabC", "rstd"] + XK)
        gtmp = sb("gtmp", [TT], F32, O_A + 36864)
        for q in range(4):
            stt(gt0[:], uT[:, q, :], ssmd[:, l, q:q + 1], yacc[:, q, :], ALU.mult, ALU.add, ["uT", "par", "yacc"], ["gt0"])
            act(gtmp[:], gt0[:], AF.Square, ["gt0"], ["cvo"])
            ts("pool", gtmp[:], gtmp[:], 0.044715, 1.0, ALU.mult, ALU.add, ["cvo"], ["cvo"])
            tt("pool", gtmp[:], gtmp[:], gt0[:], ALU.mult, ["cvo", "gt0"], ["cvo"])
            act(gtmp[:], gtmp[:], AF.Sigmoid, ["cvo"], ["cvo"], scale=1.5957691216057308)
            tt("dve", ya[:, q, :], gt0[:], gtmp[:], ALU.mult, ["gt0", "cvo"], ["ya"])

        def ev_glu(m, bi, b0, bn, p_, pk):
            act(gt0[:, 0:bn], p_, AF.Sigmoid, [pk], ["gt0"])
            tt("dve", ya2[:, m, b0:b0 + bn], ya[:, m, b0:b0 + bn], gt0[:, 0:bn], ALU.mult, ["ya", "gt0"], ["ya2"])
        stream_proj(w_glu_d[l], 4, 512, ya, ["ya"], ev_glu)
        barrier(["uT", "dg", "gt1", "sqt"])
        for q in range(8):
            for jt in range(31):
                ts("pool", dg[:, jt, :], ident[:, :], dwk[:, l, q, jt:jt + 1], 0.0, ALU.mult, ALU.add, ["par"], ["dg"])
            for (o0, v0, n) in ((0, 0, 512), (512, 512, 512), (1024, 1024, 512), (1536, 1536, 512), (T, VC0 - 15, TC)):
                pi = pscnt[0] % 4
                pscnt[0] += 1
                for jt in range(31):
                    mm(ps[pi][:, 0:n], dg[:, jt, :], vbp[:, q, v0 + jt:v0 + jt + n], jt == 0, jt == 30, ["dg", "vbp"],
                       [bkey("ps", pi)])
                act(cvo[:, q, o0:o0 + n], ps[pi][:, 0:n], AF.Identity, [bkey("ps", pi), "par"], ["cvo"],
                    bias=dwb[:, l, q:q + 1])
        for bi, (b0, bn) in enumerate(BLOCKS):
            for q in range(8):
                act(sqt[:, 0:bn], cvo[:, q, b0:b0 + bn], AF.Square, ["cvo"], ["sqt"])
                mm(ps[4][:, 0:bn], ones_bf[:, :], cvo[:, q, b0:b0 + bn], q == 0, q == 7, ["cvo", "par"], [bkey("ps", 4)])
                mm(ps[5][:, 0:bn], ones_bf[:, :], sqt[:, 0:bn], q == 0, q == 7, ["sqt", "par"], [bkey("ps", 5)])
            mean, rs_ = gt0[:, 0:bn], gt0[:, 512:512 + bn]
            ts("dve", mean, ps[4][:, 0:bn], 1.0 / D, None, ALU.mult, None, [bkey("ps", 4)], ["gt0"])
            tt("dve", rs_, mean, mean, ALU.mult, ["gt0"], ["gt0"])
            stt(rs_, ps[5][:, 0:bn], 1.0 / D, rs_, ALU.mult, ALU.subtract, [bkey("ps", 5), "gt0"], ["gt0"])
            act(rs_, rs_, AF.Sqrt, ["gt0", "par"], ["gt0"], bias=e6[:, 0:1])
            recip(rs_, rs_, ["gt0"], ["gt0"])
            for q in range(8):
                t_ = gt0[:, 1024:1024 + bn]
                tt("dve", t_, cvo[:, q, b0:b0 + bn], mean, ALU.subtract, ["cvo", "gt0"], ["gt0"])
                tt("dve", t_, t_, rs_, ALU.mult, ["gt0"], ["gt0"])
                act(cvo[:, q, b0:b0 + bn], t_, AF.Silu, ["gt0", "par"], ["cvo"], bias=cnb[:, l, q:q + 1],
                    scale=cng[:, l, q:q + 1])
        barrier(["yacc", "hT", "mrg", "vbp"])
        ld(hT[:, :, :], hin_d[:, :, :], w=["hT"])
        for m in range(8):
            wga, kga = wload(w_in_d[l], 8, 2560 + m * 128, 128)
            wgb, kgb = wload(w_in_d[l], 8, 3584 + m * 128, 128)
            for bi, (b0, bn) in enumerate(BLOCKS):
                for k in range(8):
                    mm(ps[0][:, 0:bn], wga[:, k, 0:128], hT[:, k, b0:b0 + bn], k == 0, k == 7, [kga, "hT"], [bkey("ps", 0)])
                for k in range(8):
                    mm(ps[1][:, 0:bn], wgb[:, k, 0:128], hT[:, k, b0:b0 + bn], k == 0, k == 7, [kgb, "hT"], [bkey("ps", 1)])
                act(gt0[:, b0:b0 + bn], ps[0][:, 0:bn], AF.Sigmoid, [bkey("ps", 0)], ["gt0"])
                act(gt1[:, b0:b0 + bn], ps[1][:, 0:bn], AF.Sigmoid, [bkey("ps", 1)], ["gt1"])
            wpa, kpa = wload(w_pa_d[l], 4, m * 128, 128)
            wpb, kpb = wload(w_pb_d[l], 8, m * 128, 128)
            for bi, (b0, bn) in enumerate(BLOCKS):
                for k in range(4):
                    mm(ps[2][:, 0:bn], wpa[:, k, 0:128], ya2[:, k, b0:b0 + bn], k == 0, k == 3, [kpa, "ya2"], [bkey("ps", 2)])
                for k in range(8):
                    mm(ps[3][:, 0:bn], wpb[:, k, 0:128], cvo[:, k, b0:b0 + bn], k == 0, k == 7, [kpb, "cvo"], [bkey("ps", 3)])
                tt("dve", gt0[:, b0:b0 + bn], gt0[:, b0:b0 + bn], ps[2][:, 0:bn], ALU.mult, ["gt0", bkey("ps", 2)], ["gt0"])
                tt("dve", gt1[:, b0:b0 + bn], gt1[:, b0:b0 + bn], ps[3][:, 0:bn], ALU.mult, ["gt1", bkey("ps", 3)], ["gt1"])
                tt("pool", mrg[:, m, b0:b0 + bn], gt0[:, b0:b0 + bn], gt1[:, b0:b0 + bn], ALU.add, ["gt0", "gt1"], ["mrg"])
        barrier(["ya", "ya2", "cvo"] + XK)
        for k in range(8):
            ld(xT[:, k, :], x_src[:, k, :], w=[XK[k]])

        def ev_out(m, bi, b0, bn, p_, pk):
            mc = 1 if b0 >= T else 0
            stt(xT[:, m, b0:b0 + bn], p_, modv[:, 16 + m, mc:mc + 1], xT[:, m, b0:b0 + bn], ALU.mult, ALU.add,
                [pk, "modv", XK[m]], [XK[m]])
        stream_proj(w_out_d[l], 8, D, mrg, ["mrg"], ev_out)
        barrier(["acc0", "gt0", "rstd", "tabB", "tabC", "hT", "uT", "gt1", "dg", "sqt", "mrg", "vbp"] + ACTK + WDK)
        rms_stats()
        modulate(gs2, 24)
        hb = [(0, 512), (512, 256)]
        for part in range(3):
            h0 = part * HALF
            hTh = hT[:, :, h0:h0 + HALF]

            def ev_gate(m, bi, b0, bn, p_, pk):
                act(actT[:, m, b0:b0 + bn], p_, AF.Silu, [pk], [bkey("actT", m)])

            def ev_up(m, bi, b0, bn, p_, pk):
                tt("dve", actT[:, m, b0:b0 + bn], actT[:, m, b0:b0 + bn], p_, ALU.mult, [pk, bkey("actT", m)],
                   [bkey("actT", m)])
            stream_proj(w_fg_d[l], 8, FFN, hTh, ["hT"], ev_gate, blocks=hb)
            stream_proj(w_fu_d[l], 8, FFN, hTh, ["hT"], ev_up, blocks=hb)
            for m in range(8):
                wb, wk = wload(w_fd_d[l], 22, m * 128, 128, wbufs=Wd, wkey="Wd")
                for bi, (b0, bn) in enumerate(hb):
                    pi = pscnt[0] % 4
                    pscnt[0] += 1
                    for k in range(22):
                        mm(ps[pi][:, 0:bn], wb[:, k, :], actT[:, k, b0:b0 + bn], k == 0, k == 21,
                           [wk, bkey("actT", k)], [bkey("ps", pi)])
                    t0 = h0 + b0
                    segs = [(t0, T, 0), (T, t0 + bn, 1)] if t0 < T < t0 + bn else [(t0, t0 + bn, 0 if t0 + bn <= T else 1)]
                    for (s0, s1, mc) in segs:
                        stt(xT[:, m, s0:s1], ps[pi][:, s0 - t0:s1 - t0], modv[:, 40 + m, mc:mc + 1], xT[:, m, s0:s1],
                            ALU.mult, ALU.add, [bkey("ps", pi), "modv", XK[m]], [XK[m]])
        barrier(["hT", "uT", "vbp", "mrg", "acc0", "gt0", "gt1", "dg", "sqt"] + ACTK + WDK)

    if stage == "A0":
        layer_A(0)
    elif stage == "B0A1":
        layer_B(0, xin_d)
        layer_A(1)
    else:
        layer_B(1, xin_d)
        rms_stats()
        for k in range(8):
            tt("dve", acc0[:, 0:T], xT[:, k, 0:T], rstd[:, 0:T], ALU.mult, [XK[k], "rstd"], ["acc0"])
            ts("dve", acc0[:, 0:T], acc0[:, 0:T], nfg[:, k:k + 1], None, ALU.mult, None, ["acc0", "par"], ["acc0"])
            P.dma("sync", out_d[:, k, :], acc0[:, 0:T], r=["acc0"], w=["out"])
        P.op("sync", None, r=["out"], w=["done"])
    P.emit()
    return nc


def _tT(v):
    return np.ascontiguousarray(v.reshape(-1, 128).T)


def prep_inputs(inp):
    f = np.float32
    x = np.asarray(inp["x"], f)[0]
    ctx = np.asarray(inp["ctx"], f)[0]
    common = {}
    cT = np.ascontiguousarray(ctx.T.reshape(8, 128, TC).transpose(1, 0, 2))
    cv = np.stack([np.asarray(inp["c"], f)[0], np.asarray(inp["c_ctx"], f)], axis=-1)
    common["cvec"] = np.ascontiguousarray(cv.reshape(8, 128, 2).transpose(1, 0, 2))
    q = 256
    om = (1.0 / (np.float32(10000.0) ** (np.arange(q, dtype=f) / f(q)))).astype(f)
    omega = np.ascontiguousarray(om.reshape(2, 128).T)
    cidx = (np.arange(T) % 64).astype(f)[None]
    common["tau"] = np.arange(T).astype(f)[None]
    common["ident"] = np.eye(128, dtype=f)
    for k in ("w_mod", "w_in", "w_glu", "w_proj_a", "w_proj_b", "w_out", "w_ffn_gate", "w_ffn_up", "w_ffn_down"):
        common[k] = np.ascontiguousarray(np.asarray(inp[k], f))

    def pl(v):
        v = np.asarray(v, f)
        return np.ascontiguousarray(v.reshape(DEPTH, -1, 128).transpose(2, 0, 1))

    common["bmodT"] = pl(inp["b_mod"])
    common["n1gT"] = pl(inp["norm1_g"])
    common["n2gT"] = pl(inp["norm2_g"])
    common["ssmdT"] = pl(inp["ssm_d"])
    common["dwbT"] = pl(inp["dw_bias"])
    common["cngT"] = pl(inp["conv_norm_g"])
    common["cnbT"] = pl(inp["conv_norm_b"])
    common["nfgT"] = _tT(np.asarray(inp["norm_f_g"], f))
    dk = np.asarray(inp["dw_kernel"], f)
    common["dwkT"] = np.ascontiguousarray(dk.reshape(DEPTH, 31, 8, 128).transpose(3, 0, 2, 1))
    lre = np.asarray(inp["ssm_lam_re"], f); lim = np.asarray(inp["ssm_lam_im"], f)
    ls = np.asarray(inp["ssm_log_step"], f)
    bre = np.asarray(inp["ssm_b_re"], f); bim = np.asarray(inp["ssm_b_im"], f)
    cre = np.asarray(inp["ssm_c_re"], f); cim = np.asarray(inp["ssm_c_im"], f)
    lsx = np.ascontiguousarray(np.broadcast_to(ls[..., None], lre.shape))

    def pview(a):
        return a.reshape(DEPTH, 2, 16, 128).transpose(3, 0, 1, 2).reshape(128, -1)
    common["P_l"] = np.ascontiguousarray(np.stack([pview(lre), pview(lim), pview(lsx)], axis=1))

    def eview(a):
        outa = np.zeros((DEPTH, 2, 128, 4, 128), f)
        for gi in range(8):
            for qq in range(4):
                for gl in range(2):
                    g = 8 * qq + 2 * (gi // 2) + gl
                    outa[:, :, gi * 16:(gi + 1) * 16, qq, gl * 64:(gl + 1) * 64] = a[:, :, g][:, :, None, :]
        return outa.reshape(DEPTH, 2, 128, 512)
    common["E_l"] = np.ascontiguousarray(np.stack([eview(lre), eview(lim), eview(lsx)], axis=2))

    def ebuild(b):
        outa = np.zeros((DEPTH, 2, 128, 4, 128), f)
        for gi in range(8):
            for qq in range(4):
                gl = gi % 2
                outa[:, :, gi * 16:(gi + 1) * 16, qq, gl * 64:(gl + 1) * 64] = b[:, :, 8 * qq + gi].transpose(0, 1, 3, 2)
        return outa.reshape(DEPTH, 2, 128, 512)
    common["E_b"] = np.ascontiguousarray(np.stack([ebuild(bre), ebuild(bim)], axis=2))

    def cbuild(cc):
        outa = np.zeros((DEPTH, 2, 128, 16, 32), f)
        for j in range(16):
            for gl in range(2):
                outa[:, :, gl * 64:(gl + 1) * 64, j, gl * 16:(gl + 1) * 16] = cc[:, :, 2 * j + gl].transpose(0, 1, 3, 2)
        return outa.reshape(DEPTH, 2, 128, 512)
    common["P_c"] = np.ascontiguousarray(np.stack([cbuild(cre), cbuild(cim)], axis=2))
    maps = []
    for ci in range(NCORES):
        m = dict(common)
        s = np.zeros((128, 24), f)
        s[:, ci] = 1.0
        if ci > 0:
            s[:, 8 + ci - 1] = 1.0
        if ci < NCORES - 1:
            s[:, 16 + ci + 1] = 1.0
        m["sel"] = s
        maps.append(m)
    first = []
    for ci in range(NCORES):
        xs = x[ci * T:(ci + 1) * T]
        first.append({
            "xT": np.ascontiguousarray(xs.T.reshape(8, 128, T).transpose(1, 0, 2)),
            "cT": cT, "omega": omega, "cidx": cidx,
            "ridx": ((ci * T + np.arange(T)) // 64).astype(f)[None],
        })
    return maps, first


_NC = {}
_DBG = None


def _prog(stage):
    if stage not in _NC:
        _NC[stage] = build(stage)
    return _NC[stage]


def _handover(res):
    pays = np.stack([np.asarray(res.results[ci]["pay_o"]) for ci in range(NCORES)], axis=1)
    nxt = []
    for ci in range(NCORES):
        r = res.results[ci]
        nxt.append({"xsp_i": r["xsp_o"], "hsp_i": r["hsp_o"], "u_i": r["u_o"], "vbp_i": r["vbp_o"],
                    "mods_i": r["mods_o"], "ccs_i": np.ascontiguousarray(pays)})
    return nxt


def kernel(**inputs):
    maps, first = prep_inputs(inputs)
    cores = list(range(NCORES))
    res = run_bass_kernel_spmd(_prog("A0"), [dict(maps[c], **first[c]) for c in cores], core_ids=cores)
    nxt = _handover(res)
    if _DBG is not None:
        _DBG["h1"] = nxt
    res = run_bass_kernel_spmd(_prog("B0A1"), [dict(maps[c], **nxt[c]) for c in cores], core_ids=cores)
    nxt = _handover(res)
    if _DBG is not None:
        _DBG["h2"] = nxt
    res = run_bass_kernel_spmd(_prog("B1"), [dict(maps[c], **nxt[c]) for c in cores], core_ids=cores)
    outs = []
    for ci in cores:
        o = np.asarray(res.results[ci]["outT"])
        outs.append(o.transpose(2, 1, 0).reshape(T, D))
    return np.concatenate(outs, axis=0)[None].astype(np.float32)
```
